# Optimizing a Trainium2 kernel written in Bass

```python
import math
import numpy as np
import jax
import jax.numpy as jnp
from jax import lax

D_MODEL = 1024
BATCH = 16
SEQ = 2048
DEPTH = 2
DEC_BATCH = 8
DEC_SEQ = 8192
PAST_LEN = 128

GRID_W = 64
CHUNK = 128
Q_BLOCK = 128
ROPE_THETA = 10000.0
EPS = 1e-6
D_MIX = D_MODEL
GROUP_W = D_MIX // 4
D_FF = 4 * D_MODEL

SSD_HEADS = 4
SSD_HEAD_DIM = GROUP_W // SSD_HEADS
SSD_D_INNER = GROUP_W
SSD_GROUPS = 2
SSD_STATE = 128
SSD_CONV_W = 5
SSD_XBC = SSD_D_INNER + 2 * SSD_GROUPS * SSD_STATE

GQA_HEADS = 4
GQA_KV_HEADS = 2
GQA_HEAD_DIM = GROUP_W // GQA_HEADS

GLA_HEADS = 4
GLA_DV = GROUP_W // GLA_HEADS
GLA_DK = GLA_DV // 2
GLA_LOWRANK = 16
GLA_TAU = 16.0

MLA_HEADS = 4
MLA_Q_LORA = 256
MLA_KV_LORA = 128
MLA_NOPE = 64
MLA_ROPE = 32
MLA_V = GROUP_W // MLA_HEADS
MLA_QK = MLA_NOPE + MLA_ROPE

IN_SIZES = (SSD_D_INNER, SSD_XBC, 2 * SSD_HEADS,
            GQA_HEADS * GQA_HEAD_DIM, GQA_KV_HEADS * GQA_HEAD_DIM, GQA_KV_HEADS * GQA_HEAD_DIM,
            GLA_HEADS * GLA_DK, GLA_HEADS * GLA_DK, GLA_HEADS * GLA_DV, GLA_HEADS * GLA_DV, 2 * GLA_LOWRANK,
            MLA_Q_LORA, MLA_KV_LORA, MLA_ROPE)
D_IN = sum(IN_SIZES)

kernel_name = 'hybrid_bidir_encoder_hymba4'


def rmsnorm(x, g):
    xf = x.astype(jnp.float32)
    y = xf * lax.rsqrt(jnp.mean(xf * xf, axis=-1, keepdims=True) + EPS)
    return (y * g.astype(jnp.float32)).astype(x.dtype)


def _flip(a):
    return jnp.flip(a, axis=1)


def _rotate_block(x, pos):
    n = x.shape[-1] // 2
    inv_freq = ROPE_THETA ** (-jnp.arange(n, dtype=jnp.float32) / n)
    ang = pos.astype(jnp.float32)[:, None] * inv_freq[None, :]
    cos = jnp.cos(ang)[:, None, :]
    sin = jnp.sin(ang)[:, None, :]
    xf = x.astype(jnp.float32)
    x1, x2 = xf[..., :n], xf[..., n:]
    return jnp.concatenate([x1 * cos - x2 * sin, x1 * sin + x2 * cos], axis=-1)


def axial_rope(x, row_pos, col_pos):
    half = x.shape[-1] // 2
    return jnp.concatenate([_rotate_block(x[..., :half], row_pos),
                            _rotate_block(x[..., half:], col_pos)], axis=-1).astype(x.dtype)


def block_attention(q, k, v, scale):
    b, t, g, r, d = q.shape
    nb = t // Q_BLOCK
    qb = q.reshape(b, nb, Q_BLOCK, g, r, d).transpose(1, 0, 2, 3, 4, 5)

    def one_block(qi):
        s = jnp.einsum('bqgrd,bkgd->bgrqk', qi, k, preferred_element_type=jnp.float32) * scale
        p = jax.nn.softmax(s, axis=-1)
        return jnp.einsum('bgrqk,bkge->bqgre', p.astype(v.dtype), v)

    o = lax.map(one_block, qb)
    return o.transpose(1, 0, 2, 3, 4, 5).reshape(b, t, g * r * v.shape[-1])


def centred_dwconv(x, w, bias):
    c = x.shape[-1]
    pad = SSD_CONV_W // 2
    y = lax.conv_general_dilated(x, w[:, None, :].astype(x.dtype), window_strides=(1,),
                                 padding=[(pad, pad)], dimension_numbers=('NWC', 'WIO', 'NWC'),
                                 feature_group_count=c)
    return y + bias.astype(x.dtype)


def ssd_direction(xs, dt, a, bm, cm):
    b, t, h, p = xs.shape
    g, n = bm.shape[2], bm.shape[3]
    r = h // g
    c = t // CHUNK
    x = xs.astype(jnp.float32).reshape(b, c, CHUNK, g, r, p)
    dtc = dt.astype(jnp.float32).reshape(b, c, CHUNK, g, r)
    bc = bm.astype(jnp.float32).reshape(b, c, CHUNK, g, n)
    cc = cm.astype(jnp.float32).reshape(b, c, CHUNK, g, n)
    acs = jnp.cumsum(dtc * a.astype(jnp.float32).reshape(g, r), axis=2)
    mask = jnp.tril(jnp.ones((CHUNK, CHUNK), dtype=bool))[:, :, None, None]
    seg = acs[:, :, :, None] - acs[:, :, None, :]
    decay = jnp.exp(jnp.where(mask, seg, -jnp.inf))
    xdt = x * dtc[..., None]
    scores = jnp.einsum('bclgn,bcsgn->bclsg', cc, bc)
    y_diag = jnp.einsum('bclsg,bclsgr,bcsgrp->bclgrp', scores, decay, xdt)
    end_decay = jnp.exp(acs[:, :, -1:] - acs)
    states = jnp.einsum('bcsgn,bcsgr,bcsgrp->bcgrpn', bc, end_decay, xdt)
    chunk_decay = jnp.exp(acs[:, :, -1])

    def step(h_prev, inp):
        st, dec = inp
        return h_prev * dec[..., None, None] + st, h_prev

    h0 = jnp.zeros((b, g, r, p, n), jnp.float32)
    _, h_start = lax.scan(step, h0, (states.transpose(1, 0, 2, 3, 4, 5), chunk_decay.transpose(1, 0, 2, 3)))
    h_start = h_start.transpose(1, 0, 2, 3, 4, 5)
    y_off = jnp.einsum('bclgn,bcgrpn,bclgr->bclgrp', cc, h_start, jnp.exp(acs))
    return (y_diag + y_off).reshape(b, t, h, p)


def gla_direction(q, k, v, logg):
    b, t, h, dk = q.shape
    dv = v.shape[-1]
    c = t // CHUNK
    q = q.reshape(b, c, CHUNK, h, dk)
    k = k.reshape(b, c, CHUNK, h, dk)
    v = v.reshape(b, c, CHUNK, h, dv)
    gcs = jnp.cumsum(logg.reshape(b, c, CHUNK, h, dk), axis=2)
    qg = q * jnp.exp(gcs)
    kg = k * jnp.exp(-gcs)
    mask = jnp.tril(jnp.ones((CHUNK, CHUNK), dtype=bool))
    att = jnp.where(mask, jnp.einsum('bclhk,bcshk->bchls', qg, kg), 0.0)
    o_intra = jnp.einsum('bchls,bcshv->bclhv', att, v)
    g_last = gcs[:, :, -1]
    states = jnp.einsum('bcshk,bcshv->bchkv', k * jnp.exp(g_last[:, :, None] - gcs), v)

    def step(s_prev, inp):
        st, dec = inp
        return s_prev * dec[..., None] + st, s_prev

    s0 = jnp.zeros((b, h, dk, dv), jnp.float32)
    _, s_start = lax.scan(step, s0, (states.transpose(1, 0, 2, 3, 4), jnp.exp(g_last).transpose(1, 0, 2, 3)))
    s_start = s_start.transpose(1, 0, 2, 3, 4)
    o_inter = jnp.einsum('bclhk,bchkv->bclhv', qg, s_start)
    return (o_intra + o_inter).reshape(b, t, h, dv)


def mixer_block(h, row_pos, col_pos, w_in, ssd_conv_w, ssd_conv_b, ssd_dt_bias, ssd_a_log, ssd_d,
                ssd_norm_g, gqa_q_norm_g, gqa_k_norm_g, gla_gate_w2, gla_gate_b, gla_norm_g,
                mla_q_norm_g, mla_w_uq, mla_kv_norm_g, mla_w_ukv, w_out):
    b, t, _ = h.shape
    f32 = jnp.float32
    proj = h @ w_in
    splits = np.cumsum(IN_SIZES)[:-1].tolist()
    (ssd_z, ssd_xbc, ssd_dt, gqa_q, gqa_k, gqa_v, gla_q, gla_k, gla_v, gla_r, gla_lr,
     mla_cq, mla_ckv, mla_kr) = jnp.split(proj, splits, axis=-1)

    xbc = jax.nn.silu(centred_dwconv(ssd_xbc, ssd_conv_w, ssd_conv_b))
    xs, bm, cm = jnp.split(xbc, [SSD_D_INNER, SSD_D_INNER + SSD_GROUPS * SSD_STATE], axis=-1)
    xs = xs.reshape(b, t, SSD_HEADS, SSD_HEAD_DIM)
    bm = bm.reshape(b, t, SSD_GROUPS, SSD_STATE)
    cm = cm.reshape(b, t, SSD_GROUPS, SSD_STATE)
    dt = jax.nn.softplus(ssd_dt.astype(f32).reshape(b, t, 2, SSD_HEADS) + ssd_dt_bias.astype(f32))
    a = -jnp.exp(ssd_a_log.astype(f32))
    y_fwd = ssd_direction(xs, dt[:, :, 0], a[0], bm, cm)
    y_bwd = _flip(ssd_direction(_flip(xs), _flip(dt[:, :, 1]), a[1], _flip(bm), _flip(cm)))
    y = y_fwd + y_bwd + ssd_d.astype(f32)[:, None] * xs.astype(f32)
    y = y.reshape(b, t, SSD_D_INNER)
    ssd_out = rmsnorm(y * jax.nn.silu(ssd_z.astype(f32)), ssd_norm_g)

    q = axial_rope(rmsnorm(gqa_q.reshape(b, t, GQA_HEADS, GQA_HEAD_DIM), gqa_q_norm_g), row_pos, col_pos)
    k = axial_rope(rmsnorm(gqa_k.reshape(b, t, GQA_KV_HEADS, GQA_HEAD_DIM), gqa_k_norm_g), row_pos, col_pos)
    v = gqa_v.reshape(b, t, GQA_KV_HEADS, GQA_HEAD_DIM)
    q = q.reshape(b, t, GQA_KV_HEADS, GQA_HEADS // GQA_KV_HEADS, GQA_HEAD_DIM)
    gqa_out = block_attention(q, k, v, GQA_HEAD_DIM ** -0.5)

    lq = gla_q.reshape(b, t, GLA_HEADS, GLA_DK).astype(f32) * GLA_DK ** -0.5
    lk = gla_k.reshape(b, t, GLA_HEADS, GLA_DK).astype(f32)
    lv = gla_v.reshape(b, t, GLA_HEADS, GLA_DV).astype(f32)
    lr = gla_lr.reshape(b, t, 2, GLA_LOWRANK).astype(f32)
    logg = jax.nn.log_sigmoid(jnp.einsum('btdl,dlk->btdk', lr, gla_gate_w2.astype(f32))
                              + gla_gate_b.astype(f32)) / GLA_TAU
    logg = logg.reshape(b, t, 2, GLA_HEADS, GLA_DK)
    o = gla_direction(lq, lk, lv, logg[:, :, 0]) + _flip(
        gla_direction(_flip(lq), _flip(lk), _flip(lv), _flip(logg[:, :, 1])))
    gla_out = rmsnorm(o, gla_norm_g).reshape(b, t, GLA_HEADS * GLA_DV) * jax.nn.silu(gla_r.astype(f32))

    cq = (rmsnorm(mla_cq, mla_q_norm_g) @ mla_w_uq).reshape(b, t, MLA_HEADS, MLA_QK)
    mq = jnp.concatenate([cq[..., :MLA_NOPE], axial_rope(cq[..., MLA_NOPE:], row_pos, col_pos)], axis=-1)
    kv = (rmsnorm(mla_ckv, mla_kv_norm_g) @ mla_w_ukv).reshape(b, t, MLA_HEADS, MLA_NOPE + MLA_V)
    k_rope = axial_rope(mla_kr.reshape(b, t, 1, MLA_ROPE), row_pos, col_pos)
    mk = jnp.concatenate([kv[..., :MLA_NOPE], jnp.broadcast_to(k_rope, (b, t, MLA_HEADS, MLA_ROPE))], axis=-1)
    mv = kv[..., MLA_NOPE:]
    mla_out = block_attention(mq.reshape(b, t, MLA_HEADS, 1, MLA_QK), mk, mv, MLA_QK ** -0.5)

    mix = jnp.concatenate([ssd_out.astype(h.dtype), gqa_out.astype(h.dtype),
                           gla_out.astype(h.dtype), mla_out.astype(h.dtype)], axis=-1)
    return mix @ w_out


def sq_relu_mlp(h, w_ff1, w_ff2):
    return jnp.square(jax.nn.relu(h @ w_ff1)) @ w_ff2


def trunk(x, params):
    (norm1_g, w_in, ssd_conv_w, ssd_conv_b, ssd_dt_bias, ssd_a_log, ssd_d, ssd_norm_g,
     gqa_q_norm_g, gqa_k_norm_g, gla_gate_w2, gla_gate_b, gla_norm_g, mla_q_norm_g, mla_w_uq,
     mla_kv_norm_g, mla_w_ukv, w_out, norm2_g, w_ff1, w_ff2, final_norm_g) = params
    t = x.shape[1]
    rows = t // GRID_W
    row_pos = jnp.repeat(jnp.arange(rows, dtype=jnp.int32), GRID_W)
    col_pos = jnp.tile(jnp.arange(GRID_W, dtype=jnp.int32), rows)
    for i in range(DEPTH):
        h = rmsnorm(x, norm1_g[i])
        x = x + mixer_block(h, row_pos, col_pos, w_in[i], ssd_conv_w[i], ssd_conv_b[i], ssd_dt_bias[i],
                            ssd_a_log[i], ssd_d[i], ssd_norm_g[i], gqa_q_norm_g[i], gqa_k_norm_g[i],
                            gla_gate_w2[i], gla_gate_b[i], gla_norm_g[i], mla_q_norm_g[i], mla_w_uq[i],
                            mla_kv_norm_g[i], mla_w_ukv[i], w_out[i])
        h = rmsnorm(x, norm2_g[i])
        x = x + sq_relu_mlp(h, w_ff1[i], w_ff2[i])
    return rmsnorm(x, final_norm_g)


def setup_inputs(seed: int = 0) -> dict:
    key = jax.random.key(seed)
    ks = iter(jax.random.split(key, 32))
    f32 = jnp.float32
    L = DEPTH

    def nrm(shape, scale):
        return jax.random.normal(next(ks), shape, f32) * scale

    def gain(shape):
        return 1.0 + 0.02 * jax.random.normal(next(ks), shape, f32)

    x_prompt = jax.random.normal(next(ks), (BATCH, SEQ, D_MODEL), f32)
    x_sample = jax.random.normal(next(ks), (DEC_BATCH, DEC_SEQ, D_MODEL), f32)
    norm1_g = gain((L, D_MODEL))
    w_in = nrm((L, D_MODEL, D_IN), D_MODEL ** -0.5)
    ssd_conv_w = nrm((L, SSD_CONV_W, SSD_XBC), SSD_CONV_W ** -0.5)
    ssd_conv_b = nrm((L, SSD_XBC), 0.02)
    dt0 = jnp.exp(jax.random.uniform(next(ks), (L, 2, SSD_HEADS), f32, math.log(1e-3), math.log(1e-1)))
    ssd_dt_bias = dt0 + jnp.log(-jnp.expm1(-dt0))
    ssd_a_log = jnp.log(jax.random.uniform(next(ks), (L, 2, SSD_HEADS), f32, 1.0, 16.0))
    ssd_d = gain((L, SSD_HEADS))
    ssd_norm_g = gain((L, SSD_D_INNER))
    gqa_q_norm_g = gain((L, GQA_HEAD_DIM))
    gqa_k_norm_g = gain((L, GQA_HEAD_DIM))
    gla_gate_w2 = nrm((L, 2, GLA_LOWRANK, GLA_HEADS * GLA_DK), GLA_LOWRANK ** -0.5)
    gla_gate_b = nrm((L, 2, GLA_HEADS * GLA_DK), 0.02)
    gla_norm_g = gain((L, GLA_DV))
    mla_q_norm_g = gain((L, MLA_Q_LORA))
    mla_w_uq = nrm((L, MLA_Q_LORA, MLA_HEADS * MLA_QK), MLA_Q_LORA ** -0.5)
    mla_kv_norm_g = gain((L, MLA_KV_LORA))
    mla_w_ukv = nrm((L, MLA_KV_LORA, MLA_HEADS * (MLA_NOPE + MLA_V)), MLA_KV_LORA ** -0.5)
    w_out = nrm((L, D_MIX, D_MODEL), D_MIX ** -0.5)
    norm2_g = gain((L, D_MODEL))
    w_ff1 = nrm((L, D_MODEL, D_FF), D_MODEL ** -0.5)
    w_ff2 = nrm((L, D_FF, D_MODEL), D_FF ** -0.5)
    final_norm_g = gain((D_MODEL,))
    return {'x_prompt': x_prompt, 'x_sample': x_sample, 'norm1_g': norm1_g, 'w_in': w_in,
            'ssd_conv_w': ssd_conv_w, 'ssd_conv_b': ssd_conv_b, 'ssd_dt_bias': ssd_dt_bias,
            'ssd_a_log': ssd_a_log, 'ssd_d': ssd_d, 'ssd_norm_g': ssd_norm_g,
            'gqa_q_norm_g': gqa_q_norm_g, 'gqa_k_norm_g': gqa_k_norm_g,
            'gla_gate_w2': gla_gate_w2, 'gla_gate_b': gla_gate_b, 'gla_norm_g': gla_norm_g,
            'mla_q_norm_g': mla_q_norm_g, 'mla_w_uq': mla_w_uq, 'mla_kv_norm_g': mla_kv_norm_g,
            'mla_w_ukv': mla_w_ukv, 'w_out': w_out, 'norm2_g': norm2_g, 'w_ff1': w_ff1,
            'w_ff2': w_ff2, 'final_norm_g': final_norm_g}


def reference(x_prompt, x_sample, norm1_g, w_in, ssd_conv_w, ssd_conv_b, ssd_dt_bias, ssd_a_log, ssd_d,
              ssd_norm_g, gqa_q_norm_g, gqa_k_norm_g, gla_gate_w2, gla_gate_b, gla_norm_g,
              mla_q_norm_g, mla_w_uq, mla_kv_norm_g, mla_w_ukv, w_out, norm2_g, w_ff1, w_ff2,
              final_norm_g):
    params = (norm1_g, w_in, ssd_conv_w, ssd_conv_b, ssd_dt_bias, ssd_a_log, ssd_d, ssd_norm_g,
              gqa_q_norm_g, gqa_k_norm_g, gla_gate_w2, gla_gate_b, gla_norm_g, mla_q_norm_g, mla_w_uq,
              mla_kv_norm_g, mla_w_ukv, w_out, norm2_g, w_ff1, w_ff2, final_norm_g)
    y_prompt = trunk(x_prompt, params)
    y_sample = trunk(x_sample, params)
    return (y_prompt, y_sample)
```

```python
import contextlib
import math
import numpy as np
import concourse.bass as bass
import concourse.mybir as mybir
from concourse.bass_utils import run_bass_kernel_spmd

F32 = mybir.dt.float32
BF16 = mybir.dt.bfloat16
U8 = mybir.dt.uint8
ALU = mybir.AluOpType
AF = mybir.ActivationFunctionType
AX = mybir.AxisListType

DM = 1024
DIN = 2760
DFF = 4096
EPS = 1e-6
DMA_R = 8
NTM = 1992

PNAMES = ['norm1_g', 'w_in', 'ssd_conv_w', 'ssd_conv_b', 'ssd_dt_bias', 'ssd_a_log', 'ssd_d', 'ssd_norm_g',
          'gqa_q_norm_g', 'gqa_k_norm_g', 'gla_gate_w2', 'gla_gate_b', 'gla_norm_g', 'mla_q_norm_g', 'mla_w_uq',
          'mla_kv_norm_g', 'mla_w_ukv', 'w_out', 'norm2_g', 'w_ff1', 'w_ff2', 'final_norm_g']


class Buf:
    __slots__ = ("name", "w", "r")

    def __init__(self, name=None):
        self.name = name
        self.w = None
        self.r = {}


class T:
    __slots__ = ("a", "b")

    def __init__(self, a, b=None):
        self.a = a
        self.b = b if b is not None else Buf()


class FW:
    def __init__(self, nc, stack):
        self.nc = nc
        self.stack = stack
        self.sems = {}
        self.cnt = {}
        self.dsems = {}
        for n in ("pe", "act", "dve", "pool"):
            self.sems["c_" + n] = stack.enter_context(nc.semaphore("c_" + n))
            self.cnt[n] = 0
        for q in ("sp", "qpool", "qact"):
            self.dsems[q] = []
            self.cnt[q] = 0
            for i in range(DMA_R):
                k = "d_%s%d" % (q, i)
                self.sems[k] = stack.enter_context(nc.semaphore(k))
                self.dsems[q].append(k)
        self.streams = {"pe": [], "act": [], "dve": [], "pool": [], "sp": []}
        self.stream_of = {"pe": "pe", "act": "act", "dve": "dve", "pool": "pool",
                          "sp": "sp", "qpool": "pool", "qact": "act"}
        self.known = {s: {} for s in self.streams}
        self.latest = {}
        self.nops = 0
        self.cap = None

    def capture(self, fn, *args):
        assert self.cap is None
        self.cap = []
        fn(*args)
        ops = self.cap
        self.cap = None
        return ops

    def merge_emit(self, lanes, windows=None):
        if windows is None:
            windows = [(0.0, 1.0)] * len(lanes)
        windows = [w for l, w in zip(lanes, windows) if l]
        lanes = [l for l in lanes if l]
        idx = [0] * len(lanes)
        while True:
            best, bv = -1, 9.0
            for i, l in enumerate(lanes):
                if idx[i] < len(l):
                    v = windows[i][0] + (windows[i][1] - windows[i][0]) * (idx[i] + 0.5) / len(l)
                    if v < bv:
                        best, bv = i, v
            if best < 0:
                break
            self.op(*lanes[best][idx[best]])
            idx[best] += 1

    def op(self, eng, fn, reads=(), writes=()):
        if self.cap is not None:
            self.cap.append((eng, fn, tuple(reads), tuple(writes)))
            return None
        st = self.stream_of[eng]
        deps = {}
        for t in reads:
            b = t.b if isinstance(t, T) else t
            if b.w is not None and deps.get(b.w[0], 0) < b.w[1]:
                deps[b.w[0]] = b.w[1]
        for t in writes:
            b = t.b if isinstance(t, T) else t
            if b.w is not None and deps.get(b.w[0], 0) < b.w[1]:
                deps[b.w[0]] = b.w[1]
            for k, v in b.r.items():
                if deps.get(k, 0) < v:
                    deps[k] = v
        if eng in self.dsems:
            n = self.cnt[eng]
            slot = self.dsems[eng][n % DMA_R]
            rnd = n // DMA_R
            if rnd > 0 and deps.get(slot, 0) < 16 * rnd:
                deps[slot] = 16 * rnd
            tok = (slot, 16 * (rnd + 1))
            self.cnt[eng] = n + 1
            inc = (slot, 16)
        else:
            self.cnt[eng] += 1
            tok = ("c_" + eng, self.cnt[eng])
            inc = ("c_" + eng, 1)
        self.latest[tok[0]] = tok[1]
        known = self.known[st]
        waits = []
        for k, v in deps.items():
            if eng == "pe" and k == "c_pe":
                continue
            if known.get(k, 0) >= v:
                continue
            known[k] = v
            waits.append((k, v))
        self.streams[st].append((waits, fn, inc))
        self.nops += 1
        wset = set()
        for t in writes:
            b = t.b if isinstance(t, T) else t
            b.w = tok
            b.r = {}
            wset.add(id(b))
        for t in reads:
            b = t.b if isinstance(t, T) else t
            if id(b) in wset:
                continue
            if b.r.get(tok[0], 0) < tok[1]:
                b.r[tok[0]] = tok[1]
        return tok

    def barrier(self):
        for st in self.streams:
            known = self.known[st]
            waits = []
            for k, v in self.latest.items():
                if known.get(k, 0) < v:
                    known[k] = v
                    waits.append((k, v))
            if waits:
                self.streams[st].append((waits, None, None))

    def finish(self):
        nc = self.nc
        self.barrier()
        sems = self.sems
        streams = self.streams

        def replay(engobj, ops):
            for waits, fn, inc in ops:
                for k, v in waits:
                    engobj.wait_ge(sems[k], v)
                if fn is not None:
                    ins = fn(engobj)
                    ins.then_inc(sems[inc[0]], inc[1])

        with nc.Block() as block:
            @block.tensor
            def _(eng):
                replay(eng, streams["pe"])

            @block.scalar
            def _(eng):
                replay(eng, streams["act"])

            @block.vector
            def _(eng):
                replay(eng, streams["dve"])

            @block.gpsimd
            def _(eng):
                replay(eng, streams["pool"])

            @block.sync
            def _(eng):
                replay(eng, streams["sp"])


def host_consts(tmax):
    t = np.arange(tmax)
    row = (t // 64).astype(np.float32)
    col = (t % 64).astype(np.float32)

    def tabs(n):
        inv = (10000.0 ** (-np.arange(n, dtype=np.float32) / n)).astype(np.float32)
        out = []
        for pos in (row, col):
            ang = pos[:, None].astype(np.float32) * inv[None, :]
            c = np.cos(ang).astype(np.float32)
            s = np.sin(ang).astype(np.float32)
            out.append((np.concatenate([c, c], 1), np.concatenate([-s, s], 1)))
        C = np.concatenate([out[0][0], out[1][0]], 1)
        S = np.concatenate([out[0][1], out[1][1]], 1)
        return C, S

    C64, S64 = tabs(16)
    C32, S32 = tabs(8)
    tab = np.concatenate([C64, S64, C32, S32], 1).astype(np.float32)
    j = np.arange(128)
    uf = (j[:, None] <= j[None, :]).astype(np.float32)
    ub = (j[:, None] >= j[None, :]).astype(np.float32)
    negf = np.where(uf > 0, 0.0, -30000.0).astype(np.float32)
    negb = np.where(ub > 0, 0.0, -30000.0).astype(np.float32)
    bm4 = np.zeros((128, 4), np.float32)
    for h in range(4):
        bm4[h * 32:(h + 1) * 32, h] = 1.0
    bmbig = np.repeat(bm4, 64, axis=1)
    eye = np.eye(128, dtype=np.float32)
    cm = np.concatenate([
        uf, ub,
        ub - eye, uf - eye,
        uf * (-1.0 / 16), ub * (-1.0 / 16),
        eye,
        bm4,
        bmbig,
        np.full((128, 4), -1.0 / 16, np.float32),
    ], 1).astype(np.float32)
    return tab, cm


CM_COLS = 1160


def build(seqs, depth, debug=()):
    nc = bass.Bass("TRN2", target_bir_lowering=False)
    nseq = len(seqs)
    offs = [0]
    for s in seqs:
        offs.append(offs[-1] + s)
    TT = offs[-1]
    nchunks = TT // 128
    L = depth

    def dram(name, shape, dt, kind=None):
        if kind is None:
            kind = "ExternalOutput" if name in debug else "Internal"
        return nc.dram_tensor(name, list(shape), dt, kind=kind).ap()

    x_in = dram("x", [TT, DM], F32, "ExternalInput")
    Wd = {}
    shp = {'norm1_g': [L, DM], 'w_in': [L, DM, DIN], 'ssd_conv_w': [L, 5, 768], 'ssd_conv_b': [L, 768],
           'ssd_dt_bias': [L, 8], 'ssd_a_log': [L, 8], 'ssd_d': [L, 4], 'ssd_norm_g': [L, 256],
           'gqa_q_norm_g': [L, 64], 'gqa_k_norm_g': [L, 64], 'gla_gate_w2': [L, 2, 16, 128],
           'gla_gate_b': [L, 256], 'gla_norm_g': [L, 64], 'mla_q_norm_g': [L, 256], 'mla_w_uq': [L, 256, 384],
           'mla_kv_norm_g': [L, 128], 'mla_w_ukv': [L, 128, 512], 'w_out': [L, DM, DM], 'norm2_g': [L, DM],
           'w_ff1': [L, DM, DFF], 'w_ff2': [L, DFF, DM], 'final_norm_g': [DM]}
    for k in PNAMES:
        Wd[k] = dram(k, shp[k], F32, "ExternalInput")
    tab_d = dram("tab", [max(seqs), 192], F32, "ExternalInput")
    cm_d = dram("cm", [128, CM_COLS], F32, "ExternalInput")
    y_out = dram("y", [TT, DM], F32, "ExternalOutput")

    TP = TT + 4 * nseq
    xbcpre = dram("xbcpre", [6, 128, TP], F32)
    QTg = dram("QTg", [4, 64, TT], BF16)
    KTg = dram("KTg", [128, TT], BF16)
    VAg = dram("VAg", [128, nchunks, 130], BF16)
    QTm = dram("QTm", [4, 96, TT], BF16)
    KTm = dram("KTm", [4, 96, TT], BF16)
    VAm = dram("VAm", [2, 128, nchunks, 130], BF16)
    CTd = dram("CTd", [2, 128, TT], BF16)
    qgTd = dram("qgTd", [2, 128, TT], BF16)
    eacs_d = dram("eacs", [128, nchunks, 8], F32)
    yo_d = dram("yo", [TT, 512], F32)
    zr_d = dram("zr", [TT, 512], F32)
    dtr_d = dram("dtr", [128, nchunks, 8], F32)
    stssd_d = dram("stssd", [nchunks, 128, 512], F32)
    stgla_d = dram("stgla", [nchunks, 128, 512], F32)
    mixT_d = dram("mixT", [8, 128, TT], BF16)
    xres_d = dram("xres", [TT, DM], F32)
    w1b_d = dram("w1b", [L, DM, DFF], BF16)

    with contextlib.ExitStack() as stack:
        fw = FW(nc, stack)
        SB_BYTES = 212800
        big = stack.enter_context(nc.sbuf_tensor("big", [128, SB_BYTES], U8))
        psall = stack.enter_context(nc.psum_tensor("psall", [128, 4096], F32))
        banks = [T(psall[:, i * 512:(i + 1) * 512]) for i in range(8)]
        bump = [0]

        def alloc(shape, dt, nb=1):
            esz = 4 if dt == F32 else 2
            n = int(np.prod(shape[1:])) * esz
            n = (n + 31) // 32 * 32
            res = []
            for _ in range(nb):
                off = bump[0]
                bump[0] += n
                assert bump[0] <= SB_BYTES, "SBUF overflow %d" % bump[0]
                ap = big[:, off:off + int(np.prod(shape[1:])) * esz].bitcast(dt)
                if len(shape) == 3:
                    ap = ap.rearrange("p (a b) -> p a b", a=shape[1])
                elif len(shape) == 4:
                    ap = ap.rearrange("p (a b c) -> p a b c", a=shape[1], b=shape[2])
                res.append(T(ap))
            return res[0] if nb == 1 else res

        def tt(eng, out, in0, in1, op, r, w):
            fw.op(eng, lambda e: e.tensor_tensor(out=out, in0=in0, in1=in1, op=op), r, w)

        def stt(eng, out, in0, scalar, in1, op0, op1, r, w):
            fw.op(eng, lambda e: e.scalar_tensor_tensor(out=out, in0=in0, scalar=scalar, in1=in1, op0=op0, op1=op1), r, w)

        def ts(eng, out, in0, s1, s2, op0, op1, r, w):
            if s2 is None:
                fw.op(eng, lambda e: e.tensor_scalar(out=out, in0=in0, scalar1=s1, scalar2=None, op0=op0), r, w)
            else:
                fw.op(eng, lambda e: e.tensor_scalar(out=out, in0=in0, scalar1=s1, scalar2=s2, op0=op0, op1=op1), r, w)

        def act(out, in_, func, r, w, bias=None, scale=None, accum=None):
            kw = {}
            if bias is not None:
                kw["bias"] = bias
            if scale is not None:
                kw["scale"] = scale
            if accum is not None:
                kw["accum_out"] = accum
            fw.op("act", lambda e: e.activation(out=out, in_=in_, func=func, **kw), r, w)

        def cp(eng, out, in_, r, w):
            if eng == "act":
                fw.op("act", lambda e: e.activation(out=out, in_=in_, func=AF.Copy), r, w)
            else:
                fw.op(eng, lambda e: e.tensor_copy(out=out, in_=in_), r, w)

        def red(out, in_, r, w):
            fw.op("dve", lambda e: e.tensor_reduce(out=out, in_=in_, axis=AX.X, op=ALU.add), r, w)

        def mm(out, lhsT, rhs, start, stop, r, w):
            fw.op("pe", lambda e: e.matmul(out, lhsT=lhsT, rhs=rhs, start=start, stop=stop), r, w)

        def tr(out, in_, ident, r, w):
            fw.op("pe", lambda e: e.transpose(out=out, in_=in_, identity=ident), r, w)

        def dma(q, out, in_, r, w, slow=False):
            w = [x for x in w if x is not dscr]
            if slow:
                fw.op(q, lambda e: e.dma_start(out=out, in_=in_, allow_slow_non_contiguous=True), r, w)
            else:
                fw.op(q, lambda e: e.dma_start(out=out, in_=in_), r, w)

        dscr = T(None)

        def rstd_from_ssq(ssq_ap, n, out_ap, r, w, tmp):
            act(tmp.a, ssq_ap, AF.Ln, r, [tmp], bias=epsb.a[:, 0:1], scale=1.0 / n)
            act(out_ap, tmp.a, AF.Exp, [tmp], w, scale=-0.5)

        cm = alloc([128, CM_COLS], F32)
        UF = cm.a[:, 0:128]
        UB = cm.a[:, 128:256]
        SU = [cm.a[:, 256:384], cm.a[:, 384:512]]
        UFS = cm.a[:, 512:640]
        UBS = cm.a[:, 640:768]
        IDF = cm.a[:, 768:896]
        BM4 = cm.a[:, 896:900]
        BMBIG = cm.a[:, 900:1156]
        CNEG = cm.a[:, 1156:1160]
        idb = alloc([128, 128], BF16)
        onesf = alloc([128, 128], F32)
        epsb = alloc([128, 2], F32)
        egl = alloc([128, nchunks, 2], F32)
        cdec = alloc([128, nchunks, 8], F32)
        g1bc = alloc([128, DM], F32)
        g2bc = alloc([128, DM], F32)
        gqk = alloc([128, 6, 64], F32)
        gcq = alloc([128, 256], F32)
        gckv = alloc([128, 128], F32)
        gateb = alloc([128, 256], F32)
        gssd = alloc([128, 256], F32)
        ggla = alloc([128, 64], F32)
        dtb = alloc([128, 8], F32)
        abc = alloc([128, 8], F32)
        dbc = alloc([128, 4], F32)
        cw = alloc([128, 6, 5], F32)
        cb = alloc([128, 6], F32)
        w2blk = alloc([128, 256], F32)
        wuq = alloc([128, 2, 384], BF16)
        wukv = alloc([128, 512], BF16)
        persist_mark = bump[0]

        dma("sp", cm.a, cm_d, [], [cm])
        cp("dve", idb.a, IDF, [cm], [idb])
        fw.op("dve", lambda e: e.memset(onesf.a, 1.0), [], [onesf])
        fw.op("dve", lambda e: e.memset(epsb.a, EPS), [], [epsb])

        ztile = alloc([128, 6, 4], F32)
        fw.op("dve", lambda e: e.memset(ztile.a, 0.0), [], [ztile])
        for s in range(nseq):
            c0 = offs[s] + 4 * s
            dma("qpool", xbcpre[:, :, c0:c0 + 2].rearrange("c p t -> p c t"), ztile.a[:, :, 0:2], [ztile], [dscr])
            c1 = c0 + 2 + seqs[s]
            dma("qpool", xbcpre[:, :, c1:c1 + 2].rearrange("c p t -> p c t"), ztile.a[:, :, 2:4], [ztile], [dscr])

        wst = alloc([128, 4096], F32, nb=2)
        wsb = alloc([128, 4096], BF16, nb=2)
        i = 0
        for l in range(L):
            for c in range(8):
                a, b2 = wst[i % 2], wsb[i % 2]
                dma("sp", a.a, Wd['w_ff1'][l, c * 128:(c + 1) * 128, :], [], [a])
                if i % 2 == 0:
                    cp("dve", b2.a, a.a, [a], [b2])
                else:
                    cp("act", b2.a, a.a, [a], [b2])
                dma("qpool", w1b_d[l, c * 128:(c + 1) * 128, :], b2.a, [b2], [dscr])
                i += 1
        fw.barrier()
        bump[0] = persist_mark

        supers = []
        for s in range(nseq):
            for t0 in range(0, seqs[s], 512):
                supers.append((s, t0, offs[s] + t0))

        for l in range(L):
            xsrc = x_in if l == 0 else xres_d
            last = (l == L - 1)
            tmpw = alloc([128, 1024], F32)
            dma("sp", g1bc.a, Wd['norm1_g'][l].partition_broadcast(128), [], [g1bc])
            dma("sp", g2bc.a, Wd['norm2_g'][l].partition_broadcast(128), [], [g2bc])
            for h in range(4):
                dma("sp", gqk.a[:, h, :], Wd['gqa_q_norm_g'][l].partition_broadcast(128), [], [gqk])
            for h in range(2):
                dma("sp", gqk.a[:, 4 + h, :], Wd['gqa_k_norm_g'][l].partition_broadcast(128), [], [gqk])
            dma("sp", gcq.a, Wd['mla_q_norm_g'][l].partition_broadcast(128), [], [gcq])
            dma("sp", gckv.a, Wd['mla_kv_norm_g'][l].partition_broadcast(128), [], [gckv])
            dma("sp", gateb.a, Wd['gla_gate_b'][l].partition_broadcast(128), [], [gateb])
            dma("sp", gssd.a, Wd['ssd_norm_g'][l].partition_broadcast(128), [], [gssd])
            dma("sp", ggla.a, Wd['gla_norm_g'][l].partition_broadcast(128), [], [ggla])
            dma("sp", dtb.a, Wd['ssd_dt_bias'][l].partition_broadcast(128), [], [dtb])
            dma("sp", abc.a, Wd['ssd_a_log'][l].partition_broadcast(128), [], [abc])
            dma("sp", dbc.a, Wd['ssd_d'][l].partition_broadcast(128), [], [dbc])
            act(abc.a, abc.a, AF.Exp, [abc], [abc])
            ts("dve", abc.a, abc.a, -1.0, None, ALU.mult, None, [abc], [abc])
            for k in range(5):
                dma("sp", cw.a[:, :, k], Wd['ssd_conv_w'][l, k].rearrange("(c p) -> p c", p=128), [], [cw], slow=True)
            dma("sp", cb.a, Wd['ssd_conv_b'][l].rearrange("(c p) -> p c", p=128), [], [cb], slow=True)
            fw.op("dve", lambda e: e.memset(w2blk.a, 0.0), [], [w2blk])
            dma("sp", w2blk.a[0:16, 0:128], Wd['gla_gate_w2'][l, 0], [], [w2blk])
            dma("sp", w2blk.a[16:32, 128:256], Wd['gla_gate_w2'][l, 1], [], [w2blk])
            for c in range(2):
                dma("sp", tmpw.a[:, 0:384], Wd['mla_w_uq'][l, c * 128:(c + 1) * 128, :], [], [tmpw])
                cp("dve", wuq.a[:, c, :], tmpw.a[:, 0:384], [tmpw], [wuq])
            dma("sp", tmpw.a[:, 0:512], Wd['mla_w_ukv'][l], [], [tmpw])
            cp("dve", wukv.a, tmpw.a[:, 0:512], [tmpw], [wukv])
            fw.barrier()
            bump[0] = persist_mark

            win = alloc([128, 8, DIN], BF16)
            a1_mark = bump[0]
            wstage = alloc([128, DIN], F32, nb=2)
            for c in range(8):
                wsx = wstage[c % 2]
                dma("sp", wsx.a, Wd['w_in'][l, c * 128:(c + 1) * 128, :], [], [wsx])
                eng = "dve" if c % 2 == 0 else "act"
                cp(eng, win.a[:, c, 0:1728], wsx.a[:, 1032:2760], [wsx], [win])
                cp(eng, win.a[:, c, 1728:1984], wsx.a[:, 0:256], [wsx], [win])
                cp(eng, win.a[:, c, 1984:1992], wsx.a[:, 1024:1032], [wsx], [win])
                cp(eng, win.a[:, c, 1992:2760], wsx.a[:, 256:1024], [wsx], [win])
            fw.barrier()
            bump[0] = a1_mark
            xt = alloc([128, DM], F32, nb=2)
            junk = alloc([128, DM], BF16)
            hb = alloc([128, DM], BF16, nb=2)
            hT = alloc([128, 8, 512], BF16, nb=2)
            ptok = alloc([128, NTM], F32, nb=3)
            tabt = alloc([128, 192], F32, nb=3)
            sm = alloc([128, 16], F32, nb=3)
            sm2 = alloc([128, 16], F32, nb=3)
            tA = alloc([128, 6, 64], F32)
            tB = alloc([128, 6, 64], F32)
            tC = alloc([128, 6, 64], F32)
            tD = alloc([128, 6, 64], F32)
            qrb_r = alloc([128, 256], BF16, nb=2)
            krb_r = alloc([128, 128], BF16, nb=2)
            QA_r = alloc([128, 512], BF16, nb=2)
            QB_r = alloc([128, 512], BF16, nb=2)
            KTs_r = alloc([128, 512], BF16, nb=2)
            vaug_r = alloc([128, 4, 130], BF16, nb=2)
            vaugm_r = alloc([128, 2, 4, 130], BF16, nb=2)
            cqn_r = alloc([128, 384], BF16, nb=2)
            cT_r = alloc([128, 3, 128], BF16, nb=2)
            mt1 = alloc([128, 5, 32], F32)
            mt2 = alloc([128, 5, 32], F32)
            qmb_r = alloc([128, 4, 96], BF16, nb=2)
            kmb_r = alloc([128, 4, 96], BF16, nb=2)
            QM_r = alloc([128, 4, 512], BF16, nb=1); QM_r = [QM_r, QM_r]
            KM_r = alloc([128, 4, 512], BF16, nb=1); KM_r = [KM_r, KM_r]
            lrT_r = alloc([128, 128], F32, nb=2)
            lgb = alloc([128, 256], F32)
            lsp_r = alloc([128, 256], F32, nb=2)
            Eq_r = alloc([128, 256], F32, nb=2)
            Ek_r = alloc([128, 256], F32, nb=2)
            qg_r = alloc([128, 2, 128], BF16, nb=2)
            kg_r = alloc([128, 2, 128], BF16, nb=2)
            lvb_r = alloc([128, 256], BF16, nb=2)
            qkT_r = alloc([128, 4, 128], BF16, nb=2)
            qgTs_r = alloc([128, 2, 512], BF16, nb=2)
            Qblk_r = alloc([128, 2, 512], BF16, nb=2)
            attm_r = alloc([128, 2, 512], BF16, nb=2)
            yos_r = alloc([128, 4, 256], F32, nb=2)
            stg = alloc([128, 512], F32, nb=2)
            xbst = alloc([128, 3, 512], F32)
            for v_ in vaug_r + vaugm_r:
                fw.op("dve", lambda e, v_=v_: e.memset(v_.a, 1.0), [], [v_])
            pT = banks[0]
            pTb = pT.a.bitcast(BF16)
            pgr = banks[1:3]
            pgi = [0]
            pgc = [0]
            mi = [0]
            lane_banks = {"s2a": banks[3:4], "s2b": banks[4:6], "s3": banks[6:8]}
            lane_cnt = {"s2a": 0, "s2b": 0, "s3": 0}
            cur_lane = ["s2a"]
            smb = alloc([128, 8], F32, nb=3)

            def mbank():
                ln = cur_lane[0]
                bl = lane_banks[ln]
                b = bl[lane_cnt[ln] % len(bl)]
                lane_cnt[ln] += 1
                return b

            def pgbank():
                b = pgr[pgi[0] % 2]
                pgi[0] += 1
                return b

            GRP = [(0, 512), (512, 512), (1024, 512), (1536, 456)]
            ctxs = []
            for sj, (s, t0, g0) in enumerate(supers):
                for u in range(4):
                    ctxs.append(dict(s=s, t0=t0, g0=g0, u=u, sj=sj, k=len(ctxs)))

            def S1(cx):
                s, t0, g0, u, sj, k = cx['s'], cx['t0'], cx['g0'], cx['u'], cx['sj'], cx['k']
                hTs = hT[sj % 2]
                gt = g0 + u * 128
                tl = t0 + u * 128
                X = xt[k % 2]
                P = ptok[k % 3]
                TB = tabt[k % 3]
                S1_ = sm[k % 3]
                S2_ = sm2[k % 3]
                HB = hb[k % 2]
                dma("sp", X.a, xsrc[gt:gt + 128, :], [], [X])
                dma("sp", TB.a, tab_d[tl:tl + 128, :], [], [TB])
                act(junk.a, X.a, AF.Square, [X], [junk, S1_], accum=S1_.a[:, 0:1])
                rstd_from_ssq(S1_.a[:, 0:1], DM, S1_.a[:, 1:2], [S1_], [S1_], T(S2_.a[:, 0:1], S2_.b))
                stt("dve", HB.a, X.a, S1_.a[:, 1:2], g1bc.a, ALU.mult, ALU.mult, [X, S1_, g1bc], [HB])
                for c in range(8):
                    tr(pTb[:, c * 128:(c + 1) * 128], HB.a[:, c * 128:(c + 1) * 128], idb.a, [HB, idb], [pT])
                cp("act", hTs.a[:, :, u * 128:(u + 1) * 128], pTb.rearrange("p (c t) -> p c t", c=8), [pT], [hTs])
                pend_ev = []

                def evac():
                    gi, (c0, n), pb = pend_ev.pop(0)
                    cp("act" if gi % 2 == 0 else "dve", P.a[:, c0:c0 + n], pb.a[:, 0:n], [pb], [P])

                for gi, (c0, n) in enumerate(GRP):
                    pb = pgbank()
                    for c in range(8):
                        mm(pb.a[:, 0:n], hTs.a[:, c, u * 128:(u + 1) * 128], win.a[:, c, c0:c0 + n],
                           c == 0, c == 7, [hTs, win], [pb])
                    pend_ev.append((gi, (c0, n), pb))
                    if len(pend_ev) > 1:
                        evac()
                while pend_ev:
                    evac()
                dma("qpool", zr_d[gt:gt + 128, 0:256], P.a[:, 1728:1984], [P], [dscr])
                dma("qpool", zr_d[gt:gt + 128, 256:512], P.a[:, 1024:1280], [P], [dscr])
                dma("qpool", dtr_d[:, gt // 128, :], P.a[:, 1984:1992], [P], [dscr])

            def S1c(cx):
                s, t0, g0, u, sj, k = cx['s'], cx['t0'], cx['g0'], cx['u'], cx['sj'], cx['k']
                hTs = hT[sj % 2]
                if u == 3:
                    for fc in range(6):
                        mb = pgbank()
                        for c in range(8):
                            mm(mb.a[:, 0:512], win.a[:, c, 1992 + fc * 128:1992 + (fc + 1) * 128], hTs.a[:, c, :],
                               c == 0, c == 7, [win, hTs], [mb])
                        cp("act" if fc % 2 == 0 else "dve", xbst.a[:, fc % 3, :], mb.a[:, 0:512], [mb], [xbst])
                        if fc % 3 == 2:
                            col0 = g0 + 4 * s + 2
                            dma("qpool", xbcpre[fc - 2:fc + 1, :, col0:col0 + 512].rearrange("c p t -> p c t"), xbst.a, [xbst], [dscr])

            def S2(cx):
                s, t0, g0, u, sj, k = cx['s'], cx['t0'], cx['g0'], cx['u'], cx['sj'], cx['k']
                P = ptok[k % 3]
                TB = tabt[k % 3]
                S1_ = sm[k % 3]
                S2_ = sm2[k % 3]
                qrb, krb, cqn, cT = qrb_r[k % 2], krb_r[k % 2], cqn_r[k % 2], cT_r[k % 2]
                qmb, kmb = qmb_r[k % 2], kmb_r[k % 2]
                QA, QB, KTs, vaug, vaugm, QM, KM = (QA_r[sj % 2], QB_r[sj % 2], KTs_r[sj % 2], vaug_r[sj % 2],
                                                    vaugm_r[sj % 2], QM_r[sj % 2], KM_r[sj % 2])
                qk = P.a[:, 0:384].rearrange("p (h d) -> p h d", h=6)
                act(tA.a, qk, AF.Square, [P], [tA])
                red(S1_.a[:, 2:8], tA.a, [tA], [S1_])
                rstd_from_ssq(S1_.a[:, 2:8], 64, S1_.a[:, 8:14], [S1_], [S1_], T(S2_.a[:, 2:8], S2_.b))
                tt("dve", tB.a, qk, S1_.a[:, 8:14].unsqueeze(2).to_broadcast([128, 6, 64]), ALU.mult, [P, S1_], [tB])
                tt("dve", tB.a, tB.a, gqk.a, ALU.mult, [tB, gqk], [tB])
                C64 = TB.a[:, 0:64].unsqueeze(1).to_broadcast([128, 6, 64])
                tt("dve", tC.a, tB.a, C64, ALU.mult, [tB, TB], [tC])
                tBv = tB.a.rearrange("p h (b f d) -> p h b f d", b=2, f=2)
                tDv = tD.a.rearrange("p h (b f d) -> p h b f d", b=2, f=2)
                S64v = TB.a[:, 64:128].rearrange("p (b f d) -> p b f d", b=2, f=2)
                for f in range(2):
                    tt("dve", tDv[:, :, :, f, :], tBv[:, :, :, 1 - f, :],
                       S64v[:, :, f, :].unsqueeze(1).to_broadcast([128, 6, 2, 16]), ALU.mult, [tB, TB], [tD])
                tt("dve", qrb.a.rearrange("p (ha hb d) -> p hb ha d", ha=2, hb=2),
                   tC.a[:, 0:4, :].rearrange("p (hb ha) d -> p hb ha d", ha=2),
                   tD.a[:, 0:4, :].rearrange("p (hb ha) d -> p hb ha d", ha=2), ALU.add, [tC, tD], [qrb])
                tt("dve", krb.a.rearrange("p (h d) -> p h d", h=2), tC.a[:, 4:6, :], tD.a[:, 4:6, :], ALU.add,
                   [tC, tD], [krb])
                cp("act", vaug.a[:, u, :].rearrange("p (k e) -> p k e", k=2)[:, :, 0:64],
                   P.a[:, 384:512].rearrange("p (k e) -> p k e", k=2), [P], [vaug])
                mb = mbank()
                mbb = mb.a.bitcast(BF16)
                tr(mbb[:, 0:128], qrb.a[:, 0:128], idb.a, [qrb, idb], [mb])
                tr(mbb[:, 128:256], qrb.a[:, 128:256], idb.a, [qrb, idb], [mb])
                tr(mbb[:, 256:384], krb.a, idb.a, [krb, idb], [mb])
                cp("act", QA.a[:, u * 128:(u + 1) * 128], mbb[:, 0:128], [mb], [QA])
                cp("act", QB.a[:, u * 128:(u + 1) * 128], mbb[:, 128:256], [mb], [QB])
                cp("act", KTs.a[:, u * 128:(u + 1) * 128], mbb[:, 256:384], [mb], [KTs])
                if u == 3:
                    dma("qpool", QTg[0, :, g0:g0 + 512], QA.a[0:64, :], [QA], [dscr])
                    dma("qpool", QTg[2, :, g0:g0 + 512], QA.a[64:128, :], [QA], [dscr])
                    dma("qpool", QTg[1, :, g0:g0 + 512], QB.a[0:64, :], [QB], [dscr])
                    dma("qpool", QTg[3, :, g0:g0 + 512], QB.a[64:128, :], [QB], [dscr])
                    dma("qpool", KTg[:, g0:g0 + 512], KTs.a, [KTs], [dscr])
                    dma("qpool", VAg[:, g0 // 128:g0 // 128 + 4, :], vaug.a, [vaug], [dscr])

            def S2b(cx):
                s, t0, g0, u, sj, k = cx['s'], cx['t0'], cx['g0'], cx['u'], cx['sj'], cx['k']
                P = ptok[k % 3]
                TB = tabt[k % 3]
                SB_ = smb[k % 3]
                cqn, cT = cqn_r[k % 2], cT_r[k % 2]
                qmb, kmb = qmb_r[k % 2], kmb_r[k % 2]
                vaugm, QM, KM = vaugm_r[sj % 2], QM_r[sj % 2], KM_r[sj % 2]
                act(junk.a[:, 0:256], P.a[:, 1312:1568], AF.Square, [P], [junk, SB_], accum=SB_.a[:, 0:1])
                act(junk.a[:, 256:384], P.a[:, 1568:1696], AF.Square, [P], [junk, SB_], accum=SB_.a[:, 1:2])
                rstd_from_ssq(SB_.a[:, 0:1], 256, SB_.a[:, 4:5], [SB_], [SB_], T(SB_.a[:, 2:3], SB_.b))
                rstd_from_ssq(SB_.a[:, 1:2], 128, SB_.a[:, 5:6], [SB_], [SB_], T(SB_.a[:, 3:4], SB_.b))
                stt("dve", cqn.a[:, 0:256], P.a[:, 1312:1568], SB_.a[:, 4:5], gcq.a, ALU.mult, ALU.mult, [P, SB_, gcq], [cqn])
                stt("dve", cqn.a[:, 256:384], P.a[:, 1568:1696], SB_.a[:, 5:6], gckv.a, ALU.mult, ALU.mult, [P, SB_, gckv], [cqn])
                mb = mbank()
                mbb = mb.a.bitcast(BF16)
                for c in range(3):
                    tr(mbb[:, c * 128:(c + 1) * 128], cqn.a[:, c * 128:(c + 1) * 128], idb.a, [cqn, idb], [mb])
                cp("act", cT.a.rearrange("p c t -> p (c t)"), mbb[:, 0:384], [mb], [cT])
                mq = mbank()
                for c in range(2):
                    mm(mq.a[:, 0:384], cT.a[:, c, :], wuq.a[:, c, :], c == 0, c == 1, [cT, wuq], [mq])
                mkv = mbank()
                mm(mkv.a[:, 0:512], cT.a[:, 2, :], wukv.a, True, True, [cT, wukv], [mkv])
                mqv = mq.a[:, 0:384].rearrange("p (h d) -> p h d", h=4)
                cp("dve", mt1.a[:, 0:4, :], mqv[:, :, 64:96], [mq], [mt1])
                cp("dve", mt1.a[:, 4, :], P.a[:, 1696:1728], [P], [mt1])
                C32 = TB.a[:, 128:160].unsqueeze(1).to_broadcast([128, 5, 32])
                m1v = mt1.a.rearrange("p h (b f d) -> p h b f d", b=2, f=2)
                m2v = mt2.a.rearrange("p h (b f d) -> p h b f d", b=2, f=2)
                S32v = TB.a[:, 160:192].rearrange("p (b f d) -> p b f d", b=2, f=2)
                for f in range(2):
                    tt("dve", m2v[:, :, :, f, :], m1v[:, :, :, 1 - f, :],
                       S32v[:, :, f, :].unsqueeze(1).to_broadcast([128, 5, 2, 8]), ALU.mult, [mt1, TB], [mt2])
                tt("dve", mt1.a, mt1.a, C32, ALU.mult, [mt1, TB], [mt1])
                tt("dve", qmb.a[:, :, 64:96], mt1.a[:, 0:4, :], mt2.a[:, 0:4, :], ALU.add, [mt1, mt2], [qmb])
                tt("dve", mt1.a[:, 4, :], mt1.a[:, 4, :], mt2.a[:, 4, :], ALU.add, [mt1, mt2], [mt1])
                cp("dve", kmb.a[:, :, 64:96], mt1.a[:, 4:5, :].to_broadcast([128, 4, 32]), [mt1], [kmb])
                cp("act", qmb.a[:, :, 0:64], mqv[:, :, 0:64], [mq], [qmb])
                mkvv = mkv.a[:, 0:512].rearrange("p (h d) -> p h d", h=4)
                cp("act", kmb.a[:, :, 0:64], mkvv[:, :, 0:64], [mkv], [kmb])
                cp("act", vaugm.a[:, :, u, :].rearrange("p a (hh e) -> p a hh e", hh=2)[:, :, :, 0:64],
                   mkvv[:, :, 64:128].rearrange("p (a hh) e -> p a hh e", hh=2), [mkv], [vaugm])
                mb = mbank()
                mbb = mb.a.bitcast(BF16)
                mb2 = mbank()
                mbb2 = mb2.a.bitcast(BF16)
                for h in range(4):
                    tr(mbb[0:96, h * 128:(h + 1) * 128], qmb.a[:, h, :], idb.a, [qmb, idb], [mb])
                    tr(mbb2[0:96, h * 128:(h + 1) * 128], kmb.a[:, h, :], idb.a, [kmb, idb], [mb2])
                cp("act", QM.a[0:96, :, u * 128:(u + 1) * 128], mbb[0:96, 0:512].rearrange("p (h t) -> p h t", h=4), [mb], [QM])
                cp("act", KM.a[0:96, :, u * 128:(u + 1) * 128], mbb2[0:96, 0:512].rearrange("p (h t) -> p h t", h=4), [mb2], [KM])
                if u == 3:
                    for a_ in range(2):
                        dma("qpool", VAm[a_, :, g0 // 128:g0 // 128 + 4, :], vaugm.a[:, a_, :, :], [vaugm], [dscr])
                    dma("qpool", QTm[:, :, g0:g0 + 512].rearrange("h p t -> p h t"), QM.a[0:96], [QM], [dscr])
                    dma("qpool", KTm[:, :, g0:g0 + 512].rearrange("h p t -> p h t"), KM.a[0:96], [KM], [dscr])

            def S3(cx):
                s, t0, g0, u, sj, k = cx['s'], cx['t0'], cx['g0'], cx['u'], cx['sj'], cx['k']
                gt = g0 + u * 128
                ch = gt // 128
                P = ptok[k % 3]
                lrT, lsp, Eq, Ek = lrT_r[k % 2], lsp_r[k % 2], Eq_r[k % 2], Ek_r[k % 2]
                qg, kg, lvb, qkT, Qblk, attm = qg_r[k % 2], kg_r[k % 2], lvb_r[k % 2], qkT_r[k % 2], Qblk_r[k % 2], attm_r[k % 2]
                qgTs, yos = qgTs_r[sj % 2], yos_r[sj % 2]
                mb = mbank()
                tr(mb.a[0:32, 0:128], P.a[:, 1280:1312], IDF, [P, cm], [mb])
                cp("dve", lrT.a[0:32, :], mb.a[0:32, 0:128], [mb], [lrT])
                mb = mbank()
                mm(mb.a[:, 0:256], lrT.a[0:32, :], w2blk.a[0:32, :], True, True, [lrT, w2blk], [mb])
                tt("dve", lgb.a, mb.a[:, 0:256], gateb.a, ALU.add, [mb, gateb], [lgb])
                act(lgb.a, lgb.a, AF.Exp, [lgb], [lgb], scale=-1.0)
                act(lsp.a, lgb.a, AF.Ln, [lgb], [lsp], bias=onesf.a[:, 0:1])
                mb = mbank()
                mm(mb.a[:, 0:128], UFS, lsp.a[:, 0:128], True, True, [cm, lsp], [mb])
                mm(mb.a[:, 128:256], UBS, lsp.a[:, 128:256], True, True, [cm, lsp], [mb])
                mm(mb.a[:, 256:257], lsp.a[:, 0:128], CNEG[:, 0:1], True, True, [cm, lsp], [mb])
                mm(mb.a[:, 257:258], lsp.a[:, 128:256], CNEG[:, 0:1], True, True, [cm, lsp], [mb])
                act(Eq.a, mb.a[:, 0:256], AF.Exp, [mb], [Eq])
                act(Ek.a, mb.a[:, 0:256], AF.Exp, [mb], [Ek], scale=-1.0)
                act(egl.a[:, ch, :], mb.a[:, 256:258], AF.Exp, [mb], [egl])
                lq = P.a[:, 512:640].unsqueeze(1).to_broadcast([128, 2, 128])
                lk = P.a[:, 640:768].unsqueeze(1).to_broadcast([128, 2, 128])
                stt("dve", qg.a, lq, 32 ** -0.5, Eq.a.rearrange("p (d k) -> p d k", d=2), ALU.mult, ALU.mult, [P, Eq], [qg])
                tt("dve", kg.a, lk, Ek.a.rearrange("p (d k) -> p d k", d=2), ALU.mult, [P, Ek], [kg])
                cp("act", lvb.a, P.a[:, 768:1024], [P], [lvb])
                mb = mbank()
                mbb = mb.a.bitcast(BF16)
                for d in range(2):
                    tr(mbb[:, d * 128:(d + 1) * 128], qg.a[:, d, :], idb.a, [qg, idb], [mb])
                    tr(mbb[:, (2 + d) * 128:(3 + d) * 128], kg.a[:, d, :], idb.a, [kg, idb], [mb])
                cp("act", qkT.a.rearrange("p c t -> p (c t)"), mbb[:, 0:512], [mb], [qkT])
                cp("dve", qgTs.a[:, :, u * 128:(u + 1) * 128], qkT.a[:, 0:2, :], [qkT], [qgTs])
                for d in range(2):
                    tt("dve", Qblk.a[:, d, :].rearrange("p (h l) -> p h l", h=4),
                       qkT.a[:, d:d + 1, :].to_broadcast([128, 4, 128]),
                       BM4.unsqueeze(2).to_broadcast([128, 4, 128]), ALU.mult, [qkT, cm], [Qblk])
                po = mbank()
                for d in range(2):
                    mb = mbank()
                    mm(mb.a[:, 0:512], qkT.a[:, 2 + d, :], Qblk.a[:, d, :], True, True, [qkT, Qblk], [mb])
                    msk = (UF if d == 0 else UB).unsqueeze(1).to_broadcast([128, 4, 128])
                    tt("dve", attm.a[:, d, :].rearrange("p (h l) -> p h l", h=4),
                       mb.a[:, 0:512].rearrange("p (h l) -> p h l", h=4), msk, ALU.mult, [mb, cm], [attm])
                for h in range(4):
                    for d in range(2):
                        mm(po.a[:, h * 64:(h + 1) * 64], attm.a[:, d, h * 128:(h + 1) * 128],
                           lvb.a[:, h * 64:(h + 1) * 64], d == 0, d == 1, [attm, lvb], [po])
                cp("act", yos.a[:, u, :], po.a[:, 0:256], [po], [yos])
                mb = mbank()
                for d in range(2):
                    mm(mb.a[:, d * 256:(d + 1) * 256], kg.a[:, d, :], lvb.a, True, True, [kg, lvb], [mb])
                SG = stg[ch % 2]
                for d in range(2):
                    stt("dve", SG.a[:, d * 256:(d + 1) * 256], mb.a[:, d * 256:(d + 1) * 256], egl.a[:, ch, d:d + 1],
                        BMBIG, ALU.mult, ALU.mult, [mb, egl, cm], [SG])
                dma("qpool", stgla_d[ch], SG.a, [SG], [dscr])
                if u == 3:
                    dma("qpool", qgTd[:, :, g0:g0 + 512].rearrange("d p t -> p d t"), qgTs.a, [qgTs], [dscr])
                    dma("qpool", yo_d[g0:g0 + 512, 256:512].rearrange("(u p) c -> p u c", p=128), yos.a, [yos], [dscr])

            NS = len(ctxs)

            def cap_lane(name, fn, cx):
                cur_lane[0] = name
                return fw.capture(fn, cx)

            for k in range(NS + 2):
                lanes = []
                def s1_lane(k=k):
                    if k < NS:
                        S1(ctxs[k])
                    if 0 <= k - 1 < NS and ctxs[k - 1]['u'] == 3:
                        S1c(ctxs[k - 1])
                lanes.append(fw.capture(s1_lane))
                if 0 <= k - 1 < NS:
                    lanes.append(cap_lane("s2a", S2, ctxs[k - 1]))
                    lanes.append(cap_lane("s2b", S2b, ctxs[k - 1]))
                if 0 <= k - 2 < NS:
                    lanes.append(cap_lane("s3", S3, ctxs[k - 2]))
                fw.merge_emit(lanes)
            fw.barrier()
            bump[0] = persist_mark

            xbT = alloc([128, 6, 516], F32, nb=2)
            acc = alloc([128, 6, 512], F32)
            xcb = alloc([128, 6, 512], BF16, nb=2)
            dtt = alloc([128, 4, 8], F32, nb=2)
            xsB = alloc([128, 512], BF16, nb=3)
            d1 = alloc([128, 32], F32, nb=3)
            d2 = alloc([128, 32], F32, nb=3)
            UD = alloc([128, 512], F32, nb=2)
            Ld = alloc([128, 2, 512], F32)
            scT_r = alloc([128, 512], F32, nb=2)
            Mt_r = alloc([128, 2, 512], BF16, nb=2)
            xdt = alloc([128, 512], BF16)
            xde = alloc([128, 512], BF16)
            ys_r = alloc([128, 4, 256], F32, nb=2)
            ytmp = alloc([128, 256], F32)
            eas_r = alloc([128, 4, 8], F32, nb=2)
            stg = alloc([128, 512], F32, nb=2)
            a2_banks = {"t2": banks[0:3], "t3": banks[3:5], "t4": banks[5:8]}
            a2_cnt = {"t2": 0, "t3": 0, "t4": 0}
            a2_lane = ["t2"]

            def mbank8():
                ln = a2_lane[0]
                bl = a2_banks[ln]
                b = bl[a2_cnt[ln] % len(bl)]
                a2_cnt[ln] += 1
                return b

            def T1(sj):
                s, t0, g0 = supers[sj]
                XB = xbT[sj % 2]
                XC = xcb[sj % 2]
                DT = dtt[sj % 2]
                col0 = g0 + 4 * s
                dma("sp", XB.a, xbcpre[:, :, col0:col0 + 516].rearrange("c p t -> p c t"), [], [XB])
                dma("sp", DT.a, dtr_d[:, g0 // 128:g0 // 128 + 4, :], [], [DT])
                for fc in range(6):
                    ts("dve", acc.a[:, fc, :], XB.a[:, fc, 0:512], cw.a[:, fc, 0:1], None, ALU.mult, None, [XB, cw], [acc])
                    for k in range(1, 5):
                        stt("dve", acc.a[:, fc, :], XB.a[:, fc, k:k + 512], cw.a[:, fc, k:k + 1], acc.a[:, fc, :],
                            ALU.mult, ALU.add, [XB, cw, acc], [acc])
                    act(XC.a[:, fc, :], acc.a[:, fc, :], AF.Silu, [acc, cb], [XC], bias=cb.a[:, fc:fc + 1])
                dma("qpool", CTd[:, :, g0:g0 + 512].rearrange("g p t -> p g t"), XC.a[:, 4:6, :], [XC], [dscr])

            cxs2 = []
            for sj, (s, t0, g0) in enumerate(supers):
                for u in range(4):
                    cxs2.append(dict(sj=sj, g0=g0, u=u, k=len(cxs2)))

            def T2(cx):
                sj, g0, u, k = cx['sj'], cx['g0'], cx['u'], cx['k']
                XC = xcb[sj % 2]
                DT = dtt[sj % 2]
                eas = eas_r[sj % 2]
                gt = g0 + u * 128
                ch = gt // 128
                tsl = slice(u * 128, (u + 1) * 128)
                XS = xsB[k % 3]
                A1 = d1[k % 3]
                A2 = d2[k % 3]
                scT = scT_r[k % 2]
                mb = mbank8()
                mbb = mb.a.bitcast(BF16)
                for c in range(4):
                    tr(mbb[:, c * 128:(c + 1) * 128], XC.a[:, c, tsl], idb.a, [XC, idb], [mb])
                cp("act", XS.a, mbb[:, 0:512], [mb], [XS])
                tt("dve", A1.a[:, 0:8], DT.a[:, u, :], dtb.a, ALU.add, [DT, dtb], [A1])
                act(A1.a[:, 0:8], A1.a[:, 0:8], AF.Exp, [A1], [A1])
                act(A1.a[:, 0:8], A1.a[:, 0:8], AF.Ln, [A1], [A1], bias=onesf.a[:, 0:1])
                tt("dve", A1.a[:, 8:16], A1.a[:, 0:8], abc.a, ALU.mult, [A1, abc], [A1])
                mb = mbank8()
                mm(mb.a[:, 0:4], UF, A1.a[:, 8:12], True, True, [cm, A1], [mb])
                mm(mb.a[:, 4:8], UB, A1.a[:, 12:16], True, True, [cm, A1], [mb])
                mm(mb.a[:, 8:16], onesf.a, A1.a[:, 8:16], True, True, [onesf, A1], [mb])
                cp("dve", A2.a[:, 0:16], mb.a[:, 0:16], [mb], [A2])
                ts("dve", A2.a[:, 16:24], A2.a[:, 0:8], -1.0, None, ALU.mult, None, [A2], [A2])
                act(eas.a[:, u, :], A2.a[:, 0:8], AF.Exp, [A2], [eas])
                tt("dve", A2.a[:, 24:32], A2.a[:, 8:16], A2.a[:, 0:8], ALU.subtract, [A2], [A2])
                act(A2.a[:, 24:32], A2.a[:, 24:32], AF.Exp, [A2], [A2])
                act(cdec.a[:, ch, :], A2.a[:, 8:16], AF.Exp, [A2], [cdec])
                msc = mbank8()
                for g in range(2):
                    mm(msc.a[:, g * 128:(g + 1) * 128], XC.a[:, 2 + g, tsl], XC.a[:, 4 + g, tsl], True, True, [XC], [msc])
                for d in range(2):
                    tt("dve", scT.a[:, d * 256:(d + 1) * 256].rearrange("p (g l) -> p g l", g=2),
                       msc.a[:, 0:256].rearrange("p (g l) -> p g l", g=2),
                       (UF if d == 0 else UB).unsqueeze(1).to_broadcast([128, 2, 128]), ALU.mult, [msc, cm], [scT])
                if u == 3:
                    dma("qpool", eacs_d[:, g0 // 128:g0 // 128 + 4, :], eas.a, [eas], [dscr])

            def T3(cx):
                sj, g0, u, k = cx['sj'], cx['g0'], cx['u'], cx['k']
                A1 = d1[k % 3]
                A2 = d2[k % 3]
                scT = scT_r[k % 2]
                Mt = Mt_r[k % 2]
                for d in range(2):
                    U_ = UD[d]
                    tt("dve", U_.a.rearrange("p (h l) -> p h l", h=4),
                       (UF if d == 0 else UB).unsqueeze(1).to_broadcast([128, 4, 128]),
                       A1.a[:, 8 + 4 * d:12 + 4 * d].unsqueeze(2).to_broadcast([128, 4, 128]), ALU.mult, [cm, A1], [U_])
                    mb = mbank8()
                    mm(mb.a[:, 0:512], SU[d], U_.a, True, True, [cm, U_], [mb])
                    act(Ld.a[:, d, :], mb.a[:, 0:512], AF.Exp, [mb], [Ld])
                    tt("dve", Mt.a[:, d, :].rearrange("p (g hh l) -> p g hh l", g=2, hh=2),
                       Ld.a[:, d, :].rearrange("p (g hh l) -> p g hh l", g=2, hh=2),
                       scT.a[:, d * 256:(d + 1) * 256].rearrange("p (g l) -> p g l", g=2).unsqueeze(2).to_broadcast([128, 2, 2, 128]),
                       ALU.mult, [Ld, scT], [Mt])

            def T4(cx):
                sj, g0, u, k = cx['sj'], cx['g0'], cx['u'], cx['k']
                gt = g0 + u * 128
                ch = gt // 128
                XS = xsB[k % 3]
                A1 = d1[k % 3]
                A2 = d2[k % 3]
                Mt = Mt_r[k % 2]
                ys = ys_r[sj % 2]
                tt("dve", xdt.a.rearrange("p (d h e) -> p d h e", d=2, h=4),
                   XS.a[:, 0:256].rearrange("p (h e) -> p h e", h=4).unsqueeze(1).to_broadcast([128, 2, 4, 64]),
                   A1.a[:, 0:8].rearrange("p (d h) -> p d h", d=2).unsqueeze(3).to_broadcast([128, 2, 4, 64]),
                   ALU.mult, [XS, A1], [xdt])
                tt("dve", xde.a.rearrange("p (d h e) -> p d h e", d=2, h=4),
                   xdt.a.rearrange("p (d h e) -> p d h e", d=2, h=4),
                   A2.a[:, 24:32].rearrange("p (d h) -> p d h", d=2).unsqueeze(3).to_broadcast([128, 2, 4, 64]),
                   ALU.mult, [xdt, A2], [xde])
                py = mbank8()
                for h in range(4):
                    for d in range(2):
                        mm(py.a[:, h * 64:(h + 1) * 64], Mt.a[:, d, h * 128:(h + 1) * 128],
                           xdt.a[:, d * 256 + h * 64:d * 256 + (h + 1) * 64], d == 0, d == 1, [Mt, xdt], [py])
                tt("dve", ytmp.a.rearrange("p (h e) -> p h e", h=4), XS.a[:, 0:256].rearrange("p (h e) -> p h e", h=4),
                   dbc.a.unsqueeze(2).to_broadcast([128, 4, 64]), ALU.mult, [XS, dbc], [ytmp])
                tt("dve", ys.a[:, u, :], py.a[:, 0:256], ytmp.a, ALU.add, [py, ytmp], [ys])
                pst = mbank8()
                xdev = xde.a.rearrange("p (d h e) -> p d h e", d=2, h=4)
                for g in range(2):
                    mm(pst.a[:, g * 256:(g + 1) * 256].rearrange("p (d hh e) -> p d hh e", d=2, hh=2),
                       XS.a[:, 256 + g * 128:256 + (g + 1) * 128], xdev[:, :, 2 * g:2 * g + 2, :], True, True, [XS, xde], [pst])
                SG = stg[ch % 2]
                cp("act", SG.a, pst.a[:, 0:512], [pst], [SG])
                dma("qpool", stssd_d[ch], SG.a, [SG], [dscr])
                if u == 3:
                    dma("qpool", yo_d[g0:g0 + 512, 0:256].rearrange("(u p) c -> p u c", p=128), ys.a, [ys], [dscr])

            NS2 = len(cxs2)
            T1(0)
            t1ops = []
            for k in range(NS2 + 2):
                lanes = []
                if k < NS2:
                    if cxs2[k]['u'] == 0:
                        t1ops = fw.capture(T1, cxs2[k]['sj'] + 1) if cxs2[k]['sj'] + 1 < len(supers) else []
                    uu = cxs2[k]['u']
                    q4 = (len(t1ops) + 3) // 4
                    lanes.append(t1ops[uu * q4:(uu + 1) * q4])
                    a2_lane[0] = "t2"
                    lanes.append(fw.capture(T2, cxs2[k]))
                if 0 <= k - 1 < NS2:
                    a2_lane[0] = "t3"
                    lanes.append(fw.capture(T3, cxs2[k - 1]))
                if 0 <= k - 2 < NS2:
                    a2_lane[0] = "t4"
                    lanes.append(fw.capture(T4, cxs2[k - 2]))
                fw.merge_emit(lanes)
            fw.barrier()
            bump[0] = persist_mark

            Sssd = alloc([128, 512], F32)
            Sssdb = alloc([128, 512], BF16)
            Sgla = alloc([128, 2, 256], F32)
            Sglab = alloc([128, 2, 256], BF16)
            NRB = 3
            accB = alloc([128, 512], F32, nb=NRB)
            eaB = alloc([128, 8], F32, nb=NRB)
            stS = alloc([128, 512], F32, nb=NRB)
            stG = alloc([128, 512], F32, nb=NRB)
            ctB = alloc([128, 2, 128], BF16, nb=NRB)
            qgB = alloc([128, 2, 128], BF16, nb=NRB)
            zrB = alloc([128, 512], F32, nb=NRB)
            szB = alloc([128, 512], F32)
            tmpB = alloc([128, 256], F32)
            tmpB2 = alloc([128, 256], F32)
            smB = alloc([128, 16], F32, nb=2)
            outB = alloc([128, 512], BF16)
            mxs = alloc([128, 4, 128], BF16, nb=2)
            yoB = [Buf() for _ in range(nchunks)]
            bbanks = banks[7:8]
            bbi = [0]

            def bbank():
                b = bbanks[bbi[0] % len(bbanks)]
                bbi[0] += 1
                return b

            def genB():
                it = 0
                for s in range(nseq):
                    nch = seqs[s] // 128
                    c_base = offs[s] // 128
                    for sweep in range(2):
                        d = 1 - sweep
                        fw.op("dve", lambda e: e.memset(Sssd.a, 0.0), [], [Sssd])
                        fw.op("dve", lambda e: e.memset(Sssdb.a, 0.0), [], [Sssdb])
                        fw.op("dve", lambda e: e.memset(Sgla.a, 0.0), [], [Sgla])
                        fw.op("dve", lambda e: e.memset(Sglab.a, 0.0), [], [Sglab])
                        order = list(range(nch - 1, -1, -1)) if d == 1 else list(range(nch))
                        for oi, cl in enumerate(order):
                            ch = c_base + cl
                            gt = ch * 128
                            AC = accB[it % NRB]
                            EA = eaB[it % NRB]
                            SS = stS[it % NRB]
                            SGt = stG[it % NRB]
                            CT_ = ctB[it % NRB]
                            QG = qgB[it % NRB]
                            ZR = zrB[it % NRB]
                            SM = smB[it % 2]
                            MX = mxs[it % 2]
                            it += 1
                            dma("sp", AC.a, yo_d[gt:gt + 128, :], [yoB[ch]], [AC])
                            dma("sp", EA.a, eacs_d[:, ch, :], [], [EA])
                            dma("sp", SS.a, stssd_d[ch], [], [SS])
                            dma("sp", SGt.a, stgla_d[ch], [], [SGt])
                            dma("sp", CT_.a, CTd[:, :, gt:gt + 128].rearrange("g p t -> p g t"), [], [CT_])
                            dma("sp", QG.a, qgTd[:, :, gt:gt + 128].rearrange("d p t -> p d t"), [], [QG])
                            if sweep == 1:
                                dma("sp", ZR.a, zr_d[gt:gt + 128, :], [], [ZR])
                            Sv = Sssdb.a.rearrange("p (g d hh e) -> p g d hh e", g=2, d=2, hh=2)
                            S5 = Sssd.a.rearrange("p (g d hh e) -> p g d hh e", g=2, d=2, hh=2)
                            ST5 = SS.a.rearrange("p (g d hh e) -> p g d hh e", g=2, d=2, hh=2)
                            if oi > 0:
                                cp("dve", Sv[:, :, d, :, :], S5[:, :, d, :, :], [Sssd], [Sssdb])
                                cp("dve", Sglab.a[:, d, :], Sgla.a[:, d, :], [Sgla], [Sglab])
                            if sweep == 1:
                                act(szB.a, ZR.a, AF.Exp, [ZR], [szB], scale=-1.0)
                            pr = bbank()
                            for g in range(2):
                                mm(pr.a[:, g * 128:(g + 1) * 128].rearrange("p (hh e) -> p hh e", hh=2), CT_.a[:, g, :],
                                   Sv[:, g, d, :, :], True, True, [CT_, Sssdb], [pr])
                            mm(pr.a[:, 256:512], QG.a[:, d, :], Sglab.a[:, d, :], True, True, [QG, Sglab], [pr])
                            tt("dve", tmpB.a.rearrange("p (h e) -> p h e", h=4), pr.a[:, 0:256].rearrange("p (h e) -> p h e", h=4),
                               EA.a[:, 4 * d:4 * d + 4].unsqueeze(2).to_broadcast([128, 4, 64]), ALU.mult, [pr, EA], [tmpB])
                            tt("dve", AC.a[:, 0:256], AC.a[:, 0:256], tmpB.a, ALU.add, [AC, tmpB], [AC])
                            tt("dve", AC.a[:, 256:512], AC.a[:, 256:512], pr.a[:, 256:512], ALU.add, [AC, pr], [AC])
                            cdv = cdec.a[:, ch, 4 * d:4 * d + 4].rearrange("p (g hh) -> p g hh", g=2).unsqueeze(3).to_broadcast([128, 2, 2, 64])
                            tt("dve", S5[:, :, d, :, :], S5[:, :, d, :, :], cdv, ALU.mult, [Sssd, cdec], [Sssd])
                            tt("dve", S5[:, :, d, :, :], S5[:, :, d, :, :], ST5[:, :, d, :, :], ALU.add, [Sssd, SS], [Sssd])
                            stt("dve", Sgla.a[:, d, :], Sgla.a[:, d, :], egl.a[:, ch, d:d + 1], SGt.a[:, d * 256:(d + 1) * 256],
                                ALU.mult, ALU.add, [Sgla, egl, SGt], [Sgla])
                            if sweep == 0:
                                dma("qpool", yo_d[gt:gt + 128, :], AC.a, [AC], [yoB[ch]])
                            else:
                                ts("dve", szB.a, szB.a, 1.0, None, ALU.add, None, [szB], [szB])
                                fw.op("dve", lambda e: e.reciprocal(out=szB.a, in_=szB.a), [szB], [szB])
                                tt("dve", szB.a, szB.a, ZR.a, ALU.mult, [szB, ZR], [szB])
                                tt("dve", tmpB.a, AC.a[:, 0:256], szB.a[:, 0:256], ALU.mult, [AC, szB], [tmpB])
                                tt("dve", tmpB2.a, tmpB.a, tmpB.a, ALU.mult, [tmpB], [tmpB2])
                                red(SM.a[:, 0:1], tmpB2.a, [tmpB2], [SM])
                                tt("dve", tmpB2.a, AC.a[:, 256:512], AC.a[:, 256:512], ALU.mult, [AC], [tmpB2])
                                red(SM.a[:, 1:5], tmpB2.a.rearrange("p (h e) -> p h e", h=4), [tmpB2], [SM])
                                act(SM.a[:, 8:9], SM.a[:, 0:1], AF.Ln, [SM], [SM], bias=epsb.a[:, 0:1], scale=1.0 / 256)
                                act(SM.a[:, 9:13], SM.a[:, 1:5], AF.Ln, [SM], [SM], bias=epsb.a[:, 0:1], scale=1.0 / 64)
                                act(SM.a[:, 8:13], SM.a[:, 8:13], AF.Exp, [SM], [SM], scale=-0.5)
                                stt("dve", outB.a[:, 0:256], tmpB.a, SM.a[:, 8:9], gssd.a, ALU.mult, ALU.mult, [tmpB, SM, gssd], [outB])
                                tt("dve", tmpB2.a.rearrange("p (h e) -> p h e", h=4), AC.a[:, 256:512].rearrange("p (h e) -> p h e", h=4),
                                   SM.a[:, 9:13].unsqueeze(2).to_broadcast([128, 4, 64]), ALU.mult, [AC, SM], [tmpB2])
                                tt("dve", tmpB2.a.rearrange("p (h e) -> p h e", h=4), tmpB2.a.rearrange("p (h e) -> p h e", h=4),
                                   ggla.a.unsqueeze(1).to_broadcast([128, 4, 64]), ALU.mult, [tmpB2, ggla], [tmpB2])
                                tt("dve", outB.a[:, 256:512], tmpB2.a, szB.a[:, 256:512], ALU.mult, [tmpB2, szB], [outB])
                                mb = bbank()
                                mbb = mb.a.bitcast(BF16)
                                for c in range(4):
                                    tr(mbb[:, c * 128:(c + 1) * 128], outB.a[:, c * 128:(c + 1) * 128], idb.a, [outB, idb], [mb])
                                cp("dve", MX.a.rearrange("p c t -> p (c t)"), mbb[:, 0:512], [mb], [MX])
                                dma("qpool", mixT_d[0:2, :, gt:gt + 128].rearrange("c p t -> p c t"), MX.a[:, 0:2, :], [MX], [dscr])
                                dma("qpool", mixT_d[4:6, :, gt:gt + 128].rearrange("c p t -> p c t"), MX.a[:, 2:4, :], [MX], [dscr])
                            yield

            TM = max(seqs)
            nbm = TM // 128
            KG = alloc([128, TM], BF16)
            VG = alloc([128, nbm, 130], BF16)
            KM2 = alloc([128, 2, TM], BF16)
            VM2 = alloc([128, nbm, 130], BF16)
            qlo = alloc([128, 512], BF16, nb=2)
            qhi = alloc([128, 512], BF16, nb=2)
            qml = alloc([128, 512], BF16, nb=2)
            PT = alloc([128, 1024], BF16, nb=2)
            osb = alloc([128, 512], F32, nb=2)
            ob = alloc([128, 512], BF16, nb=2)
            rlr = alloc([128, 2, 512], BF16, nb=2)
            onesb = alloc([128, 64], BF16)
            fw.op("dve", lambda e: e.memset(onesb.a, 1.0), [], [onesb])
            for q_ in qlo + qhi:
                fw.op("dve", lambda e, q_=q_: e.memset(q_.a, 0.0), [], [q_])
            sc_pairs = [T(psall[:, 0:1024]), T(psall[:, 1024:2048])]
            po_banks = banks[4:6]
            bcb = banks[6]
            cnt = {"sc": 0, "po": 0, "q": 0, "pt": 0}
            gB = genB()
            pending_tail = [None, None]
            for s in range(nseq):
                Ts = seqs[s]
                nb = Ts // 128
                o0 = offs[s]
                for (kind, heads) in (("g", [0, 1, 2, 3]), ("m", [0, 1]), ("m", [2, 3])):
                    if kind == "g":
                        dma("sp", KG.a[:, 0:Ts], KTg[:, o0:o0 + Ts], [], [KG])
                        dma("sp", VG.a[:, 0:nb, :], VAg[:, o0 // 128:o0 // 128 + nb, :], [], [VG])
                    else:
                        for hi_, h in enumerate(heads):
                            dma("sp", KM2.a[0:96, hi_, 0:Ts], KTm[h, :, o0:o0 + Ts], [], [KM2])
                        dma("sp", VM2.a[:, 0:nb, :], VAm[heads[0] // 2, :, o0 // 128:o0 // 128 + nb, :], [], [VM2])
                    for j in range(Ts // 512):
                        gq0 = o0 + j * 512
                        for hi_, h in enumerate(heads):
                            qi = cnt["q"]
                            cnt["q"] += 1
                            if kind == "g":
                                if h < 2:
                                    Qt = qlo[qi % 2]
                                    dma("sp", Qt.a[0:64, :], QTg[h, :, gq0:gq0 + 512], [], [Qt])
                                else:
                                    Qt = qhi[qi % 2]
                                    dma("sp", Qt.a[64:128, :], QTg[h, :, gq0:gq0 + 512], [], [Qt])
                                kr_ = 128
                                scale = 64 ** -0.5
                                kv = h // 2

                                def kslice(kb):
                                    return KG.a[:, kb * 128:(kb + 1) * 128], KG

                                def vslice(kb, kv=kv):
                                    return VG.a[:, kb, kv * 65:(kv + 1) * 65], VG
                                mixc, mixp = 2 + h // 2, (h % 2) * 64
                            else:
                                Qt = qml[qi % 2]
                                dma("sp", Qt.a[0:96, :], QTm[h, :, gq0:gq0 + 512], [], [Qt])
                                kr_ = 96
                                scale = 96 ** -0.5

                                def kslice(kb, hi_=hi_):
                                    return KM2.a[0:96, hi_, kb * 128:(kb + 1) * 128], KM2

                                def vslice(kb, hi_=hi_):
                                    return VM2.a[:, kb, hi_ * 65:(hi_ + 1) * 65], VM2
                                mixc, mixp = 6 + h // 2, (h % 2) * 64
                            po = po_banks[cnt["po"] % 2]
                            OS = osb[cnt["po"] % 2]
                            OB = ob[cnt["po"] % 2]
                            cnt["po"] += 1
                            pend = []
                            fw.cap = []

                            def issue_qk(kp):
                                sc = sc_pairs[cnt["sc"] % 2]
                                cnt["sc"] += 1
                                for i2 in range(2):
                                    ka, kbuf = kslice(2 * kp + i2)
                                    mm(sc.a[:, i2 * 512:(i2 + 1) * 512], ka, Qt.a[0:kr_, :], True, True, [kbuf, Qt], [sc])
                                pt = PT[cnt["pt"] % 2]
                                cnt["pt"] += 1
                                act(pt.a, sc.a, AF.Exp, [sc], [pt], scale=scale)
                                pend.append((kp, pt))

                            def issue_pv():
                                kp, pt = pend.pop(0)
                                for i2 in range(2):
                                    kb = 2 * kp + i2
                                    va, vbuf = vslice(kb)
                                    mm(po.a[0:65, 0:512], va, pt.a[:, i2 * 512:(i2 + 1) * 512], kb == 0, kb == nb - 1, [vbuf, pt], [po])

                            npair = nb // 2
                            for kp in range(npair):
                                issue_qk(kp)
                                if kp == 1 and pending_tail[0] is not None:
                                    fw.cap.extend(pending_tail[0])
                                    pending_tail[0] = None
                                if kp == min(6, npair - 1) and pending_tail[1] is not None:
                                    fw.cap.extend(pending_tail[1])
                                    pending_tail[1] = None
                                if len(pend) > 1:
                                    issue_pv()
                            while pend:
                                issue_pv()
                            uops = fw.cap
                            fw.cap = []
                            RL = rlr[(cnt["po"] - 1) % 2]
                            cp("act", OS.a[0:65, :], po.a[0:65, 0:512], [po], [OS])
                            fw.op("dve", lambda e, OS=OS: e.reciprocal(out=OS.a[64:65, :], in_=OS.a[64:65, :]), [OS], [OS])
                            cp("dve", RL.a[64:65, 0, :], OS.a[64:65, :], [OS], [RL])
                            tt("dve", RL.a[64:65, 1, :], OS.a[64:65, :], RL.a[64:65, 0, :], ALU.subtract, [OS, RL], [RL])
                            pending_tail[0] = fw.cap
                            fw.cap = []
                            mm(bcb.a[0:64, 0:512], onesb.a[64:65, 0:64], RL.a[64:65, 0, :], True, False, [onesb, RL], [bcb])
                            mm(bcb.a[0:64, 0:512], onesb.a[64:65, 0:64], RL.a[64:65, 1, :], False, True, [onesb, RL], [bcb])
                            tt("dve", OB.a[0:64, :], OS.a[0:64, :], bcb.a[0:64, 0:512], ALU.mult, [OS, bcb], [OB])
                            dma("qpool", mixT_d[mixc, mixp:mixp + 64, gq0:gq0 + 512], OB.a[0:64, :], [OB], [dscr])
                            pending_tail[1] = fw.cap
                            fw.cap = []
                            next(gB, None)
                            bops = fw.cap
                            fw.cap = None
                            fw.merge_emit([uops, bops])
            for pi_ in range(2):
                if pending_tail[pi_] is not None:
                    for op_ in pending_tail[pi_]:
                        fw.op(*op_)
                    pending_tail[pi_] = None
            for _ in gB:
                pass
            fw.barrier()
            bump[0] = persist_mark

            wo = alloc([128, 8, DM], BF16)
            w2 = alloc([128, 32, DM], BF16)
            d_mark = bump[0]
            wstg = alloc([128, DM], F32, nb=2)
            for c in range(8):
                wsx = wstg[c % 2]
                dma("sp", wsx.a, Wd['w_out'][l, c * 128:(c + 1) * 128, :], [], [wsx])
                cp("dve" if c % 2 == 0 else "act", wo.a[:, c, :], wsx.a, [wsx], [wo])
            for c in range(32):
                wsx = wstg[c % 2]
                dma("sp", wsx.a, Wd['w_ff2'][l, c * 128:(c + 1) * 128, :], [], [wsx])
                cp("dve" if c % 2 == 0 else "act", w2.a[:, c, :], wsx.a, [wsx], [w2])
            if last:
                dma("sp", g1bc.a, Wd['final_norm_g'].partition_broadcast(128), [], [g1bc])
            fw.barrier()
            bump[0] = d_mark
            x1_r = alloc([128, 4, DM], F32, nb=2)
            x1u = [[Buf() for _ in range(4)] for _ in range(2)]
            hb2 = alloc([128, DM], BF16)
            junk2 = hb2
            h2T_r = alloc([128, 8, 512], BF16, nb=2)
            w1s = alloc([128, 8, 256], BF16, nb=2)
            actT = alloc([128, 32, 512], BF16)
            mixs = alloc([128, 8, 512], BF16)
            rtmp = alloc([128, 512], F32, nb=1)
            rtmp = [rtmp, rtmp]
            smD = alloc([128, 8], F32, nb=2)
            smE = alloc([128, 8], F32, nb=2)
            dcn = {"w1": 0, "k5": 0, "k6": 0}

            def DPa(sj):
                s, t0, g0 = supers[sj]
                x1 = x1_r[sj % 2]
                dma("sp", mixs.a, mixT_d[:, :, g0:g0 + 512].rearrange("c p t -> p c t"), [], [mixs])
                dma("sp", x1.a, xsrc[g0:g0 + 512, :].rearrange("(u p) c -> p u c", p=128), [], [x1] + x1u[sj % 2])
                for u in range(4):
                    for hf in range(2):
                        pb = banks[dcn["k5"] % 2]
                        dcn["k5"] += 1
                        for c in range(8):
                            mm(pb.a[:, 0:512], mixs.a[:, c, u * 128:(u + 1) * 128], wo.a[:, c, hf * 512:(hf + 1) * 512],
                               c == 0, c == 7, [mixs, wo], [pb])
                        tt("dve", x1.a[:, u, hf * 512:(hf + 1) * 512], x1.a[:, u, hf * 512:(hf + 1) * 512], pb.a[:, 0:512],
                           ALU.add, [x1, pb], [x1, x1u[sj % 2][u]])

            def DPb(sj):
                x1 = x1_r[sj % 2]
                h2T = h2T_r[sj % 2]
                for u in range(4):
                    SM = smD[u % 2]
                    act(junk2.a, x1.a[:, u, :], AF.Square, [x1u[sj % 2][u]], [junk2, SM], accum=SM.a[:, 0:1])
                    act(SM.a[:, 1:2], SM.a[:, 0:1], AF.Ln, [SM], [SM], bias=epsb.a[:, 0:1], scale=1.0 / DM)
                    act(SM.a[:, 2:3], SM.a[:, 1:2], AF.Exp, [SM], [SM], scale=-0.5)
                    stt("dve", hb2.a, x1.a[:, u, :], SM.a[:, 2:3], g2bc.a, ALU.mult, ALU.mult, [x1u[sj % 2][u], SM, g2bc], [hb2])
                    pTt = banks[2]
                    pTtb = pTt.a.bitcast(BF16)
                    for c in range(8):
                        tr(pTtb[:, c * 128:(c + 1) * 128], hb2.a[:, c * 128:(c + 1) * 128], idb.a, [hb2, idb], [pTt])
                    cp("act", h2T.a[:, :, u * 128:(u + 1) * 128], pTtb.rearrange("p (c t) -> p c t", c=8), [pTt], [h2T])

            def DM_(sj):
                s, t0, g0 = supers[sj]
                x1 = x1_r[sj % 2]
                h2T = h2T_r[sj % 2]
                for fg in range(16):
                    W1 = w1s[dcn["w1"] % 2]
                    dcn["w1"] += 1
                    dma("sp", W1.a, w1b_d[l, :, fg * 256:(fg + 1) * 256].rearrange("(c p) f -> p c f", p=128), [], [W1])
                    for f2 in range(2):
                        f = fg * 2 + f2
                        pb = banks[3 + (f % 3)]
                        for c in range(8):
                            mm(pb.a[:, 0:512], W1.a[:, c, f2 * 128:(f2 + 1) * 128], h2T.a[:, c, :], c == 0, c == 7, [W1, h2T], [pb])
                        R_ = rtmp[f % 2]
                        act(R_.a, pb.a[:, 0:512], AF.Relu, [pb], [R_])
                        tt("dve", actT.a[:, f, :], R_.a, R_.a, ALU.mult, [R_], [actT])
                for u in range(4):
                    XO = T(x1.a[:, u, :], x1.b)
                    SM = smE[u % 2]
                    for hf in range(2):
                        pb = banks[6 + (dcn["k6"] % 2)]
                        dcn["k6"] += 1
                        for f in range(32):
                            mm(pb.a[:, 0:512], actT.a[:, f, u * 128:(u + 1) * 128], w2.a[:, f, hf * 512:(hf + 1) * 512],
                               f == 0, f == 31, [actT, w2], [pb])
                        tt("dve", XO.a[:, hf * 512:(hf + 1) * 512], x1.a[:, u, hf * 512:(hf + 1) * 512], pb.a[:, 0:512],
                           ALU.add, [x1, pb], [XO, x1u[sj % 2][u]])
                    if not last:
                        dma("qpool", xres_d[g0 + u * 128:g0 + (u + 1) * 128, :], XO.a, [XO], [dscr])
                    else:
                        act(rtmp[0].a.bitcast(BF16), XO.a, AF.Square, [XO], [rtmp[0], SM], accum=SM.a[:, 4:5])
                        act(SM.a[:, 5:6], SM.a[:, 4:5], AF.Ln, [SM], [SM], bias=epsb.a[:, 0:1], scale=1.0 / DM)
                        act(SM.a[:, 6:7], SM.a[:, 5:6], AF.Exp, [SM], [SM], scale=-0.5)
                        stt("dve", XO.a, XO.a, SM.a[:, 6:7], g1bc.a, ALU.mult, ALU.mult, [XO, SM, g1bc], [XO])
                        dma("qpool", y_out[g0 + u * 128:g0 + (u + 1) * 128, :], XO.a, [XO], [dscr])

            DPa(0)
            DPb(0)
            for sj in range(len(supers)):
                lanes = [fw.capture(DM_, sj)]
                wins = [(0.0, 1.0)]
                if sj + 1 < len(supers):
                    lanes.append(fw.capture(DPa, sj + 1))
                    wins.append((0.0, 0.7))
                    lanes.append(fw.capture(DPb, sj + 1))
                    wins.append((0.25, 0.98))
                fw.merge_emit(lanes, wins)
            fw.barrier()
            bump[0] = persist_mark

        fw.finish()
        build.nops = fw.nops
    return nc


SEQS = [2048, 2048, 8192]
DEPTH = 2
_cache = {}


def kernel(**inputs):
    ncores = 8
    xp = np.asarray(inputs['x_prompt'], np.float32)
    xs = np.asarray(inputs['x_sample'], np.float32)
    key = "main"
    if key not in _cache:
        _cache[key] = build(SEQS, DEPTH)
    nc = _cache[key]
    tab, cmat = host_consts(max(SEQS))
    base = {}
    for k in PNAMES:
        a = np.asarray(inputs[k], np.float32)
        if k in ('ssd_dt_bias', 'ssd_a_log'):
            a = a.reshape(DEPTH, 8)
        if k == 'gla_gate_b':
            a = a.reshape(DEPTH, 256)
        base[k] = np.ascontiguousarray(a)
    base['tab'] = tab
    base['cm'] = cmat
    in_maps = []
    for c in range(ncores):
        xc = np.concatenate([xp[2 * c], xp[2 * c + 1], xs[c]], axis=0)
        m = dict(base)
        m['x'] = np.ascontiguousarray(xc)
        in_maps.append(m)
    res = run_bass_kernel_spmd(nc, in_maps, core_ids=list(range(ncores)))
    yp = np.empty_like(xp)
    ys = np.empty_like(xs)
    for c in range(ncores):
        y = res.results[c]['y']
        yp[2 * c] = y[0:2048]
        yp[2 * c + 1] = y[2048:4096]
        ys[c] = y[4096:12288]
    return (yp, ys)
```

```python
import contextlib
import math
import numpy as np
import concourse.bass as bass
import concourse.mybir as mybir
from concourse.bass_utils import run_bass_kernel_spmd

F32 = mybir.dt.float32
BF16 = mybir.dt.bfloat16
U8 = mybir.dt.uint8
ALU = mybir.AluOpType
AF = mybir.ActivationFunctionType
AX = mybir.AxisListType

DM = 1024
DIN = 2760
DFF = 4096
EPS = 1e-6
DMA_R = 8
NTM = 1992

PNAMES = ['norm1_g', 'w_in', 'ssd_conv_w', 'ssd_conv_b', 'ssd_dt_bias', 'ssd_a_log', 'ssd_d', 'ssd_norm_g',
          'gqa_q_norm_g', 'gqa_k_norm_g', 'gla_gate_w2', 'gla_gate_b', 'gla_norm_g', 'mla_q_norm_g', 'mla_w_uq',
          'mla_kv_norm_g', 'mla_w_ukv', 'w_out', 'norm2_g', 'w_ff1', 'w_ff2', 'final_norm_g']


class Buf:
    __slots__ = ("name", "w", "r")

    def __init__(self, name=None):
        self.name = name
        self.w = None
        self.r = {}


class T:
    __slots__ = ("a", "b")

    def __init__(self, a, b=None):
        self.a = a
        self.b = b if b is not None else Buf()


class FW:
    def __init__(self, nc, stack):
        self.nc = nc
        self.stack = stack
        self.sems = {}
        self.cnt = {}
        self.dsems = {}
        for n in ("pe", "act", "dve", "pool"):
            self.sems["c_" + n] = stack.enter_context(nc.semaphore("c_" + n))
            self.cnt[n] = 0
        for q in ("sp", "qpool", "qact"):
            self.dsems[q] = []
            self.cnt[q] = 0
            for i in range(DMA_R):
                k = "d_%s%d" % (q, i)
                self.sems[k] = stack.enter_context(nc.semaphore(k))
                self.dsems[q].append(k)
        self.streams = {"pe": [], "act": [], "dve": [], "pool": [], "sp": []}
        self.stream_of = {"pe": "pe", "act": "act", "dve": "dve", "pool": "pool",
                          "sp": "sp", "qpool": "pool", "qact": "act"}
        self.known = {s: {} for s in self.streams}
        self.latest = {}
        self.nops = 0
        self.cap = None

    def capture(self, fn, *args):
        assert self.cap is None
        self.cap = []
        fn(*args)
        ops = self.cap
        self.cap = None
        return ops

    def merge_emit(self, lanes, windows=None):
        if windows is None:
            windows = [(0.0, 1.0)] * len(lanes)
        windows = [w for l, w in zip(lanes, windows) if l]
        lanes = [l for l in lanes if l]
        idx = [0] * len(lanes)
        while True:
            best, bv = -1, 9.0
            for i, l in enumerate(lanes):
                if idx[i] < len(l):
                    v = windows[i][0] + (windows[i][1] - windows[i][0]) * (idx[i] + 0.5) / len(l)
                    if v < bv:
                        best, bv = i, v
            if best < 0:
                break
            self.op(*lanes[best][idx[best]])
            idx[best] += 1

    def op(self, eng, fn, reads=(), writes=()):
        if self.cap is not None:
            self.cap.append((eng, fn, tuple(reads), tuple(writes)))
            return None
        st = self.stream_of[eng]
        deps = {}
        for t in reads:
            b = t.b if isinstance(t, T) else t
            if b.w is not None and deps.get(b.w[0], 0) < b.w[1]:
                deps[b.w[0]] = b.w[1]
        for t in writes:
            b = t.b if isinstance(t, T) else t
            if b.w is not None and deps.get(b.w[0], 0) < b.w[1]:
                deps[b.w[0]] = b.w[1]
            for k, v in b.r.items():
                if deps.get(k, 0) < v:
                    deps[k] = v
        if eng in self.dsems:
            n = self.cnt[eng]
            slot = self.dsems[eng][n % DMA_R]
            rnd = n // DMA_R
            if rnd > 0 and deps.get(slot, 0) < 16 * rnd:
                deps[slot] = 16 * rnd
            tok = (slot, 16 * (rnd + 1))
            self.cnt[eng] = n + 1
            inc = (slot, 16)
        else:
            self.cnt[eng] += 1
            tok = ("c_" + eng, self.cnt[eng])
            inc = ("c_" + eng, 1)
        self.latest[tok[0]] = tok[1]
        known = self.known[st]
        waits = []
        for k, v in deps.items():
            if eng == "pe" and k == "c_pe":
                continue
            if known.get(k, 0) >= v:
                continue
            known[k] = v
            waits.append((k, v))
        self.streams[st].append((waits, fn, inc))
        self.nops += 1
        wset = set()
        for t in writes:
            b = t.b if isinstance(t, T) else t
            b.w = tok
            b.r = {}
            wset.add(id(b))
        for t in reads:
            b = t.b if isinstance(t, T) else t
            if id(b) in wset:
                continue
            if b.r.get(tok[0], 0) < tok[1]:
                b.r[tok[0]] = tok[1]
        return tok

    def barrier(self):
        for st in self.streams:
            known = self.known[st]
            waits = []
            for k, v in self.latest.items():
                if known.get(k, 0) < v:
                    known[k] = v
                    waits.append((k, v))
            if waits:
                self.streams[st].append((waits, None, None))

    def finish(self):
        nc = self.nc
        self.barrier()
        sems = self.sems
        streams = self.streams

        def replay(engobj, ops):
            for waits, fn, inc in ops:
                for k, v in waits:
                    engobj.wait_ge(sems[k], v)
                if fn is not None:
                    ins = fn(engobj)
                    ins.then_inc(sems[inc[0]], inc[1])

        with nc.Block() as block:
            @block.tensor
            def _(eng):
                replay(eng, streams["pe"])

            @block.scalar
            def _(eng):
                replay(eng, streams["act"])

            @block.vector
            def _(eng):
                replay(eng, streams["dve"])

            @block.gpsimd
            def _(eng):
                replay(eng, streams["pool"])

            @block.sync
            def _(eng):
                replay(eng, streams["sp"])


def host_consts(tmax):
    t = np.arange(tmax)
    row = (t // 64).astype(np.float32)
    col = (t % 64).astype(np.float32)

    def tabs(n):
        inv = (10000.0 ** (-np.arange(n, dtype=np.float32) / n)).astype(np.float32)
        out = []
        for pos in (row, col):
            ang = pos[:, None].astype(np.float32) * inv[None, :]
            c = np.cos(ang).astype(np.float32)
            s = np.sin(ang).astype(np.float32)
            out.append((np.concatenate([c, c], 1), np.concatenate([-s, s], 1)))
        C = np.concatenate([out[0][0], out[1][0]], 1)
        S = np.concatenate([out[0][1], out[1][1]], 1)
        return C, S

    C64, S64 = tabs(16)
    C32, S32 = tabs(8)
    tab = np.concatenate([C64, S64, C32, S32], 1).astype(np.float32)
    j = np.arange(128)
    uf = (j[:, None] <= j[None, :]).astype(np.float32)
    ub = (j[:, None] >= j[None, :]).astype(np.float32)
    negf = np.where(uf > 0, 0.0, -30000.0).astype(np.float32)
    negb = np.where(ub > 0, 0.0, -30000.0).astype(np.float32)
    bm4 = np.zeros((128, 4), np.float32)
    for h in range(4):
        bm4[h * 32:(h + 1) * 32, h] = 1.0
    bmbig = np.repeat(bm4, 64, axis=1)
    eye = np.eye(128, dtype=np.float32)
    cm = np.concatenate([
        uf, ub,
        ub - eye, uf - eye,
        uf * (-1.0 / 16), ub * (-1.0 / 16),
        eye,
        bm4,
        bmbig,
        np.full((128, 4), -1.0 / 16, np.float32),
    ], 1).astype(np.float32)
    return tab, cm


CM_COLS = 1160


def build(seqs, depth, debug=()):
    nc = bass.Bass("TRN2", target_bir_lowering=False)
    nseq = len(seqs)
    offs = [0]
    for s in seqs:
        offs.append(offs[-1] + s)
    TT = offs[-1]
    nchunks = TT // 128
    L = depth

    def dram(name, shape, dt, kind=None):
        if kind is None:
            kind = "ExternalOutput" if name in debug else "Internal"
        return nc.dram_tensor(name, list(shape), dt, kind=kind).ap()

    x_in = dram("x", [TT, DM], F32, "ExternalInput")
    Wd = {}
    shp = {'norm1_g': [L, DM], 'w_in': [L, DM, DIN], 'ssd_conv_w': [L, 5, 768], 'ssd_conv_b': [L, 768],
           'ssd_dt_bias': [L, 8], 'ssd_a_log': [L, 8], 'ssd_d': [L, 4], 'ssd_norm_g': [L, 256],
           'gqa_q_norm_g': [L, 64], 'gqa_k_norm_g': [L, 64], 'gla_gate_w2': [L, 2, 16, 128],
           'gla_gate_b': [L, 256], 'gla_norm_g': [L, 64], 'mla_q_norm_g': [L, 256], 'mla_w_uq': [L, 256, 384],
           'mla_kv_norm_g': [L, 128], 'mla_w_ukv': [L, 128, 512], 'w_out': [L, DM, DM], 'norm2_g': [L, DM],
           'w_ff1': [L, DM, DFF], 'w_ff2': [L, DFF, DM], 'final_norm_g': [DM]}
    for k in PNAMES:
        Wd[k] = dram(k, shp[k], F32, "ExternalInput")
    tab_d = dram("tab", [max(seqs), 192], F32, "ExternalInput")
    cm_d = dram("cm", [128, CM_COLS], F32, "ExternalInput")
    y_out = dram("y", [TT, DM], F32, "ExternalOutput")

    TP = TT + 4 * nseq
    xbcpre = dram("xbcpre", [6, 128, TP], F32)
    QTg = dram("QTg", [4, 64, TT], BF16)
    KTg = dram("KTg", [128, TT], BF16)
    VAg = dram("VAg", [128, nchunks, 130], BF16)
    QTm = dram("QTm", [4, 96, TT], BF16)
    KTm = dram("KTm", [4, 96, TT], BF16)
    VAm = dram("VAm", [2, 128, nchunks, 130], BF16)
    CTd = dram("CTd", [2, 128, TT], BF16)
    qgTd = dram("qgTd", [2, 128, TT], BF16)
    eacs_d = dram("eacs", [128, nchunks, 8], F32)
    yo_d = dram("yo", [TT, 512], F32)
    zr_d = dram("zr", [TT, 512], F32)
    dtr_d = dram("dtr", [128, nchunks, 8], F32)
    stssd_d = dram("stssd", [nchunks, 128, 512], F32)
    stgla_d = dram("stgla", [nchunks, 128, 512], F32)
    mixT_d = dram("mixT", [8, 128, TT], BF16)
    xres_d = dram("xres", [TT, DM], F32)
    w1b_d = dram("w1b", [L, DM, DFF], BF16)

    with contextlib.ExitStack() as stack:
        fw = FW(nc, stack)
        SB_BYTES = 212800
        big = stack.enter_context(nc.sbuf_tensor("big", [128, SB_BYTES], U8))
        banks = [T(stack.enter_context(nc.psum_tensor("bank%d" % i, [128, 512], F32))[:]) for i in range(8)]
        bump = [0]

        def alloc(shape, dt, nb=1):
            esz = 4 if dt == F32 else 2
            n = int(np.prod(shape[1:])) * esz
            n = (n + 31) // 32 * 32
            res = []
            for _ in range(nb):
                off = bump[0]
                bump[0] += n
                assert bump[0] <= SB_BYTES, "SBUF overflow %d" % bump[0]
                ap = big[:, off:off + int(np.prod(shape[1:])) * esz].bitcast(dt)
                if len(shape) == 3:
                    ap = ap.rearrange("p (a b) -> p a b", a=shape[1])
                elif len(shape) == 4:
                    ap = ap.rearrange("p (a b c) -> p a b c", a=shape[1], b=shape[2])
                res.append(T(ap))
            return res[0] if nb == 1 else res

        def tt(eng, out, in0, in1, op, r, w):
            fw.op(eng, lambda e: e.tensor_tensor(out=out, in0=in0, in1=in1, op=op), r, w)

        def stt(eng, out, in0, scalar, in1, op0, op1, r, w):
            fw.op(eng, lambda e: e.scalar_tensor_tensor(out=out, in0=in0, scalar=scalar, in1=in1, op0=op0, op1=op1), r, w)

        def ts(eng, out, in0, s1, s2, op0, op1, r, w):
            if s2 is None:
                fw.op(eng, lambda e: e.tensor_scalar(out=out, in0=in0, scalar1=s1, scalar2=None, op0=op0), r, w)
            else:
                fw.op(eng, lambda e: e.tensor_scalar(out=out, in0=in0, scalar1=s1, scalar2=s2, op0=op0, op1=op1), r, w)

        def act(out, in_, func, r, w, bias=None, scale=None, accum=None):
            kw = {}
            if bias is not None:
                kw["bias"] = bias
            if scale is not None:
                kw["scale"] = scale
            if accum is not None:
                kw["accum_out"] = accum
            fw.op("act", lambda e: e.activation(out=out, in_=in_, func=func, **kw), r, w)

        def cp(eng, out, in_, r, w):
            if eng == "act":
                fw.op("act", lambda e: e.activation(out=out, in_=in_, func=AF.Copy), r, w)
            else:
                fw.op(eng, lambda e: e.tensor_copy(out=out, in_=in_), r, w)

        def red(out, in_, r, w):
            fw.op("dve", lambda e: e.tensor_reduce(out=out, in_=in_, axis=AX.X, op=ALU.add), r, w)

        def mm(out, lhsT, rhs, start, stop, r, w):
            fw.op("pe", lambda e: e.matmul(out, lhsT=lhsT, rhs=rhs, start=start, stop=stop), r, w)

        def tr(out, in_, ident, r, w):
            fw.op("pe", lambda e: e.transpose(out=out, in_=in_, identity=ident), r, w)

        def dma(q, out, in_, r, w, slow=False):
            w = [x for x in w if x is not dscr]
            if slow:
                fw.op(q, lambda e: e.dma_start(out=out, in_=in_, allow_slow_non_contiguous=True), r, w)
            else:
                fw.op(q, lambda e: e.dma_start(out=out, in_=in_), r, w)

        dscr = T(None)

        def rstd_from_ssq(ssq_ap, n, out_ap, r, w, tmp):
            act(tmp.a, ssq_ap, AF.Ln, r, [tmp], bias=epsb.a[:, 0:1], scale=1.0 / n)
            act(out_ap, tmp.a, AF.Exp, [tmp], w, scale=-0.5)

        cm = alloc([128, CM_COLS], F32)
        UF = cm.a[:, 0:128]
        UB = cm.a[:, 128:256]
        SU = [cm.a[:, 256:384], cm.a[:, 384:512]]
        UFS = cm.a[:, 512:640]
        UBS = cm.a[:, 640:768]
        IDF = cm.a[:, 768:896]
        BM4 = cm.a[:, 896:900]
        BMBIG = cm.a[:, 900:1156]
        CNEG = cm.a[:, 1156:1160]
        idb = alloc([128, 128], BF16)
        onesf = alloc([128, 128], F32)
        epsb = alloc([128, 2], F32)
        egl = alloc([128, nchunks, 2], F32)
        cdec = alloc([128, nchunks, 8], F32)
        g1bc = alloc([128, DM], F32)
        g2bc = alloc([128, DM], F32)
        gqk = alloc([128, 6, 64], F32)
        gcq = alloc([128, 256], F32)
        gckv = alloc([128, 128], F32)
        gateb = alloc([128, 256], F32)
        gssd = alloc([128, 256], F32)
        ggla = alloc([128, 64], F32)
        dtb = alloc([128, 8], F32)
        abc = alloc([128, 8], F32)
        dbc = alloc([128, 4], F32)
        cw = alloc([128, 6, 5], F32)
        cb = alloc([128, 6], F32)
        w2blk = alloc([128, 256], F32)
        wuq = alloc([128, 2, 384], BF16)
        wukv = alloc([128, 512], BF16)
        persist_mark = bump[0]

        dma("sp", cm.a, cm_d, [], [cm])
        cp("dve", idb.a, IDF, [cm], [idb])
        fw.op("dve", lambda e: e.memset(onesf.a, 1.0), [], [onesf])
        fw.op("dve", lambda e: e.memset(epsb.a, EPS), [], [epsb])

        ztile = alloc([128, 6, 4], F32)
        fw.op("dve", lambda e: e.memset(ztile.a, 0.0), [], [ztile])
        for s in range(nseq):
            c0 = offs[s] + 4 * s
            dma("qpool", xbcpre[:, :, c0:c0 + 2].rearrange("c p t -> p c t"), ztile.a[:, :, 0:2], [ztile], [dscr])
            c1 = c0 + 2 + seqs[s]
            dma("qpool", xbcpre[:, :, c1:c1 + 2].rearrange("c p t -> p c t"), ztile.a[:, :, 2:4], [ztile], [dscr])

        wst = alloc([128, 4096], F32, nb=2)
        wsb = alloc([128, 4096], BF16, nb=2)
        i = 0
        for l in range(L):
            for c in range(8):
                a, b2 = wst[i % 2], wsb[i % 2]
                dma("sp", a.a, Wd['w_ff1'][l, c * 128:(c + 1) * 128, :], [], [a])
                if i % 2 == 0:
                    cp("dve", b2.a, a.a, [a], [b2])
                else:
                    cp("act", b2.a, a.a, [a], [b2])
                dma("qpool", w1b_d[l, c * 128:(c + 1) * 128, :], b2.a, [b2], [dscr])
                i += 1
        fw.barrier()
        bump[0] = persist_mark

        supers = []
        for s in range(nseq):
            for t0 in range(0, seqs[s], 512):
                supers.append((s, t0, offs[s] + t0))

        for l in range(L):
            xsrc = x_in if l == 0 else xres_d
            last = (l == L - 1)
            tmpw = alloc([128, 1024], F32)
            dma("sp", g1bc.a, Wd['norm1_g'][l].partition_broadcast(128), [], [g1bc])
            dma("sp", g2bc.a, Wd['norm2_g'][l].partition_broadcast(128), [], [g2bc])
            for h in range(4):
                dma("sp", gqk.a[:, h, :], Wd['gqa_q_norm_g'][l].partition_broadcast(128), [], [gqk])
            for h in range(2):
                dma("sp", gqk.a[:, 4 + h, :], Wd['gqa_k_norm_g'][l].partition_broadcast(128), [], [gqk])
            dma("sp", gcq.a, Wd['mla_q_norm_g'][l].partition_broadcast(128), [], [gcq])
            dma("sp", gckv.a, Wd['mla_kv_norm_g'][l].partition_broadcast(128), [], [gckv])
            dma("sp", gateb.a, Wd['gla_gate_b'][l].partition_broadcast(128), [], [gateb])
            dma("sp", gssd.a, Wd['ssd_norm_g'][l].partition_broadcast(128), [], [gssd])
            dma("sp", ggla.a, Wd['gla_norm_g'][l].partition_broadcast(128), [], [ggla])
            dma("sp", dtb.a, Wd['ssd_dt_bias'][l].partition_broadcast(128), [], [dtb])
            dma("sp", abc.a, Wd['ssd_a_log'][l].partition_broadcast(128), [], [abc])
            dma("sp", dbc.a, Wd['ssd_d'][l].partition_broadcast(128), [], [dbc])
            act(abc.a, abc.a, AF.Exp, [abc], [abc])
            ts("dve", abc.a, abc.a, -1.0, None, ALU.mult, None, [abc], [abc])
            for k in range(5):
                dma("sp", cw.a[:, :, k], Wd['ssd_conv_w'][l, k].rearrange("(c p) -> p c", p=128), [], [cw], slow=True)
            dma("sp", cb.a, Wd['ssd_conv_b'][l].rearrange("(c p) -> p c", p=128), [], [cb], slow=True)
            fw.op("dve", lambda e: e.memset(w2blk.a, 0.0), [], [w2blk])
            dma("sp", w2blk.a[0:16, 0:128], Wd['gla_gate_w2'][l, 0], [], [w2blk])
            dma("sp", w2blk.a[16:32, 128:256], Wd['gla_gate_w2'][l, 1], [], [w2blk])
            for c in range(2):
                dma("sp", tmpw.a[:, 0:384], Wd['mla_w_uq'][l, c * 128:(c + 1) * 128, :], [], [tmpw])
                cp("dve", wuq.a[:, c, :], tmpw.a[:, 0:384], [tmpw], [wuq])
            dma("sp", tmpw.a[:, 0:512], Wd['mla_w_ukv'][l], [], [tmpw])
            cp("dve", wukv.a, tmpw.a[:, 0:512], [tmpw], [wukv])
            fw.barrier()
            bump[0] = persist_mark

            win = alloc([128, 8, DIN], BF16)
            a1_mark = bump[0]
            wstage = alloc([128, DIN], F32, nb=2)
            for c in range(8):
                wsx = wstage[c % 2]
                dma("sp", wsx.a, Wd['w_in'][l, c * 128:(c + 1) * 128, :], [], [wsx])
                eng = "dve" if c % 2 == 0 else "act"
                cp(eng, win.a[:, c, 0:1728], wsx.a[:, 1032:2760], [wsx], [win])
                cp(eng, win.a[:, c, 1728:1984], wsx.a[:, 0:256], [wsx], [win])
                cp(eng, win.a[:, c, 1984:1992], wsx.a[:, 1024:1032], [wsx], [win])
                cp(eng, win.a[:, c, 1992:2760], wsx.a[:, 256:1024], [wsx], [win])
            fw.barrier()
            bump[0] = a1_mark
            xt = alloc([128, DM], F32, nb=2)
            junk = alloc([128, DM], BF16)
            hb = alloc([128, DM], BF16, nb=2)
            hT = alloc([128, 8, 512], BF16, nb=2)
            ptok = alloc([128, NTM], F32, nb=3)
            tabt = alloc([128, 192], F32, nb=3)
            sm = alloc([128, 16], F32, nb=3)
            sm2 = alloc([128, 16], F32, nb=3)
            tA = alloc([128, 6, 64], F32)
            tB = alloc([128, 6, 64], F32)
            tC = alloc([128, 6, 64], F32)
            tD = alloc([128, 6, 64], F32)
            qrb_r = alloc([128, 256], BF16, nb=2)
            krb_r = alloc([128, 128], BF16, nb=2)
            QA_r = alloc([128, 512], BF16, nb=2)
            QB_r = alloc([128, 512], BF16, nb=2)
            KTs_r = alloc([128, 512], BF16, nb=2)
            vaug_r = alloc([128, 4, 130], BF16, nb=2)
            vaugm_r = alloc([128, 2, 4, 130], BF16, nb=2)
            cqn_r = alloc([128, 384], BF16, nb=2)
            cT_r = alloc([128, 3, 128], BF16, nb=2)
            mt1 = alloc([128, 5, 32], F32)
            mt2 = alloc([128, 5, 32], F32)
            qmb_r = alloc([128, 4, 96], BF16, nb=2)
            kmb_r = alloc([128, 4, 96], BF16, nb=2)
            QM_r = alloc([128, 4, 512], BF16, nb=1); QM_r = [QM_r, QM_r]
            KM_r = alloc([128, 4, 512], BF16, nb=1); KM_r = [KM_r, KM_r]
            lrT_r = alloc([128, 128], F32, nb=2)
            lgb = alloc([128, 256], F32)
            lsp_r = alloc([128, 256], F32, nb=2)
            Eq_r = alloc([128, 256], F32, nb=2)
            Ek_r = alloc([128, 256], F32, nb=2)
            qg_r = alloc([128, 2, 128], BF16, nb=2)
            kg_r = alloc([128, 2, 128], BF16, nb=2)
            lvb_r = alloc([128, 256], BF16, nb=2)
            qkT_r = alloc([128, 4, 128], BF16, nb=2)
            qgTs_r = alloc([128, 2, 512], BF16, nb=2)
            Qblk_r = alloc([128, 2, 512], BF16, nb=2)
            attm_r = alloc([128, 2, 512], BF16, nb=2)
            yos_r = alloc([128, 4, 256], F32, nb=2)
            stg = alloc([128, 512], F32, nb=2)
            xbst = alloc([128, 3, 512], F32)
            for v_ in vaug_r + vaugm_r:
                fw.op("dve", lambda e, v_=v_: e.memset(v_.a, 1.0), [], [v_])
            pT = banks[0]
            pTb = pT.a.bitcast(BF16)
            pgr = banks[1:3]
            pgi = [0]
            pgc = [0]
            mi = [0]
            lane_banks = {"s2a": banks[3:4], "s2b": banks[4:6], "s3": banks[6:8]}
            lane_cnt = {"s2a": 0, "s2b": 0, "s3": 0}
            cur_lane = ["s2a"]
            smb = alloc([128, 8], F32, nb=3)

            def mbank():
                ln = cur_lane[0]
                bl = lane_banks[ln]
                b = bl[lane_cnt[ln] % len(bl)]
                lane_cnt[ln] += 1
                return b

            def pgbank():
                b = pgr[pgi[0] % 2]
                pgi[0] += 1
                return b

            GRP = [(0, 512), (512, 512), (1024, 512), (1536, 456)]
            ctxs = []
            for sj, (s, t0, g0) in enumerate(supers):
                for u in range(4):
                    ctxs.append(dict(s=s, t0=t0, g0=g0, u=u, sj=sj, k=len(ctxs)))

            def S1(cx):
                s, t0, g0, u, sj, k = cx['s'], cx['t0'], cx['g0'], cx['u'], cx['sj'], cx['k']
                hTs = hT[sj % 2]
                gt = g0 + u * 128
                tl = t0 + u * 128
                X = xt[k % 2]
                P = ptok[k % 3]
                TB = tabt[k % 3]
                S1_ = sm[k % 3]
                S2_ = sm2[k % 3]
                HB = hb[k % 2]
                dma("sp", X.a, xsrc[gt:gt + 128, :], [], [X])
                dma("sp", TB.a, tab_d[tl:tl + 128, :], [], [TB])
                act(junk.a, X.a, AF.Square, [X], [junk, S1_], accum=S1_.a[:, 0:1])
                rstd_from_ssq(S1_.a[:, 0:1], DM, S1_.a[:, 1:2], [S1_], [S1_], T(S2_.a[:, 0:1], S2_.b))
                stt("dve", HB.a, X.a, S1_.a[:, 1:2], g1bc.a, ALU.mult, ALU.mult, [X, S1_, g1bc], [HB])
                for c in range(8):
                    tr(pTb[:, c * 128:(c + 1) * 128], HB.a[:, c * 128:(c + 1) * 128], idb.a, [HB, idb], [pT])
                cp("act", hTs.a[:, :, u * 128:(u + 1) * 128], pTb.rearrange("p (c t) -> p c t", c=8), [pT], [hTs])
                pend_ev = []

                def evac():
                    gi, (c0, n), pb = pend_ev.pop(0)
                    cp("act" if gi % 2 == 0 else "dve", P.a[:, c0:c0 + n], pb.a[:, 0:n], [pb], [P])

                for gi, (c0, n) in enumerate(GRP):
                    pb = pgbank()
                    for c in range(8):
                        mm(pb.a[:, 0:n], hTs.a[:, c, u * 128:(u + 1) * 128], win.a[:, c, c0:c0 + n],
                           c == 0, c == 7, [hTs, win], [pb])
                    pend_ev.append((gi, (c0, n), pb))
                    if len(pend_ev) > 1:
                        evac()
                while pend_ev:
                    evac()
                dma("qpool", zr_d[gt:gt + 128, 0:256], P.a[:, 1728:1984], [P], [dscr])
                dma("qpool", zr_d[gt:gt + 128, 256:512], P.a[:, 1024:1280], [P], [dscr])
                dma("qpool", dtr_d[:, gt // 128, :], P.a[:, 1984:1992], [P], [dscr])

            def S1c(cx):
                s, t0, g0, u, sj, k = cx['s'], cx['t0'], cx['g0'], cx['u'], cx['sj'], cx['k']
                hTs = hT[sj % 2]
                if u == 3:
                    for fc in range(6):
                        mb = pgbank()
                        for c in range(8):
                            mm(mb.a[:, 0:512], win.a[:, c, 1992 + fc * 128:1992 + (fc + 1) * 128], hTs.a[:, c, :],
                               c == 0, c == 7, [win, hTs], [mb])
                        cp("act" if fc % 2 == 0 else "dve", xbst.a[:, fc % 3, :], mb.a[:, 0:512], [mb], [xbst])
                        if fc % 3 == 2:
                            col0 = g0 + 4 * s + 2
                            dma("qpool", xbcpre[fc - 2:fc + 1, :, col0:col0 + 512].rearrange("c p t -> p c t"), xbst.a, [xbst], [dscr])

            def S2(cx):
                s, t0, g0, u, sj, k = cx['s'], cx['t0'], cx['g0'], cx['u'], cx['sj'], cx['k']
                P = ptok[k % 3]
                TB = tabt[k % 3]
                S1_ = sm[k % 3]
                S2_ = sm2[k % 3]
                qrb, krb, cqn, cT = qrb_r[k % 2], krb_r[k % 2], cqn_r[k % 2], cT_r[k % 2]
                qmb, kmb = qmb_r[k % 2], kmb_r[k % 2]
                QA, QB, KTs, vaug, vaugm, QM, KM = (QA_r[sj % 2], QB_r[sj % 2], KTs_r[sj % 2], vaug_r[sj % 2],
                                                    vaugm_r[sj % 2], QM_r[sj % 2], KM_r[sj % 2])
                qk = P.a[:, 0:384].rearrange("p (h d) -> p h d", h=6)
                act(tA.a, qk, AF.Square, [P], [tA])
                red(S1_.a[:, 2:8], tA.a, [tA], [S1_])
                rstd_from_ssq(S1_.a[:, 2:8], 64, S1_.a[:, 8:14], [S1_], [S1_], T(S2_.a[:, 2:8], S2_.b))
                tt("dve", tB.a, qk, S1_.a[:, 8:14].unsqueeze(2).to_broadcast([128, 6, 64]), ALU.mult, [P, S1_], [tB])
                tt("dve", tB.a, tB.a, gqk.a, ALU.mult, [tB, gqk], [tB])
                C64 = TB.a[:, 0:64].unsqueeze(1).to_broadcast([128, 6, 64])
                tt("dve", tC.a, tB.a, C64, ALU.mult, [tB, TB], [tC])
                tBv = tB.a.rearrange("p h (b f d) -> p h b f d", b=2, f=2)
                tDv = tD.a.rearrange("p h (b f d) -> p h b f d", b=2, f=2)
                S64v = TB.a[:, 64:128].rearrange("p (b f d) -> p b f d", b=2, f=2)
                for f in range(2):
                    tt("dve", tDv[:, :, :, f, :], tBv[:, :, :, 1 - f, :],
                       S64v[:, :, f, :].unsqueeze(1).to_broadcast([128, 6, 2, 16]), ALU.mult, [tB, TB], [tD])
                tt("dve", qrb.a.rearrange("p (ha hb d) -> p hb ha d", ha=2, hb=2),
                   tC.a[:, 0:4, :].rearrange("p (hb ha) d -> p hb ha d", ha=2),
                   tD.a[:, 0:4, :].rearrange("p (hb ha) d -> p hb ha d", ha=2), ALU.add, [tC, tD], [qrb])
                tt("dve", krb.a.rearrange("p (h d) -> p h d", h=2), tC.a[:, 4:6, :], tD.a[:, 4:6, :], ALU.add,
                   [tC, tD], [krb])
                cp("act", vaug.a[:, u, :].rearrange("p (k e) -> p k e", k=2)[:, :, 0:64],
                   P.a[:, 384:512].rearrange("p (k e) -> p k e", k=2), [P], [vaug])
                mb = mbank()
                mbb = mb.a.bitcast(BF16)
                tr(mbb[:, 0:128], qrb.a[:, 0:128], idb.a, [qrb, idb], [mb])
                tr(mbb[:, 128:256], qrb.a[:, 128:256], idb.a, [qrb, idb], [mb])
                tr(mbb[:, 256:384], krb.a, idb.a, [krb, idb], [mb])
                cp("act", QA.a[:, u * 128:(u + 1) * 128], mbb[:, 0:128], [mb], [QA])
                cp("act", QB.a[:, u * 128:(u + 1) * 128], mbb[:, 128:256], [mb], [QB])
                cp("act", KTs.a[:, u * 128:(u + 1) * 128], mbb[:, 256:384], [mb], [KTs])
                if u == 3:
                    dma("qpool", QTg[0, :, g0:g0 + 512], QA.a[0:64, :], [QA], [dscr])
                    dma("qpool", QTg[2, :, g0:g0 + 512], QA.a[64:128, :], [QA], [dscr])
                    dma("qpool", QTg[1, :, g0:g0 + 512], QB.a[0:64, :], [QB], [dscr])
                    dma("qpool", QTg[3, :, g0:g0 + 512], QB.a[64:128, :], [QB], [dscr])
                    dma("qpool", KTg[:, g0:g0 + 512], KTs.a, [KTs], [dscr])
                    dma("qpool", VAg[:, g0 // 128:g0 // 128 + 4, :], vaug.a, [vaug], [dscr])

            def S2b(cx):
                s, t0, g0, u, sj, k = cx['s'], cx['t0'], cx['g0'], cx['u'], cx['sj'], cx['k']
                P = ptok[k % 3]
                TB = tabt[k % 3]
                SB_ = smb[k % 3]
                cqn, cT = cqn_r[k % 2], cT_r[k % 2]
                qmb, kmb = qmb_r[k % 2], kmb_r[k % 2]
                vaugm, QM, KM = vaugm_r[sj % 2], QM_r[sj % 2], KM_r[sj % 2]
                act(junk.a[:, 0:256], P.a[:, 1312:1568], AF.Square, [P], [junk, SB_], accum=SB_.a[:, 0:1])
                act(junk.a[:, 256:384], P.a[:, 1568:1696], AF.Square, [P], [junk, SB_], accum=SB_.a[:, 1:2])
                rstd_from_ssq(SB_.a[:, 0:1], 256, SB_.a[:, 4:5], [SB_], [SB_], T(SB_.a[:, 2:3], SB_.b))
                rstd_from_ssq(SB_.a[:, 1:2], 128, SB_.a[:, 5:6], [SB_], [SB_], T(SB_.a[:, 3:4], SB_.b))
                stt("dve", cqn.a[:, 0:256], P.a[:, 1312:1568], SB_.a[:, 4:5], gcq.a, ALU.mult, ALU.mult, [P, SB_, gcq], [cqn])
                stt("dve", cqn.a[:, 256:384], P.a[:, 1568:1696], SB_.a[:, 5:6], gckv.a, ALU.mult, ALU.mult, [P, SB_, gckv], [cqn])
                mb = mbank()
                mbb = mb.a.bitcast(BF16)
                for c in range(3):
                    tr(mbb[:, c * 128:(c + 1) * 128], cqn.a[:, c * 128:(c + 1) * 128], idb.a, [cqn, idb], [mb])
                cp("act", cT.a.rearrange("p c t -> p (c t)"), mbb[:, 0:384], [mb], [cT])
                mq = mbank()
                for c in range(2):
                    mm(mq.a[:, 0:384], cT.a[:, c, :], wuq.a[:, c, :], c == 0, c == 1, [cT, wuq], [mq])
                mkv = mbank()
                mm(mkv.a[:, 0:512], cT.a[:, 2, :], wukv.a, True, True, [cT, wukv], [mkv])
                mqv = mq.a[:, 0:384].rearrange("p (h d) -> p h d", h=4)
                cp("dve", mt1.a[:, 0:4, :], mqv[:, :, 64:96], [mq], [mt1])
                cp("dve", mt1.a[:, 4, :], P.a[:, 1696:1728], [P], [mt1])
                C32 = TB.a[:, 128:160].unsqueeze(1).to_broadcast([128, 5, 32])
                m1v = mt1.a.rearrange("p h (b f d) -> p h b f d", b=2, f=2)
                m2v = mt2.a.rearrange("p h (b f d) -> p h b f d", b=2, f=2)
                S32v = TB.a[:, 160:192].rearrange("p (b f d) -> p b f d", b=2, f=2)
                for f in range(2):
                    tt("dve", m2v[:, :, :, f, :], m1v[:, :, :, 1 - f, :],
                       S32v[:, :, f, :].unsqueeze(1).to_broadcast([128, 5, 2, 8]), ALU.mult, [mt1, TB], [mt2])
                tt("dve", mt1.a, mt1.a, C32, ALU.mult, [mt1, TB], [mt1])
                tt("dve", qmb.a[:, :, 64:96], mt1.a[:, 0:4, :], mt2.a[:, 0:4, :], ALU.add, [mt1, mt2], [qmb])
                tt("dve", mt1.a[:, 4, :], mt1.a[:, 4, :], mt2.a[:, 4, :], ALU.add, [mt1, mt2], [mt1])
                cp("dve", kmb.a[:, :, 64:96], mt1.a[:, 4:5, :].to_broadcast([128, 4, 32]), [mt1], [kmb])
                cp("act", qmb.a[:, :, 0:64], mqv[:, :, 0:64], [mq], [qmb])
                mkvv = mkv.a[:, 0:512].rearrange("p (h d) -> p h d", h=4)
                cp("act", kmb.a[:, :, 0:64], mkvv[:, :, 0:64], [mkv], [kmb])
                cp("act", vaugm.a[:, :, u, :].rearrange("p a (hh e) -> p a hh e", hh=2)[:, :, :, 0:64],
                   mkvv[:, :, 64:128].rearrange("p (a hh) e -> p a hh e", hh=2), [mkv], [vaugm])
                mb = mbank()
                mbb = mb.a.bitcast(BF16)
                mb2 = mbank()
                mbb2 = mb2.a.bitcast(BF16)
                for h in range(4):
                    tr(mbb[0:96, h * 128:(h + 1) * 128], qmb.a[:, h, :], idb.a, [qmb, idb], [mb])
                    tr(mbb2[0:96, h * 128:(h + 1) * 128], kmb.a[:, h, :], idb.a, [kmb, idb], [mb2])
                cp("act", QM.a[0:96, :, u * 128:(u + 1) * 128], mbb[0:96, 0:512].rearrange("p (h t) -> p h t", h=4), [mb], [QM])
                cp("act", KM.a[0:96, :, u * 128:(u + 1) * 128], mbb2[0:96, 0:512].rearrange("p (h t) -> p h t", h=4), [mb2], [KM])
                if u == 3:
                    for a_ in range(2):
                        dma("qpool", VAm[a_, :, g0 // 128:g0 // 128 + 4, :], vaugm.a[:, a_, :, :], [vaugm], [dscr])
                    dma("qpool", QTm[:, :, g0:g0 + 512].rearrange("h p t -> p h t"), QM.a[0:96], [QM], [dscr])
                    dma("qpool", KTm[:, :, g0:g0 + 512].rearrange("h p t -> p h t"), KM.a[0:96], [KM], [dscr])

            def S3(cx):
                s, t0, g0, u, sj, k = cx['s'], cx['t0'], cx['g0'], cx['u'], cx['sj'], cx['k']
                gt = g0 + u * 128
                ch = gt // 128
                P = ptok[k % 3]
                lrT, lsp, Eq, Ek = lrT_r[k % 2], lsp_r[k % 2], Eq_r[k % 2], Ek_r[k % 2]
                qg, kg, lvb, qkT, Qblk, attm = qg_r[k % 2], kg_r[k % 2], lvb_r[k % 2], qkT_r[k % 2], Qblk_r[k % 2], attm_r[k % 2]
                qgTs, yos = qgTs_r[sj % 2], yos_r[sj % 2]
                mb = mbank()
                tr(mb.a[0:32, 0:128], P.a[:, 1280:1312], IDF, [P, cm], [mb])
                cp("dve", lrT.a[0:32, :], mb.a[0:32, 0:128], [mb], [lrT])
                mb = mbank()
                mm(mb.a[:, 0:256], lrT.a[0:32, :], w2blk.a[0:32, :], True, True, [lrT, w2blk], [mb])
                tt("dve", lgb.a, mb.a[:, 0:256], gateb.a, ALU.add, [mb, gateb], [lgb])
                act(lgb.a, lgb.a, AF.Exp, [lgb], [lgb], scale=-1.0)
                act(lsp.a, lgb.a, AF.Ln, [lgb], [lsp], bias=onesf.a[:, 0:1])
                mb = mbank()
                mm(mb.a[:, 0:128], UFS, lsp.a[:, 0:128], True, True, [cm, lsp], [mb])
                mm(mb.a[:, 128:256], UBS, lsp.a[:, 128:256], True, True, [cm, lsp], [mb])
                mm(mb.a[:, 256:257], lsp.a[:, 0:128], CNEG[:, 0:1], True, True, [cm, lsp], [mb])
                mm(mb.a[:, 257:258], lsp.a[:, 128:256], CNEG[:, 0:1], True, True, [cm, lsp], [mb])
                act(Eq.a, mb.a[:, 0:256], AF.Exp, [mb], [Eq])
                act(Ek.a, mb.a[:, 0:256], AF.Exp, [mb], [Ek], scale=-1.0)
                act(egl.a[:, ch, :], mb.a[:, 256:258], AF.Exp, [mb], [egl])
                lq = P.a[:, 512:640].unsqueeze(1).to_broadcast([128, 2, 128])
                lk = P.a[:, 640:768].unsqueeze(1).to_broadcast([128, 2, 128])
                stt("dve", qg.a, lq, 32 ** -0.5, Eq.a.rearrange("p (d k) -> p d k", d=2), ALU.mult, ALU.mult, [P, Eq], [qg])
                tt("dve", kg.a, lk, Ek.a.rearrange("p (d k) -> p d k", d=2), ALU.mult, [P, Ek], [kg])
                cp("act", lvb.a, P.a[:, 768:1024], [P], [lvb])
                mb = mbank()
                mbb = mb.a.bitcast(BF16)
                for d in range(2):
                    tr(mbb[:, d * 128:(d + 1) * 128], qg.a[:, d, :], idb.a, [qg, idb], [mb])
                    tr(mbb[:, (2 + d) * 128:(3 + d) * 128], kg.a[:, d, :], idb.a, [kg, idb], [mb])
                cp("act", qkT.a.rearrange("p c t -> p (c t)"), mbb[:, 0:512], [mb], [qkT])
                cp("dve", qgTs.a[:, :, u * 128:(u + 1) * 128], qkT.a[:, 0:2, :], [qkT], [qgTs])
                for d in range(2):
                    tt("dve", Qblk.a[:, d, :].rearrange("p (h l) -> p h l", h=4),
                       qkT.a[:, d:d + 1, :].to_broadcast([128, 4, 128]),
                       BM4.unsqueeze(2).to_broadcast([128, 4, 128]), ALU.mult, [qkT, cm], [Qblk])
                po = mbank()
                for d in range(2):
                    mb = mbank()
                    mm(mb.a[:, 0:512], qkT.a[:, 2 + d, :], Qblk.a[:, d, :], True, True, [qkT, Qblk], [mb])
                    msk = (UF if d == 0 else UB).unsqueeze(1).to_broadcast([128, 4, 128])
                    tt("dve", attm.a[:, d, :].rearrange("p (h l) -> p h l", h=4),
                       mb.a[:, 0:512].rearrange("p (h l) -> p h l", h=4), msk, ALU.mult, [mb, cm], [attm])
                for h in range(4):
                    for d in range(2):
                        mm(po.a[:, h * 64:(h + 1) * 64], attm.a[:, d, h * 128:(h + 1) * 128],
                           lvb.a[:, h * 64:(h + 1) * 64], d == 0, d == 1, [attm, lvb], [po])
                cp("act", yos.a[:, u, :], po.a[:, 0:256], [po], [yos])
                mb = mbank()
                for d in range(2):
                    mm(mb.a[:, d * 256:(d + 1) * 256], kg.a[:, d, :], lvb.a, True, True, [kg, lvb], [mb])
                SG = stg[ch % 2]
                for d in range(2):
                    stt("dve", SG.a[:, d * 256:(d + 1) * 256], mb.a[:, d * 256:(d + 1) * 256], egl.a[:, ch, d:d + 1],
                        BMBIG, ALU.mult, ALU.mult, [mb, egl, cm], [SG])
                dma("qpool", stgla_d[ch], SG.a, [SG], [dscr])
                if u == 3:
                    dma("qpool", qgTd[:, :, g0:g0 + 512].rearrange("d p t -> p d t"), qgTs.a, [qgTs], [dscr])
                    dma("qpool", yo_d[g0:g0 + 512, 256:512].rearrange("(u p) c -> p u c", p=128), yos.a, [yos], [dscr])

            NS = len(ctxs)

            def cap_lane(name, fn, cx):
                cur_lane[0] = name
                return fw.capture(fn, cx)

            for k in range(NS + 2):
                lanes = []
                def s1_lane(k=k):
                    if k < NS:
                        S1(ctxs[k])
                    if 0 <= k - 1 < NS and ctxs[k - 1]['u'] == 3:
                        S1c(ctxs[k - 1])
                lanes.append(fw.capture(s1_lane))
                if 0 <= k - 1 < NS:
                    lanes.append(cap_lane("s2a", S2, ctxs[k - 1]))
                    lanes.append(cap_lane("s2b", S2b, ctxs[k - 1]))
                if 0 <= k - 2 < NS:
                    lanes.append(cap_lane("s3", S3, ctxs[k - 2]))
                fw.merge_emit(lanes)
            fw.barrier()
            bump[0] = persist_mark

            xbT = alloc([128, 6, 516], F32, nb=2)
            acc = alloc([128, 6, 512], F32)
            xcb = alloc([128, 6, 512], BF16, nb=2)
            dtt = alloc([128, 4, 8], F32, nb=2)
            xsB = alloc([128, 512], BF16, nb=3)
            d1 = alloc([128, 32], F32, nb=3)
            d2 = alloc([128, 32], F32, nb=3)
            UD = alloc([128, 512], F32, nb=2)
            Ld = alloc([128, 2, 512], F32)
            scT_r = alloc([128, 512], F32, nb=2)
            Mt_r = alloc([128, 2, 512], BF16, nb=2)
            xdt = alloc([128, 512], BF16)
            xde = alloc([128, 512], BF16)
            ys_r = alloc([128, 4, 256], F32, nb=2)
            ytmp = alloc([128, 256], F32)
            eas_r = alloc([128, 4, 8], F32, nb=2)
            stg = alloc([128, 512], F32, nb=2)
            a2_banks = {"t2": banks[0:3], "t3": banks[3:5], "t4": banks[5:8]}
            a2_cnt = {"t2": 0, "t3": 0, "t4": 0}
            a2_lane = ["t2"]

            def mbank8():
                ln = a2_lane[0]
                bl = a2_banks[ln]
                b = bl[a2_cnt[ln] % len(bl)]
                a2_cnt[ln] += 1
                return b

            def T1(sj):
                s, t0, g0 = supers[sj]
                XB = xbT[sj % 2]
                XC = xcb[sj % 2]
                DT = dtt[sj % 2]
                col0 = g0 + 4 * s
                dma("sp", XB.a, xbcpre[:, :, col0:col0 + 516].rearrange("c p t -> p c t"), [], [XB])
                dma("sp", DT.a, dtr_d[:, g0 // 128:g0 // 128 + 4, :], [], [DT])
                for fc in range(6):
                    ts("dve", acc.a[:, fc, :], XB.a[:, fc, 0:512], cw.a[:, fc, 0:1], None, ALU.mult, None, [XB, cw], [acc])
                    for k in range(1, 5):
                        stt("dve", acc.a[:, fc, :], XB.a[:, fc, k:k + 512], cw.a[:, fc, k:k + 1], acc.a[:, fc, :],
                            ALU.mult, ALU.add, [XB, cw, acc], [acc])
                    act(XC.a[:, fc, :], acc.a[:, fc, :], AF.Silu, [acc, cb], [XC], bias=cb.a[:, fc:fc + 1])
                dma("qpool", CTd[:, :, g0:g0 + 512].rearrange("g p t -> p g t"), XC.a[:, 4:6, :], [XC], [dscr])

            cxs2 = []
            for sj, (s, t0, g0) in enumerate(supers):
                for u in range(4):
                    cxs2.append(dict(sj=sj, g0=g0, u=u, k=len(cxs2)))

            def T2(cx):
                sj, g0, u, k = cx['sj'], cx['g0'], cx['u'], cx['k']
                XC = xcb[sj % 2]
                DT = dtt[sj % 2]
                eas = eas_r[sj % 2]
                gt = g0 + u * 128
                ch = gt // 128
                tsl = slice(u * 128, (u + 1) * 128)
                XS = xsB[k % 3]
                A1 = d1[k % 3]
                A2 = d2[k % 3]
                scT = scT_r[k % 2]
                mb = mbank8()
                mbb = mb.a.bitcast(BF16)
                for c in range(4):
                    tr(mbb[:, c * 128:(c + 1) * 128], XC.a[:, c, tsl], idb.a, [XC, idb], [mb])
                cp("act", XS.a, mbb[:, 0:512], [mb], [XS])
                tt("dve", A1.a[:, 0:8], DT.a[:, u, :], dtb.a, ALU.add, [DT, dtb], [A1])
                act(A1.a[:, 0:8], A1.a[:, 0:8], AF.Exp, [A1], [A1])
                act(A1.a[:, 0:8], A1.a[:, 0:8], AF.Ln, [A1], [A1], bias=onesf.a[:, 0:1])
                tt("dve", A1.a[:, 8:16], A1.a[:, 0:8], abc.a, ALU.mult, [A1, abc], [A1])
                mb = mbank8()
                mm(mb.a[:, 0:4], UF, A1.a[:, 8:12], True, True, [cm, A1], [mb])
                mm(mb.a[:, 4:8], UB, A1.a[:, 12:16], True, True, [cm, A1], [mb])
                mm(mb.a[:, 8:16], onesf.a, A1.a[:, 8:16], True, True, [onesf, A1], [mb])
                cp("dve", A2.a[:, 0:16], mb.a[:, 0:16], [mb], [A2])
                ts("dve", A2.a[:, 16:24], A2.a[:, 0:8], -1.0, None, ALU.mult, None, [A2], [A2])
                act(eas.a[:, u, :], A2.a[:, 0:8], AF.Exp, [A2], [eas])
                tt("dve", A2.a[:, 24:32], A2.a[:, 8:16], A2.a[:, 0:8], ALU.subtract, [A2], [A2])
                act(A2.a[:, 24:32], A2.a[:, 24:32], AF.Exp, [A2], [A2])
                act(cdec.a[:, ch, :], A2.a[:, 8:16], AF.Exp, [A2], [cdec])
                msc = mbank8()
                for g in range(2):
                    mm(msc.a[:, g * 128:(g + 1) * 128], XC.a[:, 2 + g, tsl], XC.a[:, 4 + g, tsl], True, True, [XC], [msc])
                for d in range(2):
                    tt("dve", scT.a[:, d * 256:(d + 1) * 256].rearrange("p (g l) -> p g l", g=2),
                       msc.a[:, 0:256].rearrange("p (g l) -> p g l", g=2),
                       (UF if d == 0 else UB).unsqueeze(1).to_broadcast([128, 2, 128]), ALU.mult, [msc, cm], [scT])
                if u == 3:
                    dma("qpool", eacs_d[:, g0 // 128:g0 // 128 + 4, :], eas.a, [eas], [dscr])

            def T3(cx):
                sj, g0, u, k = cx['sj'], cx['g0'], cx['u'], cx['k']
                A1 = d1[k % 3]
                A2 = d2[k % 3]
                scT = scT_r[k % 2]
                Mt = Mt_r[k % 2]
                for d in range(2):
                    U_ = UD[d]
                    tt("dve", U_.a.rearrange("p (h l) -> p h l", h=4),
                       (UF if d == 0 else UB).unsqueeze(1).to_broadcast([128, 4, 128]),
                       A1.a[:, 8 + 4 * d:12 + 4 * d].unsqueeze(2).to_broadcast([128, 4, 128]), ALU.mult, [cm, A1], [U_])
                    mb = mbank8()
                    mm(mb.a[:, 0:512], SU[d], U_.a, True, True, [cm, U_], [mb])
                    act(Ld.a[:, d, :], mb.a[:, 0:512], AF.Exp, [mb], [Ld])
                    tt("dve", Mt.a[:, d, :].rearrange("p (g hh l) -> p g hh l", g=2, hh=2),
                       Ld.a[:, d, :].rearrange("p (g hh l) -> p g hh l", g=2, hh=2),
                       scT.a[:, d * 256:(d + 1) * 256].rearrange("p (g l) -> p g l", g=2).unsqueeze(2).to_broadcast([128, 2, 2, 128]),
                       ALU.mult, [Ld, scT], [Mt])

            def T4(cx):
                sj, g0, u, k = cx['sj'], cx['g0'], cx['u'], cx['k']
                gt = g0 + u * 128
                ch = gt // 128
                XS = xsB[k % 3]
                A1 = d1[k % 3]
                A2 = d2[k % 3]
                Mt = Mt_r[k % 2]
                ys = ys_r[sj % 2]
                tt("dve", xdt.a.rearrange("p (d h e) -> p d h e", d=2, h=4),
                   XS.a[:, 0:256].rearrange("p (h e) -> p h e", h=4).unsqueeze(1).to_broadcast([128, 2, 4, 64]),
                   A1.a[:, 0:8].rearrange("p (d h) -> p d h", d=2).unsqueeze(3).to_broadcast([128, 2, 4, 64]),
                   ALU.mult, [XS, A1], [xdt])
                tt("dve", xde.a.rearrange("p (d h e) -> p d h e", d=2, h=4),
                   xdt.a.rearrange("p (d h e) -> p d h e", d=2, h=4),
                   A2.a[:, 24:32].rearrange("p (d h) -> p d h", d=2).unsqueeze(3).to_broadcast([128, 2, 4, 64]),
                   ALU.mult, [xdt, A2], [xde])
                py = mbank8()
                for h in range(4):
                    for d in range(2):
                        mm(py.a[:, h * 64:(h + 1) * 64], Mt.a[:, d, h * 128:(h + 1) * 128],
                           xdt.a[:, d * 256 + h * 64:d * 256 + (h + 1) * 64], d == 0, d == 1, [Mt, xdt], [py])
                tt("dve", ytmp.a.rearrange("p (h e) -> p h e", h=4), XS.a[:, 0:256].rearrange("p (h e) -> p h e", h=4),
                   dbc.a.unsqueeze(2).to_broadcast([128, 4, 64]), ALU.mult, [XS, dbc], [ytmp])
                tt("dve", ys.a[:, u, :], py.a[:, 0:256], ytmp.a, ALU.add, [py, ytmp], [ys])
                pst = mbank8()
                xdev = xde.a.rearrange("p (d h e) -> p d h e", d=2, h=4)
                for g in range(2):
                    mm(pst.a[:, g * 256:(g + 1) * 256].rearrange("p (d hh e) -> p d hh e", d=2, hh=2),
                       XS.a[:, 256 + g * 128:256 + (g + 1) * 128], xdev[:, :, 2 * g:2 * g + 2, :], True, True, [XS, xde], [pst])
                SG = stg[ch % 2]
                cp("act", SG.a, pst.a[:, 0:512], [pst], [SG])
                dma("qpool", stssd_d[ch], SG.a, [SG], [dscr])
                if u == 3:
                    dma("qpool", yo_d[g0:g0 + 512, 0:256].rearrange("(u p) c -> p u c", p=128), ys.a, [ys], [dscr])

            NS2 = len(cxs2)
            T1(0)
            t1ops = []
            for k in range(NS2 + 2):
                lanes = []
                if k < NS2:
                    if cxs2[k]['u'] == 0:
                        t1ops = fw.capture(T1, cxs2[k]['sj'] + 1) if cxs2[k]['sj'] + 1 < len(supers) else []
                    uu = cxs2[k]['u']
                    q4 = (len(t1ops) + 3) // 4
                    lanes.append(t1ops[uu * q4:(uu + 1) * q4])
                    a2_lane[0] = "t2"
                    lanes.append(fw.capture(T2, cxs2[k]))
                if 0 <= k - 1 < NS2:
                    a2_lane[0] = "t3"
                    lanes.append(fw.capture(T3, cxs2[k - 1]))
                if 0 <= k - 2 < NS2:
                    a2_lane[0] = "t4"
                    lanes.append(fw.capture(T4, cxs2[k - 2]))
                fw.merge_emit(lanes)
            fw.barrier()
            bump[0] = persist_mark

            Sssd = alloc([128, 512], F32)
            Sssdb = alloc([128, 512], BF16)
            Sgla = alloc([128, 2, 256], F32)
            Sglab = alloc([128, 2, 256], BF16)
            NRB = 3
            accB = alloc([128, 512], F32, nb=NRB)
            eaB = alloc([128, 8], F32, nb=NRB)
            stS = alloc([128, 512], F32, nb=NRB)
            stG = alloc([128, 512], F32, nb=NRB)
            ctB = alloc([128, 2, 128], BF16, nb=NRB)
            qgB = alloc([128, 2, 128], BF16, nb=NRB)
            zrB = alloc([128, 512], F32, nb=NRB)
            szB = alloc([128, 512], F32)
            tmpB = alloc([128, 256], F32)
            tmpB2 = alloc([128, 256], F32)
            smB = alloc([128, 16], F32, nb=2)
            outB = alloc([128, 512], BF16)
            mxs = alloc([128, 4, 128], BF16, nb=2)
            yoB = [Buf() for _ in range(nchunks)]
            bbanks = banks[6:8]
            bbi = [0]

            def bbank():
                b = bbanks[bbi[0] % 2]
                bbi[0] += 1
                return b

            def genB():
                it = 0
                for s in range(nseq):
                    nch = seqs[s] // 128
                    c_base = offs[s] // 128
                    for sweep in range(2):
                        d = 1 - sweep
                        fw.op("dve", lambda e: e.memset(Sssd.a, 0.0), [], [Sssd])
                        fw.op("dve", lambda e: e.memset(Sssdb.a, 0.0), [], [Sssdb])
                        fw.op("dve", lambda e: e.memset(Sgla.a, 0.0), [], [Sgla])
                        fw.op("dve", lambda e: e.memset(Sglab.a, 0.0), [], [Sglab])
                        order = list(range(nch - 1, -1, -1)) if d == 1 else list(range(nch))
                        for oi, cl in enumerate(order):
                            ch = c_base + cl
                            gt = ch * 128
                            AC = accB[it % NRB]
                            EA = eaB[it % NRB]
                            SS = stS[it % NRB]
                            SGt = stG[it % NRB]
                            CT_ = ctB[it % NRB]
                            QG = qgB[it % NRB]
                            ZR = zrB[it % NRB]
                            SM = smB[it % 2]
                            MX = mxs[it % 2]
                            it += 1
                            dma("sp", AC.a, yo_d[gt:gt + 128, :], [yoB[ch]], [AC])
                            dma("sp", EA.a, eacs_d[:, ch, :], [], [EA])
                            dma("sp", SS.a, stssd_d[ch], [], [SS])
                            dma("sp", SGt.a, stgla_d[ch], [], [SGt])
                            dma("sp", CT_.a, CTd[:, :, gt:gt + 128].rearrange("g p t -> p g t"), [], [CT_])
                            dma("sp", QG.a, qgTd[:, :, gt:gt + 128].rearrange("d p t -> p d t"), [], [QG])
                            if sweep == 1:
                                dma("sp", ZR.a, zr_d[gt:gt + 128, :], [], [ZR])
                            Sv = Sssdb.a.rearrange("p (g d hh e) -> p g d hh e", g=2, d=2, hh=2)
                            S5 = Sssd.a.rearrange("p (g d hh e) -> p g d hh e", g=2, d=2, hh=2)
                            ST5 = SS.a.rearrange("p (g d hh e) -> p g d hh e", g=2, d=2, hh=2)
                            if oi > 0:
                                cp("dve", Sv[:, :, d, :, :], S5[:, :, d, :, :], [Sssd], [Sssdb])
                                cp("dve", Sglab.a[:, d, :], Sgla.a[:, d, :], [Sgla], [Sglab])
                            if sweep == 1:
                                act(szB.a, ZR.a, AF.Exp, [ZR], [szB], scale=-1.0)
                            pr = bbank()
                            for g in range(2):
                                mm(pr.a[:, g * 128:(g + 1) * 128].rearrange("p (hh e) -> p hh e", hh=2), CT_.a[:, g, :],
                                   Sv[:, g, d, :, :], True, True, [CT_, Sssdb], [pr])
                            mm(pr.a[:, 256:512], QG.a[:, d, :], Sglab.a[:, d, :], True, True, [QG, Sglab], [pr])
                            tt("dve", tmpB.a.rearrange("p (h e) -> p h e", h=4), pr.a[:, 0:256].rearrange("p (h e) -> p h e", h=4),
                               EA.a[:, 4 * d:4 * d + 4].unsqueeze(2).to_broadcast([128, 4, 64]), ALU.mult, [pr, EA], [tmpB])
                            tt("dve", AC.a[:, 0:256], AC.a[:, 0:256], tmpB.a, ALU.add, [AC, tmpB], [AC])
                            tt("dve", AC.a[:, 256:512], AC.a[:, 256:512], pr.a[:, 256:512], ALU.add, [AC, pr], [AC])
                            cdv = cdec.a[:, ch, 4 * d:4 * d + 4].rearrange("p (g hh) -> p g hh", g=2).unsqueeze(3).to_broadcast([128, 2, 2, 64])
                            tt("dve", S5[:, :, d, :, :], S5[:, :, d, :, :], cdv, ALU.mult, [Sssd, cdec], [Sssd])
                            tt("dve", S5[:, :, d, :, :], S5[:, :, d, :, :], ST5[:, :, d, :, :], ALU.add, [Sssd, SS], [Sssd])
                            stt("dve", Sgla.a[:, d, :], Sgla.a[:, d, :], egl.a[:, ch, d:d + 1], SGt.a[:, d * 256:(d + 1) * 256],
                                ALU.mult, ALU.add, [Sgla, egl, SGt], [Sgla])
                            if sweep == 0:
                                dma("qpool", yo_d[gt:gt + 128, :], AC.a, [AC], [yoB[ch]])
                            else:
                                ts("dve", szB.a, szB.a, 1.0, None, ALU.add, None, [szB], [szB])
                                fw.op("dve", lambda e: e.reciprocal(out=szB.a, in_=szB.a), [szB], [szB])
                                tt("dve", szB.a, szB.a, ZR.a, ALU.mult, [szB, ZR], [szB])
                                tt("dve", tmpB.a, AC.a[:, 0:256], szB.a[:, 0:256], ALU.mult, [AC, szB], [tmpB])
                                tt("dve", tmpB2.a, tmpB.a, tmpB.a, ALU.mult, [tmpB], [tmpB2])
                                red(SM.a[:, 0:1], tmpB2.a, [tmpB2], [SM])
                                tt("dve", tmpB2.a, AC.a[:, 256:512], AC.a[:, 256:512], ALU.mult, [AC], [tmpB2])
                                red(SM.a[:, 1:5], tmpB2.a.rearrange("p (h e) -> p h e", h=4), [tmpB2], [SM])
                                act(SM.a[:, 8:9], SM.a[:, 0:1], AF.Ln, [SM], [SM], bias=epsb.a[:, 0:1], scale=1.0 / 256)
                                act(SM.a[:, 9:13], SM.a[:, 1:5], AF.Ln, [SM], [SM], bias=epsb.a[:, 0:1], scale=1.0 / 64)
                                act(SM.a[:, 8:13], SM.a[:, 8:13], AF.Exp, [SM], [SM], scale=-0.5)
                                stt("dve", outB.a[:, 0:256], tmpB.a, SM.a[:, 8:9], gssd.a, ALU.mult, ALU.mult, [tmpB, SM, gssd], [outB])
                                tt("dve", tmpB2.a.rearrange("p (h e) -> p h e", h=4), AC.a[:, 256:512].rearrange("p (h e) -> p h e", h=4),
                                   SM.a[:, 9:13].unsqueeze(2).to_broadcast([128, 4, 64]), ALU.mult, [AC, SM], [tmpB2])
                                tt("dve", tmpB2.a.rearrange("p (h e) -> p h e", h=4), tmpB2.a.rearrange("p (h e) -> p h e", h=4),
                                   ggla.a.unsqueeze(1).to_broadcast([128, 4, 64]), ALU.mult, [tmpB2, ggla], [tmpB2])
                                tt("dve", outB.a[:, 256:512], tmpB2.a, szB.a[:, 256:512], ALU.mult, [tmpB2, szB], [outB])
                                mb = bbank()
                                mbb = mb.a.bitcast(BF16)
                                for c in range(4):
                                    tr(mbb[:, c * 128:(c + 1) * 128], outB.a[:, c * 128:(c + 1) * 128], idb.a, [outB, idb], [mb])
                                cp("dve", MX.a.rearrange("p c t -> p (c t)"), mbb[:, 0:512], [mb], [MX])
                                dma("qpool", mixT_d[0:2, :, gt:gt + 128].rearrange("c p t -> p c t"), MX.a[:, 0:2, :], [MX], [dscr])
                                dma("qpool", mixT_d[4:6, :, gt:gt + 128].rearrange("c p t -> p c t"), MX.a[:, 2:4, :], [MX], [dscr])
                            yield

            TM = max(seqs)
            nbm = TM // 128
            KG = alloc([128, TM], BF16)
            VG = alloc([128, nbm, 130], BF16)
            KM2 = alloc([128, 2, TM], BF16)
            VM2 = alloc([128, nbm, 130], BF16)
            qlo = alloc([128, 512], BF16, nb=2)
            qhi = alloc([128, 512], BF16, nb=2)
            qml = alloc([128, 512], BF16, nb=2)
            PT = alloc([128, 512], BF16, nb=4)
            osb = alloc([128, 512], F32, nb=2)
            ob = alloc([128, 512], BF16, nb=2)
            rlr = alloc([128, 2, 512], BF16, nb=2)
            onesb = alloc([128, 64], BF16)
            fw.op("dve", lambda e: e.memset(onesb.a, 1.0), [], [onesb])
            for q_ in qlo + qhi:
                fw.op("dve", lambda e, q_=q_: e.memset(q_.a, 0.0), [], [q_])
            sc_banks = banks[0:3]
            po_banks = banks[3:5]
            bcb = banks[5]
            cnt = {"sc": 0, "po": 0, "q": 0, "pt": 0}
            gB = genB()
            pending_tail = [None, None]
            for s in range(nseq):
                Ts = seqs[s]
                nb = Ts // 128
                o0 = offs[s]
                for (kind, heads) in (("g", [0, 1, 2, 3]), ("m", [0, 1]), ("m", [2, 3])):
                    if kind == "g":
                        dma("sp", KG.a[:, 0:Ts], KTg[:, o0:o0 + Ts], [], [KG])
                        dma("sp", VG.a[:, 0:nb, :], VAg[:, o0 // 128:o0 // 128 + nb, :], [], [VG])
                    else:
                        for hi_, h in enumerate(heads):
                            dma("sp", KM2.a[0:96, hi_, 0:Ts], KTm[h, :, o0:o0 + Ts], [], [KM2])
                        dma("sp", VM2.a[:, 0:nb, :], VAm[heads[0] // 2, :, o0 // 128:o0 // 128 + nb, :], [], [VM2])
                    for j in range(Ts // 512):
                        gq0 = o0 + j * 512
                        for hi_, h in enumerate(heads):
                            qi = cnt["q"]
                            cnt["q"] += 1
                            if kind == "g":
                                if h < 2:
                                    Qt = qlo[qi % 2]
                                    dma("sp", Qt.a[0:64, :], QTg[h, :, gq0:gq0 + 512], [], [Qt])
                                else:
                                    Qt = qhi[qi % 2]
                                    dma("sp", Qt.a[64:128, :], QTg[h, :, gq0:gq0 + 512], [], [Qt])
                                kr_ = 128
                                scale = 64 ** -0.5
                                kv = h // 2

                                def kslice(kb):
                                    return KG.a[:, kb * 128:(kb + 1) * 128], KG

                                def vslice(kb, kv=kv):
                                    return VG.a[:, kb, kv * 65:(kv + 1) * 65], VG
                                mixc, mixp = 2 + h // 2, (h % 2) * 64
                            else:
                                Qt = qml[qi % 2]
                                dma("sp", Qt.a[0:96, :], QTm[h, :, gq0:gq0 + 512], [], [Qt])
                                kr_ = 96
                                scale = 96 ** -0.5

                                def kslice(kb, hi_=hi_):
                                    return KM2.a[0:96, hi_, kb * 128:(kb + 1) * 128], KM2

                                def vslice(kb, hi_=hi_):
                                    return VM2.a[:, kb, hi_ * 65:(hi_ + 1) * 65], VM2
                                mixc, mixp = 6 + h // 2, (h % 2) * 64
                            po = po_banks[cnt["po"] % 2]
                            OS = osb[cnt["po"] % 2]
                            OB = ob[cnt["po"] % 2]
                            cnt["po"] += 1
                            pend = []
                            fw.cap = []

                            def issue_qk(kb):
                                sc = sc_banks[cnt["sc"] % 3]
                                cnt["sc"] += 1
                                ka, kbuf = kslice(kb)
                                mm(sc.a[:, 0:512], ka, Qt.a[0:kr_, :], True, True, [kbuf, Qt], [sc])
                                pt = PT[cnt["pt"] % 4]
                                cnt["pt"] += 1
                                act(pt.a, sc.a[:, 0:512], AF.Exp, [sc], [pt], scale=scale)
                                pend.append((kb, pt))

                            def issue_pv():
                                kb, pt = pend.pop(0)
                                va, vbuf = vslice(kb)
                                mm(po.a[0:65, 0:512], va, pt.a, kb == 0, kb == nb - 1, [vbuf, pt], [po])

                            for kb in range(nb):
                                issue_qk(kb)
                                if kb == 2 and pending_tail[0] is not None:
                                    fw.cap.extend(pending_tail[0])
                                    pending_tail[0] = None
                                if kb == min(12, nb - 1) and pending_tail[1] is not None:
                                    fw.cap.extend(pending_tail[1])
                                    pending_tail[1] = None
                                if len(pend) > 2:
                                    issue_pv()
                            while pend:
                                issue_pv()
                            uops = fw.cap
                            fw.cap = []
                            RL = rlr[(cnt["po"] - 1) % 2]
                            cp("act", OS.a[0:65, :], po.a[0:65, 0:512], [po], [OS])
                            fw.op("dve", lambda e, OS=OS: e.reciprocal(out=OS.a[64:65, :], in_=OS.a[64:65, :]), [OS], [OS])
                            cp("dve", RL.a[64:65, 0, :], OS.a[64:65, :], [OS], [RL])
                            tt("dve", RL.a[64:65, 1, :], OS.a[64:65, :], RL.a[64:65, 0, :], ALU.subtract, [OS, RL], [RL])
                            pending_tail[0] = fw.cap
                            fw.cap = []
                            mm(bcb.a[0:64, 0:512], onesb.a[64:65, 0:64], RL.a[64:65, 0, :], True, False, [onesb, RL], [bcb])
                            mm(bcb.a[0:64, 0:512], onesb.a[64:65, 0:64], RL.a[64:65, 1, :], False, True, [onesb, RL], [bcb])
                            tt("dve", OB.a[0:64, :], OS.a[0:64, :], bcb.a[0:64, 0:512], ALU.mult, [OS, bcb], [OB])
                            dma("qpool", mixT_d[mixc, mixp:mixp + 64, gq0:gq0 + 512], OB.a[0:64, :], [OB], [dscr])
                            pending_tail[1] = fw.cap
                            fw.cap = []
                            next(gB, None)
                            bops = fw.cap
                            fw.cap = None
                            fw.merge_emit([uops, bops])
            for pi_ in range(2):
                if pending_tail[pi_] is not None:
                    for op_ in pending_tail[pi_]:
                        fw.op(*op_)
                    pending_tail[pi_] = None
            for _ in gB:
                pass
            fw.barrier()
            bump[0] = persist_mark

            wo = alloc([128, 8, DM], BF16)
            w2 = alloc([128, 32, DM], BF16)
            d_mark = bump[0]
            wstg = alloc([128, DM], F32, nb=2)
            for c in range(8):
                wsx = wstg[c % 2]
                dma("sp", wsx.a, Wd['w_out'][l, c * 128:(c + 1) * 128, :], [], [wsx])
                cp("dve" if c % 2 == 0 else "act", wo.a[:, c, :], wsx.a, [wsx], [wo])
            for c in range(32):
                wsx = wstg[c % 2]
                dma("sp", wsx.a, Wd['w_ff2'][l, c * 128:(c + 1) * 128, :], [], [wsx])
                cp("dve" if c % 2 == 0 else "act", w2.a[:, c, :], wsx.a, [wsx], [w2])
            if last:
                dma("sp", g1bc.a, Wd['final_norm_g'].partition_broadcast(128), [], [g1bc])
            fw.barrier()
            bump[0] = d_mark
            x1_r = alloc([128, 4, DM], F32, nb=2)
            x1u = [[Buf() for _ in range(4)] for _ in range(2)]
            hb2 = alloc([128, DM], BF16)
            junk2 = hb2
            h2T_r = alloc([128, 8, 512], BF16, nb=2)
            w1s = alloc([128, 8, 256], BF16, nb=2)
            actT = alloc([128, 32, 512], BF16)
            mixs = alloc([128, 8, 512], BF16)
            rtmp = alloc([128, 512], F32, nb=1)
            rtmp = [rtmp, rtmp]
            smD = alloc([128, 8], F32, nb=2)
            smE = alloc([128, 8], F32, nb=2)
            dcn = {"w1": 0, "k5": 0, "k6": 0}

            def DPa(sj):
                s, t0, g0 = supers[sj]
                x1 = x1_r[sj % 2]
                dma("sp", mixs.a, mixT_d[:, :, g0:g0 + 512].rearrange("c p t -> p c t"), [], [mixs])
                dma("sp", x1.a, xsrc[g0:g0 + 512, :].rearrange("(u p) c -> p u c", p=128), [], [x1] + x1u[sj % 2])
                for u in range(4):
                    for hf in range(2):
                        pb = banks[dcn["k5"] % 2]
                        dcn["k5"] += 1
                        for c in range(8):
                            mm(pb.a[:, 0:512], mixs.a[:, c, u * 128:(u + 1) * 128], wo.a[:, c, hf * 512:(hf + 1) * 512],
                               c == 0, c == 7, [mixs, wo], [pb])
                        tt("dve", x1.a[:, u, hf * 512:(hf + 1) * 512], x1.a[:, u, hf * 512:(hf + 1) * 512], pb.a[:, 0:512],
                           ALU.add, [x1, pb], [x1, x1u[sj % 2][u]])

            def DPb(sj):
                x1 = x1_r[sj % 2]
                h2T = h2T_r[sj % 2]
                for u in range(4):
                    SM = smD[u % 2]
                    act(junk2.a, x1.a[:, u, :], AF.Square, [x1u[sj % 2][u]], [junk2, SM], accum=SM.a[:, 0:1])
                    act(SM.a[:, 1:2], SM.a[:, 0:1], AF.Ln, [SM], [SM], bias=epsb.a[:, 0:1], scale=1.0 / DM)
                    act(SM.a[:, 2:3], SM.a[:, 1:2], AF.Exp, [SM], [SM], scale=-0.5)
                    stt("dve", hb2.a, x1.a[:, u, :], SM.a[:, 2:3], g2bc.a, ALU.mult, ALU.mult, [x1u[sj % 2][u], SM, g2bc], [hb2])
                    pTt = banks[2]
                    pTtb = pTt.a.bitcast(BF16)
                    for c in range(8):
                        tr(pTtb[:, c * 128:(c + 1) * 128], hb2.a[:, c * 128:(c + 1) * 128], idb.a, [hb2, idb], [pTt])
                    cp("act", h2T.a[:, :, u * 128:(u + 1) * 128], pTtb.rearrange("p (c t) -> p c t", c=8), [pTt], [h2T])

            def DM_(sj):
                s, t0, g0 = supers[sj]
                x1 = x1_r[sj % 2]
                h2T = h2T_r[sj % 2]
                for fg in range(16):
                    W1 = w1s[dcn["w1"] % 2]
                    dcn["w1"] += 1
                    dma("sp", W1.a, w1b_d[l, :, fg * 256:(fg + 1) * 256].rearrange("(c p) f -> p c f", p=128), [], [W1])
                    for f2 in range(2):
                        f = fg * 2 + f2
                        pb = banks[3 + (f % 3)]
                        for c in range(8):
                            mm(pb.a[:, 0:512], W1.a[:, c, f2 * 128:(f2 + 1) * 128], h2T.a[:, c, :], c == 0, c == 7, [W1, h2T], [pb])
                        R_ = rtmp[f % 2]
                        act(R_.a, pb.a[:, 0:512], AF.Relu, [pb], [R_])
                        tt("dve", actT.a[:, f, :], R_.a, R_.a, ALU.mult, [R_], [actT])
                for u in range(4):
                    XO = T(x1.a[:, u, :], x1.b)
                    SM = smE[u % 2]
                    for hf in range(2):
                        pb = banks[6 + (dcn["k6"] % 2)]
                        dcn["k6"] += 1
                        for f in range(32):
                            mm(pb.a[:, 0:512], actT.a[:, f, u * 128:(u + 1) * 128], w2.a[:, f, hf * 512:(hf + 1) * 512],
                               f == 0, f == 31, [actT, w2], [pb])
                        tt("dve", XO.a[:, hf * 512:(hf + 1) * 512], x1.a[:, u, hf * 512:(hf + 1) * 512], pb.a[:, 0:512],
                           ALU.add, [x1, pb], [XO, x1u[sj % 2][u]])
                    if not last:
                        dma("qpool", xres_d[g0 + u * 128:g0 + (u + 1) * 128, :], XO.a, [XO], [dscr])
                    else:
                        act(rtmp[0].a.bitcast(BF16), XO.a, AF.Square, [XO], [rtmp[0], SM], accum=SM.a[:, 4:5])
                        act(SM.a[:, 5:6], SM.a[:, 4:5], AF.Ln, [SM], [SM], bias=epsb.a[:, 0:1], scale=1.0 / DM)
                        act(SM.a[:, 6:7], SM.a[:, 5:6], AF.Exp, [SM], [SM], scale=-0.5)
                        stt("dve", XO.a, XO.a, SM.a[:, 6:7], g1bc.a, ALU.mult, ALU.mult, [XO, SM, g1bc], [XO])
                        dma("qpool", y_out[g0 + u * 128:g0 + (u + 1) * 128, :], XO.a, [XO], [dscr])

            DPa(0)
            DPb(0)
            for sj in range(len(supers)):
                lanes = [fw.capture(DM_, sj)]
                wins = [(0.0, 1.0)]
                if sj + 1 < len(supers):
                    lanes.append(fw.capture(DPa, sj + 1))
                    wins.append((0.0, 0.7))
                    lanes.append(fw.capture(DPb, sj + 1))
                    wins.append((0.25, 0.98))
                fw.merge_emit(lanes, wins)
            fw.barrier()
            bump[0] = persist_mark

        fw.finish()
        build.nops = fw.nops
    return nc


SEQS = [2048, 2048, 8192]
DEPTH = 2
_cache = {}


def kernel(**inputs):
    ncores = 8
    xp = np.asarray(inputs['x_prompt'], np.float32)
    xs = np.asarray(inputs['x_sample'], np.float32)
    key = "main"
    if key not in _cache:
        _cache[key] = build(SEQS, DEPTH)
    nc = _cache[key]
    tab, cmat = host_consts(max(SEQS))
    base = {}
    for k in PNAMES:
        a = np.asarray(inputs[k], np.float32)
        if k in ('ssd_dt_bias', 'ssd_a_log'):
            a = a.reshape(DEPTH, 8)
        if k == 'gla_gate_b':
            a = a.reshape(DEPTH, 256)
        base[k] = np.ascontiguousarray(a)
    base['tab'] = tab
    base['cm'] = cmat
    in_maps = []
    for c in range(ncores):
        xc = np.concatenate([xp[2 * c], xp[2 * c + 1], xs[c]], axis=0)
        m = dict(base)
        m['x'] = np.ascontiguousarray(xc)
        in_maps.append(m)
    res = run_bass_kernel_spmd(nc, in_maps, core_ids=list(range(ncores)))
    yp = np.empty_like(xp)
    ys = np.empty_like(xs)
    for c in range(ncores):
        y = res.results[c]['y']
        yp[2 * c] = y[0:2048]
        yp[2 * c + 1] = y[2048:4096]
        ys[c] = y[4096:12288]
    return (yp, ys)
```

```python
import contextlib
import math
import numpy as np
import concourse.bass as bass
import concourse.mybir as mybir
from concourse.bass_utils import run_bass_kernel_spmd

F32 = mybir.dt.float32
BF16 = mybir.dt.bfloat16
U8 = mybir.dt.uint8
ALU = mybir.AluOpType
AF = mybir.ActivationFunctionType
AX = mybir.AxisListType

DM = 1024
DIN = 2760
DFF = 4096
EPS = 1e-6
DMA_R = 8
NTM = 1992

PNAMES = ['norm1_g', 'w_in', 'ssd_conv_w', 'ssd_conv_b', 'ssd_dt_bias', 'ssd_a_log', 'ssd_d', 'ssd_norm_g',
          'gqa_q_norm_g', 'gqa_k_norm_g', 'gla_gate_w2', 'gla_gate_b', 'gla_norm_g', 'mla_q_norm_g', 'mla_w_uq',
          'mla_kv_norm_g', 'mla_w_ukv', 'w_out', 'norm2_g', 'w_ff1', 'w_ff2', 'final_norm_g']


class Buf:
    __slots__ = ("name", "w", "r")

    def __init__(self, name=None):
        self.name = name
        self.w = None
        self.r = {}


class T:
    __slots__ = ("a", "b")

    def __init__(self, a, b=None):
        self.a = a
        self.b = b if b is not None else Buf()


class FW:
    def __init__(self, nc, stack):
        self.nc = nc
        self.stack = stack
        self.sems = {}
        self.cnt = {}
        self.dsems = {}
        for n in ("pe", "act", "dve", "pool"):
            self.sems["c_" + n] = stack.enter_context(nc.semaphore("c_" + n))
            self.cnt[n] = 0
        for q in ("sp", "qpool", "qact"):
            self.dsems[q] = []
            self.cnt[q] = 0
            for i in range(DMA_R):
                k = "d_%s%d" % (q, i)
                self.sems[k] = stack.enter_context(nc.semaphore(k))
                self.dsems[q].append(k)
        self.streams = {"pe": [], "act": [], "dve": [], "pool": [], "sp": []}
        self.stream_of = {"pe": "pe", "act": "act", "dve": "dve", "pool": "pool",
                          "sp": "sp", "qpool": "pool", "qact": "act"}
        self.known = {s: {} for s in self.streams}
        self.latest = {}
        self.nops = 0
        self.cap = None

    def capture(self, fn, *args):
        assert self.cap is None
        self.cap = []
        fn(*args)
        ops = self.cap
        self.cap = None
        return ops

    def merge_emit(self, lanes, windows=None):
        if windows is None:
            windows = [(0.0, 1.0)] * len(lanes)
        windows = [w for l, w in zip(lanes, windows) if l]
        lanes = [l for l in lanes if l]
        idx = [0] * len(lanes)
        while True:
            best, bv = -1, 9.0
            for i, l in enumerate(lanes):
                if idx[i] < len(l):
                    v = windows[i][0] + (windows[i][1] - windows[i][0]) * (idx[i] + 0.5) / len(l)
                    if v < bv:
                        best, bv = i, v
            if best < 0:
                break
            self.op(*lanes[best][idx[best]])
            idx[best] += 1

    def op(self, eng, fn, reads=(), writes=()):
        if self.cap is not None:
            self.cap.append((eng, fn, tuple(reads), tuple(writes)))
            return None
        st = self.stream_of[eng]
        deps = {}
        for t in reads:
            b = t.b if isinstance(t, T) else t
            if b.w is not None and deps.get(b.w[0], 0) < b.w[1]:
                deps[b.w[0]] = b.w[1]
        for t in writes:
            b = t.b if isinstance(t, T) else t
            if b.w is not None and deps.get(b.w[0], 0) < b.w[1]:
                deps[b.w[0]] = b.w[1]
            for k, v in b.r.items():
                if deps.get(k, 0) < v:
                    deps[k] = v
        if eng in self.dsems:
            n = self.cnt[eng]
            slot = self.dsems[eng][n % DMA_R]
            rnd = n // DMA_R
            if rnd > 0 and deps.get(slot, 0) < 16 * rnd:
                deps[slot] = 16 * rnd
            tok = (slot, 16 * (rnd + 1))
            self.cnt[eng] = n + 1
            inc = (slot, 16)
        else:
            self.cnt[eng] += 1
            tok = ("c_" + eng, self.cnt[eng])
            inc = ("c_" + eng, 1)
        self.latest[tok[0]] = tok[1]
        known = self.known[st]
        waits = []
        for k, v in deps.items():
            if eng == "pe" and k == "c_pe":
                continue
            if known.get(k, 0) >= v:
                continue
            known[k] = v
            waits.append((k, v))
        self.streams[st].append((waits, fn, inc))
        self.nops += 1
        wset = set()
        for t in writes:
            b = t.b if isinstance(t, T) else t
            b.w = tok
            b.r = {}
            wset.add(id(b))
        for t in reads:
            b = t.b if isinstance(t, T) else t
            if id(b) in wset:
                continue
            if b.r.get(tok[0], 0) < tok[1]:
                b.r[tok[0]] = tok[1]
        return tok

    def barrier(self):
        for st in self.streams:
            known = self.known[st]
            waits = []
            for k, v in self.latest.items():
                if known.get(k, 0) < v:
                    known[k] = v
                    waits.append((k, v))
            if waits:
                self.streams[st].append((waits, None, None))

    def finish(self):
        nc = self.nc
        self.barrier()
        sems = self.sems
        streams = self.streams

        def replay(engobj, ops):
            for waits, fn, inc in ops:
                for k, v in waits:
                    engobj.wait_ge(sems[k], v)
                if fn is not None:
                    ins = fn(engobj)
                    ins.then_inc(sems[inc[0]], inc[1])

        with nc.Block() as block:
            @block.tensor
            def _(eng):
                replay(eng, streams["pe"])

            @block.scalar
            def _(eng):
                replay(eng, streams["act"])

            @block.vector
            def _(eng):
                replay(eng, streams["dve"])

            @block.gpsimd
            def _(eng):
                replay(eng, streams["pool"])

            @block.sync
            def _(eng):
                replay(eng, streams["sp"])


def host_consts(tmax):
    t = np.arange(tmax)
    row = (t // 64).astype(np.float32)
    col = (t % 64).astype(np.float32)

    def tabs(n):
        inv = (10000.0 ** (-np.arange(n, dtype=np.float32) / n)).astype(np.float32)
        out = []
        for pos in (row, col):
            ang = pos[:, None].astype(np.float32) * inv[None, :]
            c = np.cos(ang).astype(np.float32)
            s = np.sin(ang).astype(np.float32)
            out.append((np.concatenate([c, c], 1), np.concatenate([-s, s], 1)))
        C = np.concatenate([out[0][0], out[1][0]], 1)
        S = np.concatenate([out[0][1], out[1][1]], 1)
        return C, S

    C64, S64 = tabs(16)
    C32, S32 = tabs(8)
    tab = np.concatenate([C64, S64, C32, S32], 1).astype(np.float32)
    j = np.arange(128)
    uf = (j[:, None] <= j[None, :]).astype(np.float32)
    ub = (j[:, None] >= j[None, :]).astype(np.float32)
    negf = np.where(uf > 0, 0.0, -30000.0).astype(np.float32)
    negb = np.where(ub > 0, 0.0, -30000.0).astype(np.float32)
    bm4 = np.zeros((128, 4), np.float32)
    for h in range(4):
        bm4[h * 32:(h + 1) * 32, h] = 1.0
    bmbig = np.repeat(bm4, 64, axis=1)
    eye = np.eye(128, dtype=np.float32)
    cm = np.concatenate([
        uf, ub,
        ub - eye, uf - eye,
        uf * (-1.0 / 16), ub * (-1.0 / 16),
        eye,
        bm4,
        bmbig,
        np.full((128, 4), -1.0 / 16, np.float32),
    ], 1).astype(np.float32)
    return tab, cm


CM_COLS = 1160


def build(seqs, depth, debug=()):
    nc = bass.Bass("TRN2", target_bir_lowering=False)
    nseq = len(seqs)
    offs = [0]
    for s in seqs:
        offs.append(offs[-1] + s)
    TT = offs[-1]
    nchunks = TT // 128
    L = depth

    def dram(name, shape, dt, kind=None):
        if kind is None:
            kind = "ExternalOutput" if name in debug else "Internal"
        return nc.dram_tensor(name, list(shape), dt, kind=kind).ap()

    x_in = dram("x", [TT, DM], F32, "ExternalInput")
    Wd = {}
    shp = {'norm1_g': [L, DM], 'w_in': [L, DM, DIN], 'ssd_conv_w': [L, 5, 768], 'ssd_conv_b': [L, 768],
           'ssd_dt_bias': [L, 8], 'ssd_a_log': [L, 8], 'ssd_d': [L, 4], 'ssd_norm_g': [L, 256],
           'gqa_q_norm_g': [L, 64], 'gqa_k_norm_g': [L, 64], 'gla_gate_w2': [L, 2, 16, 128],
           'gla_gate_b': [L, 256], 'gla_norm_g': [L, 64], 'mla_q_norm_g': [L, 256], 'mla_w_uq': [L, 256, 384],
           'mla_kv_norm_g': [L, 128], 'mla_w_ukv': [L, 128, 512], 'w_out': [L, DM, DM], 'norm2_g': [L, DM],
           'w_ff1': [L, DM, DFF], 'w_ff2': [L, DFF, DM], 'final_norm_g': [DM]}
    for k in PNAMES:
        Wd[k] = dram(k, shp[k], F32, "ExternalInput")
    tab_d = dram("tab", [max(seqs), 192], F32, "ExternalInput")
    cm_d = dram("cm", [128, CM_COLS], F32, "ExternalInput")
    y_out = dram("y", [TT, DM], F32, "ExternalOutput")

    TP = TT + 4 * nseq
    xbcpre = dram("xbcpre", [6, 128, TP], F32)
    QTg = dram("QTg", [4, 64, TT], BF16)
    KTg = dram("KTg", [128, TT], BF16)
    VAg = dram("VAg", [128, nchunks, 130], BF16)
    QTm = dram("QTm", [4, 96, TT], BF16)
    KTm = dram("KTm", [4, 96, TT], BF16)
    VAm = dram("VAm", [2, 128, nchunks, 130], BF16)
    CTd = dram("CTd", [2, 128, TT], BF16)
    qgTd = dram("qgTd", [2, 128, TT], BF16)
    eacs_d = dram("eacs", [128, nchunks, 8], F32)
    yo_d = dram("yo", [TT, 512], F32)
    zr_d = dram("zr", [TT, 512], F32)
    dtr_d = dram("dtr", [128, nchunks, 8], F32)
    stssd_d = dram("stssd", [nchunks, 128, 512], F32)
    stgla_d = dram("stgla", [nchunks, 128, 512], F32)
    mixT_d = dram("mixT", [8, 128, TT], BF16)
    xres_d = dram("xres", [TT, DM], F32)
    w1b_d = dram("w1b", [L, DM, DFF], BF16)

    with contextlib.ExitStack() as stack:
        fw = FW(nc, stack)
        SB_BYTES = 212800
        big = stack.enter_context(nc.sbuf_tensor("big", [128, SB_BYTES], U8))
        banks = [T(stack.enter_context(nc.psum_tensor("bank%d" % i, [128, 512], F32))[:]) for i in range(8)]
        bump = [0]

        def alloc(shape, dt, nb=1):
            esz = 4 if dt == F32 else 2
            n = int(np.prod(shape[1:])) * esz
            n = (n + 31) // 32 * 32
            res = []
            for _ in range(nb):
                off = bump[0]
                bump[0] += n
                assert bump[0] <= SB_BYTES, "SBUF overflow %d" % bump[0]
                ap = big[:, off:off + int(np.prod(shape[1:])) * esz].bitcast(dt)
                if len(shape) == 3:
                    ap = ap.rearrange("p (a b) -> p a b", a=shape[1])
                elif len(shape) == 4:
                    ap = ap.rearrange("p (a b c) -> p a b c", a=shape[1], b=shape[2])
                res.append(T(ap))
            return res[0] if nb == 1 else res

        def tt(eng, out, in0, in1, op, r, w):
            fw.op(eng, lambda e: e.tensor_tensor(out=out, in0=in0, in1=in1, op=op), r, w)

        def stt(eng, out, in0, scalar, in1, op0, op1, r, w):
            fw.op(eng, lambda e: e.scalar_tensor_tensor(out=out, in0=in0, scalar=scalar, in1=in1, op0=op0, op1=op1), r, w)

        def ts(eng, out, in0, s1, s2, op0, op1, r, w):
            if s2 is None:
                fw.op(eng, lambda e: e.tensor_scalar(out=out, in0=in0, scalar1=s1, scalar2=None, op0=op0), r, w)
            else:
                fw.op(eng, lambda e: e.tensor_scalar(out=out, in0=in0, scalar1=s1, scalar2=s2, op0=op0, op1=op1), r, w)

        def act(out, in_, func, r, w, bias=None, scale=None, accum=None):
            kw = {}
            if bias is not None:
                kw["bias"] = bias
            if scale is not None:
                kw["scale"] = scale
            if accum is not None:
                kw["accum_out"] = accum
            fw.op("act", lambda e: e.activation(out=out, in_=in_, func=func, **kw), r, w)

        def cp(eng, out, in_, r, w):
            if eng == "act":
                fw.op("act", lambda e: e.activation(out=out, in_=in_, func=AF.Copy), r, w)
            else:
                fw.op(eng, lambda e: e.tensor_copy(out=out, in_=in_), r, w)

        def red(out, in_, r, w):
            fw.op("dve", lambda e: e.tensor_reduce(out=out, in_=in_, axis=AX.X, op=ALU.add), r, w)

        def mm(out, lhsT, rhs, start, stop, r, w):
            fw.op("pe", lambda e: e.matmul(out, lhsT=lhsT, rhs=rhs, start=start, stop=stop), r, w)

        def tr(out, in_, ident, r, w):
            fw.op("pe", lambda e: e.transpose(out=out, in_=in_, identity=ident), r, w)

        def dma(q, out, in_, r, w, slow=False):
            w = [x for x in w if x is not dscr]
            if slow:
                fw.op(q, lambda e: e.dma_start(out=out, in_=in_, allow_slow_non_contiguous=True), r, w)
            else:
                fw.op(q, lambda e: e.dma_start(out=out, in_=in_), r, w)

        dscr = T(None)

        def rstd_from_ssq(ssq_ap, n, out_ap, r, w, tmp):
            act(tmp.a, ssq_ap, AF.Ln, r, [tmp], bias=epsb.a[:, 0:1], scale=1.0 / n)
            act(out_ap, tmp.a, AF.Exp, [tmp], w, scale=-0.5)

        cm = alloc([128, CM_COLS], F32)
        UF = cm.a[:, 0:128]
        UB = cm.a[:, 128:256]
        SU = [cm.a[:, 256:384], cm.a[:, 384:512]]
        UFS = cm.a[:, 512:640]
        UBS = cm.a[:, 640:768]
        IDF = cm.a[:, 768:896]
        BM4 = cm.a[:, 896:900]
        BMBIG = cm.a[:, 900:1156]
        CNEG = cm.a[:, 1156:1160]
        idb = alloc([128, 128], BF16)
        onesf = alloc([128, 128], F32)
        epsb = alloc([128, 2], F32)
        egl = alloc([128, nchunks, 2], F32)
        cdec = alloc([128, nchunks, 8], F32)
        g1bc = alloc([128, DM], F32)
        g2bc = alloc([128, DM], F32)
        gqk = alloc([128, 6, 64], F32)
        gcq = alloc([128, 256], F32)
        gckv = alloc([128, 128], F32)
        gateb = alloc([128, 256], F32)
        gssd = alloc([128, 256], F32)
        ggla = alloc([128, 64], F32)
        dtb = alloc([128, 8], F32)
        abc = alloc([128, 8], F32)
        dbc = alloc([128, 4], F32)
        cw = alloc([128, 6, 5], F32)
        cb = alloc([128, 6], F32)
        w2blk = alloc([128, 256], F32)
        wuq = alloc([128, 2, 384], BF16)
        wukv = alloc([128, 512], BF16)
        persist_mark = bump[0]

        dma("sp", cm.a, cm_d, [], [cm])
        cp("dve", idb.a, IDF, [cm], [idb])
        fw.op("dve", lambda e: e.memset(onesf.a, 1.0), [], [onesf])
        fw.op("dve", lambda e: e.memset(epsb.a, EPS), [], [epsb])

        ztile = alloc([128, 6, 4], F32)
        fw.op("dve", lambda e: e.memset(ztile.a, 0.0), [], [ztile])
        for s in range(nseq):
            c0 = offs[s] + 4 * s
            dma("qpool", xbcpre[:, :, c0:c0 + 2].rearrange("c p t -> p c t"), ztile.a[:, :, 0:2], [ztile], [dscr])
            c1 = c0 + 2 + seqs[s]
            dma("qpool", xbcpre[:, :, c1:c1 + 2].rearrange("c p t -> p c t"), ztile.a[:, :, 2:4], [ztile], [dscr])

        wst = alloc([128, 4096], F32, nb=2)
        wsb = alloc([128, 4096], BF16, nb=2)
        i = 0
        for l in range(L):
            for c in range(8):
                a, b2 = wst[i % 2], wsb[i % 2]
                dma("sp", a.a, Wd['w_ff1'][l, c * 128:(c + 1) * 128, :], [], [a])
                if i % 2 == 0:
                    cp("dve", b2.a, a.a, [a], [b2])
                else:
                    cp("act", b2.a, a.a, [a], [b2])
                dma("qpool", w1b_d[l, c * 128:(c + 1) * 128, :], b2.a, [b2], [dscr])
                i += 1
        fw.barrier()
        bump[0] = persist_mark

        supers = []
        for s in range(nseq):
            for t0 in range(0, seqs[s], 512):
                supers.append((s, t0, offs[s] + t0))

        for l in range(L):
            xsrc = x_in if l == 0 else xres_d
            last = (l == L - 1)
            tmpw = alloc([128, 1024], F32)
            dma("sp", g1bc.a, Wd['norm1_g'][l].partition_broadcast(128), [], [g1bc])
            dma("sp", g2bc.a, Wd['norm2_g'][l].partition_broadcast(128), [], [g2bc])
            for h in range(4):
                dma("sp", gqk.a[:, h, :], Wd['gqa_q_norm_g'][l].partition_broadcast(128), [], [gqk])
            for h in range(2):
                dma("sp", gqk.a[:, 4 + h, :], Wd['gqa_k_norm_g'][l].partition_broadcast(128), [], [gqk])
            dma("sp", gcq.a, Wd['mla_q_norm_g'][l].partition_broadcast(128), [], [gcq])
            dma("sp", gckv.a, Wd['mla_kv_norm_g'][l].partition_broadcast(128), [], [gckv])
            dma("sp", gateb.a, Wd['gla_gate_b'][l].partition_broadcast(128), [], [gateb])
            dma("sp", gssd.a, Wd['ssd_norm_g'][l].partition_broadcast(128), [], [gssd])
            dma("sp", ggla.a, Wd['gla_norm_g'][l].partition_broadcast(128), [], [ggla])
            dma("sp", dtb.a, Wd['ssd_dt_bias'][l].partition_broadcast(128), [], [dtb])
            dma("sp", abc.a, Wd['ssd_a_log'][l].partition_broadcast(128), [], [abc])
            dma("sp", dbc.a, Wd['ssd_d'][l].partition_broadcast(128), [], [dbc])
            act(abc.a, abc.a, AF.Exp, [abc], [abc])
            ts("dve", abc.a, abc.a, -1.0, None, ALU.mult, None, [abc], [abc])
            for k in range(5):
                dma("sp", cw.a[:, :, k], Wd['ssd_conv_w'][l, k].rearrange("(c p) -> p c", p=128), [], [cw], slow=True)
            dma("sp", cb.a, Wd['ssd_conv_b'][l].rearrange("(c p) -> p c", p=128), [], [cb], slow=True)
            fw.op("dve", lambda e: e.memset(w2blk.a, 0.0), [], [w2blk])
            dma("sp", w2blk.a[0:16, 0:128], Wd['gla_gate_w2'][l, 0], [], [w2blk])
            dma("sp", w2blk.a[16:32, 128:256], Wd['gla_gate_w2'][l, 1], [], [w2blk])
            for c in range(2):
                dma("sp", tmpw.a[:, 0:384], Wd['mla_w_uq'][l, c * 128:(c + 1) * 128, :], [], [tmpw])
                cp("dve", wuq.a[:, c, :], tmpw.a[:, 0:384], [tmpw], [wuq])
            dma("sp", tmpw.a[:, 0:512], Wd['mla_w_ukv'][l], [], [tmpw])
            cp("dve", wukv.a, tmpw.a[:, 0:512], [tmpw], [wukv])
            fw.barrier()
            bump[0] = persist_mark

            win = alloc([128, 8, DIN], BF16)
            a1_mark = bump[0]
            wstage = alloc([128, DIN], F32, nb=2)
            for c in range(8):
                wsx = wstage[c % 2]
                dma("sp", wsx.a, Wd['w_in'][l, c * 128:(c + 1) * 128, :], [], [wsx])
                eng = "dve" if c % 2 == 0 else "act"
                cp(eng, win.a[:, c, 0:1728], wsx.a[:, 1032:2760], [wsx], [win])
                cp(eng, win.a[:, c, 1728:1984], wsx.a[:, 0:256], [wsx], [win])
                cp(eng, win.a[:, c, 1984:1992], wsx.a[:, 1024:1032], [wsx], [win])
                cp(eng, win.a[:, c, 1992:2760], wsx.a[:, 256:1024], [wsx], [win])
            fw.barrier()
            bump[0] = a1_mark
            xt = alloc([128, DM], F32, nb=2)
            junk = alloc([128, DM], BF16)
            hb = alloc([128, DM], BF16, nb=2)
            hT = alloc([128, 8, 512], BF16, nb=2)
            ptok = alloc([128, NTM], F32, nb=3)
            tabt = alloc([128, 192], F32, nb=3)
            sm = alloc([128, 16], F32, nb=3)
            sm2 = alloc([128, 16], F32, nb=3)
            tA = alloc([128, 6, 64], F32)
            tB = alloc([128, 6, 64], F32)
            tC = alloc([128, 6, 64], F32)
            tD = alloc([128, 6, 64], F32)
            qrb_r = alloc([128, 256], BF16, nb=2)
            krb_r = alloc([128, 128], BF16, nb=2)
            QA_r = alloc([128, 512], BF16, nb=2)
            QB_r = alloc([128, 512], BF16, nb=2)
            KTs_r = alloc([128, 512], BF16, nb=2)
            vaug_r = alloc([128, 4, 130], BF16, nb=2)
            vaugm_r = alloc([128, 2, 4, 130], BF16, nb=2)
            cqn_r = alloc([128, 384], BF16, nb=2)
            cT_r = alloc([128, 3, 128], BF16, nb=2)
            mt1 = alloc([128, 5, 32], F32)
            mt2 = alloc([128, 5, 32], F32)
            qmb_r = alloc([128, 4, 96], BF16, nb=2)
            kmb_r = alloc([128, 4, 96], BF16, nb=2)
            QM_r = alloc([128, 4, 512], BF16, nb=1); QM_r = [QM_r, QM_r]
            KM_r = alloc([128, 4, 512], BF16, nb=1); KM_r = [KM_r, KM_r]
            lrT_r = alloc([128, 128], F32, nb=2)
            lgb = alloc([128, 256], F32)
            lsp_r = alloc([128, 256], F32, nb=2)
            Eq_r = alloc([128, 256], F32, nb=2)
            Ek_r = alloc([128, 256], F32, nb=2)
            qg_r = alloc([128, 2, 128], BF16, nb=2)
            kg_r = alloc([128, 2, 128], BF16, nb=2)
            lvb_r = alloc([128, 256], BF16, nb=2)
            qkT_r = alloc([128, 4, 128], BF16, nb=2)
            qgTs_r = alloc([128, 2, 512], BF16, nb=2)
            Qblk_r = alloc([128, 2, 512], BF16, nb=2)
            attm_r = alloc([128, 2, 512], BF16, nb=2)
            yos_r = alloc([128, 4, 256], F32, nb=2)
            stg = alloc([128, 512], F32, nb=2)
            xbst = alloc([128, 3, 512], F32)
            for v_ in vaug_r + vaugm_r:
                fw.op("dve", lambda e, v_=v_: e.memset(v_.a, 1.0), [], [v_])
            pT = banks[0]
            pTb = pT.a.bitcast(BF16)
            pgr = banks[1:3]
            pgi = [0]
            pgc = [0]
            mi = [0]
            lane_banks = {"s2a": banks[3:4], "s2b": banks[4:6], "s3": banks[6:8]}
            lane_cnt = {"s2a": 0, "s2b": 0, "s3": 0}
            cur_lane = ["s2a"]
            smb = alloc([128, 8], F32, nb=3)

            def mbank():
                ln = cur_lane[0]
                bl = lane_banks[ln]
                b = bl[lane_cnt[ln] % len(bl)]
                lane_cnt[ln] += 1
                return b

            def pgbank():
                b = pgr[pgi[0] % 2]
                pgi[0] += 1
                return b

            GRP = [(0, 512), (512, 512), (1024, 512), (1536, 456)]
            ctxs = []
            for sj, (s, t0, g0) in enumerate(supers):
                for u in range(4):
                    ctxs.append(dict(s=s, t0=t0, g0=g0, u=u, sj=sj, k=len(ctxs)))

            def S1(cx):
                s, t0, g0, u, sj, k = cx['s'], cx['t0'], cx['g0'], cx['u'], cx['sj'], cx['k']
                hTs = hT[sj % 2]
                gt = g0 + u * 128
                tl = t0 + u * 128
                X = xt[k % 2]
                P = ptok[k % 3]
                TB = tabt[k % 3]
                S1_ = sm[k % 3]
                S2_ = sm2[k % 3]
                HB = hb[k % 2]
                dma("sp", X.a, xsrc[gt:gt + 128, :], [], [X])
                dma("sp", TB.a, tab_d[tl:tl + 128, :], [], [TB])
                act(junk.a, X.a, AF.Square, [X], [junk, S1_], accum=S1_.a[:, 0:1])
                rstd_from_ssq(S1_.a[:, 0:1], DM, S1_.a[:, 1:2], [S1_], [S1_], T(S2_.a[:, 0:1], S2_.b))
                stt("dve", HB.a, X.a, S1_.a[:, 1:2], g1bc.a, ALU.mult, ALU.mult, [X, S1_, g1bc], [HB])
                for c in range(8):
                    tr(pTb[:, c * 128:(c + 1) * 128], HB.a[:, c * 128:(c + 1) * 128], idb.a, [HB, idb], [pT])
                cp("act", hTs.a[:, :, u * 128:(u + 1) * 128], pTb.rearrange("p (c t) -> p c t", c=8), [pT], [hTs])
                pend_ev = []

                def evac():
                    gi, (c0, n), pb = pend_ev.pop(0)
                    cp("act" if gi % 2 == 0 else "dve", P.a[:, c0:c0 + n], pb.a[:, 0:n], [pb], [P])

                for gi, (c0, n) in enumerate(GRP):
                    pb = pgbank()
                    for c in range(8):
                        mm(pb.a[:, 0:n], hTs.a[:, c, u * 128:(u + 1) * 128], win.a[:, c, c0:c0 + n],
                           c == 0, c == 7, [hTs, win], [pb])
                    pend_ev.append((gi, (c0, n), pb))
                    if len(pend_ev) > 1:
                        evac()
                while pend_ev:
                    evac()
                dma("qpool", zr_d[gt:gt + 128, 0:256], P.a[:, 1728:1984], [P], [dscr])
                dma("qpool", zr_d[gt:gt + 128, 256:512], P.a[:, 1024:1280], [P], [dscr])
                dma("qpool", dtr_d[:, gt // 128, :], P.a[:, 1984:1992], [P], [dscr])

            def S1c(cx):
                s, t0, g0, u, sj, k = cx['s'], cx['t0'], cx['g0'], cx['u'], cx['sj'], cx['k']
                hTs = hT[sj % 2]
                if u == 3:
                    for fc in range(6):
                        mb = pgbank()
                        for c in range(8):
                            mm(mb.a[:, 0:512], win.a[:, c, 1992 + fc * 128:1992 + (fc + 1) * 128], hTs.a[:, c, :],
                               c == 0, c == 7, [win, hTs], [mb])
                        cp("act" if fc % 2 == 0 else "dve", xbst.a[:, fc % 3, :], mb.a[:, 0:512], [mb], [xbst])
                        if fc % 3 == 2:
                            col0 = g0 + 4 * s + 2
                            dma("qpool", xbcpre[fc - 2:fc + 1, :, col0:col0 + 512].rearrange("c p t -> p c t"), xbst.a, [xbst], [dscr])

            def S2(cx):
                s, t0, g0, u, sj, k = cx['s'], cx['t0'], cx['g0'], cx['u'], cx['sj'], cx['k']
                P = ptok[k % 3]
                TB = tabt[k % 3]
                S1_ = sm[k % 3]
                S2_ = sm2[k % 3]
                qrb, krb, cqn, cT = qrb_r[k % 2], krb_r[k % 2], cqn_r[k % 2], cT_r[k % 2]
                qmb, kmb = qmb_r[k % 2], kmb_r[k % 2]
                QA, QB, KTs, vaug, vaugm, QM, KM = (QA_r[sj % 2], QB_r[sj % 2], KTs_r[sj % 2], vaug_r[sj % 2],
                                                    vaugm_r[sj % 2], QM_r[sj % 2], KM_r[sj % 2])
                qk = P.a[:, 0:384].rearrange("p (h d) -> p h d", h=6)
                act(tA.a, qk, AF.Square, [P], [tA])
                red(S1_.a[:, 2:8], tA.a, [tA], [S1_])
                rstd_from_ssq(S1_.a[:, 2:8], 64, S1_.a[:, 8:14], [S1_], [S1_], T(S2_.a[:, 2:8], S2_.b))
                tt("dve", tB.a, qk, S1_.a[:, 8:14].unsqueeze(2).to_broadcast([128, 6, 64]), ALU.mult, [P, S1_], [tB])
                tt("dve", tB.a, tB.a, gqk.a, ALU.mult, [tB, gqk], [tB])
                C64 = TB.a[:, 0:64].unsqueeze(1).to_broadcast([128, 6, 64])
                tt("dve", tC.a, tB.a, C64, ALU.mult, [tB, TB], [tC])
                tBv = tB.a.rearrange("p h (b f d) -> p h b f d", b=2, f=2)
                tDv = tD.a.rearrange("p h (b f d) -> p h b f d", b=2, f=2)
                S64v = TB.a[:, 64:128].rearrange("p (b f d) -> p b f d", b=2, f=2)
                for f in range(2):
                    tt("dve", tDv[:, :, :, f, :], tBv[:, :, :, 1 - f, :],
                       S64v[:, :, f, :].unsqueeze(1).to_broadcast([128, 6, 2, 16]), ALU.mult, [tB, TB], [tD])
                tt("dve", qrb.a.rearrange("p (ha hb d) -> p hb ha d", ha=2, hb=2),
                   tC.a[:, 0:4, :].rearrange("p (hb ha) d -> p hb ha d", ha=2),
                   tD.a[:, 0:4, :].rearrange("p (hb ha) d -> p hb ha d", ha=2), ALU.add, [tC, tD], [qrb])
                tt("dve", krb.a.rearrange("p (h d) -> p h d", h=2), tC.a[:, 4:6, :], tD.a[:, 4:6, :], ALU.add,
                   [tC, tD], [krb])
                cp("act", vaug.a[:, u, :].rearrange("p (k e) -> p k e", k=2)[:, :, 0:64],
                   P.a[:, 384:512].rearrange("p (k e) -> p k e", k=2), [P], [vaug])
                mb = mbank()
                mbb = mb.a.bitcast(BF16)
                tr(mbb[:, 0:128], qrb.a[:, 0:128], idb.a, [qrb, idb], [mb])
                tr(mbb[:, 128:256], qrb.a[:, 128:256], idb.a, [qrb, idb], [mb])
                tr(mbb[:, 256:384], krb.a, idb.a, [krb, idb], [mb])
                cp("act", QA.a[:, u * 128:(u + 1) * 128], mbb[:, 0:128], [mb], [QA])
                cp("act", QB.a[:, u * 128:(u + 1) * 128], mbb[:, 128:256], [mb], [QB])
                cp("act", KTs.a[:, u * 128:(u + 1) * 128], mbb[:, 256:384], [mb], [KTs])
                if u == 3:
                    dma("qpool", QTg[0, :, g0:g0 + 512], QA.a[0:64, :], [QA], [dscr])
                    dma("qpool", QTg[2, :, g0:g0 + 512], QA.a[64:128, :], [QA], [dscr])
                    dma("qpool", QTg[1, :, g0:g0 + 512], QB.a[0:64, :], [QB], [dscr])
                    dma("qpool", QTg[3, :, g0:g0 + 512], QB.a[64:128, :], [QB], [dscr])
                    dma("qpool", KTg[:, g0:g0 + 512], KTs.a, [KTs], [dscr])
                    dma("qpool", VAg[:, g0 // 128:g0 // 128 + 4, :], vaug.a, [vaug], [dscr])

            def S2b(cx):
                s, t0, g0, u, sj, k = cx['s'], cx['t0'], cx['g0'], cx['u'], cx['sj'], cx['k']
                P = ptok[k % 3]
                TB = tabt[k % 3]
                SB_ = smb[k % 3]
                cqn, cT = cqn_r[k % 2], cT_r[k % 2]
                qmb, kmb = qmb_r[k % 2], kmb_r[k % 2]
                vaugm, QM, KM = vaugm_r[sj % 2], QM_r[sj % 2], KM_r[sj % 2]
                act(junk.a[:, 0:256], P.a[:, 1312:1568], AF.Square, [P], [junk, SB_], accum=SB_.a[:, 0:1])
                act(junk.a[:, 256:384], P.a[:, 1568:1696], AF.Square, [P], [junk, SB_], accum=SB_.a[:, 1:2])
                rstd_from_ssq(SB_.a[:, 0:1], 256, SB_.a[:, 4:5], [SB_], [SB_], T(SB_.a[:, 2:3], SB_.b))
                rstd_from_ssq(SB_.a[:, 1:2], 128, SB_.a[:, 5:6], [SB_], [SB_], T(SB_.a[:, 3:4], SB_.b))
                stt("dve", cqn.a[:, 0:256], P.a[:, 1312:1568], SB_.a[:, 4:5], gcq.a, ALU.mult, ALU.mult, [P, SB_, gcq], [cqn])
                stt("dve", cqn.a[:, 256:384], P.a[:, 1568:1696], SB_.a[:, 5:6], gckv.a, ALU.mult, ALU.mult, [P, SB_, gckv], [cqn])
                mb = mbank()
                mbb = mb.a.bitcast(BF16)
                for c in range(3):
                    tr(mbb[:, c * 128:(c + 1) * 128], cqn.a[:, c * 128:(c + 1) * 128], idb.a, [cqn, idb], [mb])
                cp("act", cT.a.rearrange("p c t -> p (c t)"), mbb[:, 0:384], [mb], [cT])
                mq = mbank()
                for c in range(2):
                    mm(mq.a[:, 0:384], cT.a[:, c, :], wuq.a[:, c, :], c == 0, c == 1, [cT, wuq], [mq])
                mkv = mbank()
                mm(mkv.a[:, 0:512], cT.a[:, 2, :], wukv.a, True, True, [cT, wukv], [mkv])
                mqv = mq.a[:, 0:384].rearrange("p (h d) -> p h d", h=4)
                cp("dve", mt1.a[:, 0:4, :], mqv[:, :, 64:96], [mq], [mt1])
                cp("dve", mt1.a[:, 4, :], P.a[:, 1696:1728], [P], [mt1])
                C32 = TB.a[:, 128:160].unsqueeze(1).to_broadcast([128, 5, 32])
                m1v = mt1.a.rearrange("p h (b f d) -> p h b f d", b=2, f=2)
                m2v = mt2.a.rearrange("p h (b f d) -> p h b f d", b=2, f=2)
                S32v = TB.a[:, 160:192].rearrange("p (b f d) -> p b f d", b=2, f=2)
                for f in range(2):
                    tt("dve", m2v[:, :, :, f, :], m1v[:, :, :, 1 - f, :],
                       S32v[:, :, f, :].unsqueeze(1).to_broadcast([128, 5, 2, 8]), ALU.mult, [mt1, TB], [mt2])
                tt("dve", mt1.a, mt1.a, C32, ALU.mult, [mt1, TB], [mt1])
                tt("dve", qmb.a[:, :, 64:96], mt1.a[:, 0:4, :], mt2.a[:, 0:4, :], ALU.add, [mt1, mt2], [qmb])
                tt("dve", mt1.a[:, 4, :], mt1.a[:, 4, :], mt2.a[:, 4, :], ALU.add, [mt1, mt2], [mt1])
                cp("dve", kmb.a[:, :, 64:96], mt1.a[:, 4:5, :].to_broadcast([128, 4, 32]), [mt1], [kmb])
                cp("act", qmb.a[:, :, 0:64], mqv[:, :, 0:64], [mq], [qmb])
                mkvv = mkv.a[:, 0:512].rearrange("p (h d) -> p h d", h=4)
                cp("act", kmb.a[:, :, 0:64], mkvv[:, :, 0:64], [mkv], [kmb])
                cp("act", vaugm.a[:, :, u, :].rearrange("p a (hh e) -> p a hh e", hh=2)[:, :, :, 0:64],
                   mkvv[:, :, 64:128].rearrange("p (a hh) e -> p a hh e", hh=2), [mkv], [vaugm])
                mb = mbank()
                mbb = mb.a.bitcast(BF16)
                mb2 = mbank()
                mbb2 = mb2.a.bitcast(BF16)
                for h in range(4):
                    tr(mbb[0:96, h * 128:(h + 1) * 128], qmb.a[:, h, :], idb.a, [qmb, idb], [mb])
                    tr(mbb2[0:96, h * 128:(h + 1) * 128], kmb.a[:, h, :], idb.a, [kmb, idb], [mb2])
                cp("act", QM.a[0:96, :, u * 128:(u + 1) * 128], mbb[0:96, 0:512].rearrange("p (h t) -> p h t", h=4), [mb], [QM])
                cp("act", KM.a[0:96, :, u * 128:(u + 1) * 128], mbb2[0:96, 0:512].rearrange("p (h t) -> p h t", h=4), [mb2], [KM])
                if u == 3:
                    for a_ in range(2):
                        dma("qpool", VAm[a_, :, g0 // 128:g0 // 128 + 4, :], vaugm.a[:, a_, :, :], [vaugm], [dscr])
                    dma("qpool", QTm[:, :, g0:g0 + 512].rearrange("h p t -> p h t"), QM.a[0:96], [QM], [dscr])
                    dma("qpool", KTm[:, :, g0:g0 + 512].rearrange("h p t -> p h t"), KM.a[0:96], [KM], [dscr])

            def S3(cx):
                s, t0, g0, u, sj, k = cx['s'], cx['t0'], cx['g0'], cx['u'], cx['sj'], cx['k']
                gt = g0 + u * 128
                ch = gt // 128
                P = ptok[k % 3]
                lrT, lsp, Eq, Ek = lrT_r[k % 2], lsp_r[k % 2], Eq_r[k % 2], Ek_r[k % 2]
                qg, kg, lvb, qkT, Qblk, attm = qg_r[k % 2], kg_r[k % 2], lvb_r[k % 2], qkT_r[k % 2], Qblk_r[k % 2], attm_r[k % 2]
                qgTs, yos = qgTs_r[sj % 2], yos_r[sj % 2]
                mb = mbank()
                tr(mb.a[0:32, 0:128], P.a[:, 1280:1312], IDF, [P, cm], [mb])
                cp("dve", lrT.a[0:32, :], mb.a[0:32, 0:128], [mb], [lrT])
                mb = mbank()
                mm(mb.a[:, 0:256], lrT.a[0:32, :], w2blk.a[0:32, :], True, True, [lrT, w2blk], [mb])
                tt("dve", lgb.a, mb.a[:, 0:256], gateb.a, ALU.add, [mb, gateb], [lgb])
                act(lgb.a, lgb.a, AF.Exp, [lgb], [lgb], scale=-1.0)
                act(lsp.a, lgb.a, AF.Ln, [lgb], [lsp], bias=onesf.a[:, 0:1])
                mb = mbank()
                mm(mb.a[:, 0:128], UFS, lsp.a[:, 0:128], True, True, [cm, lsp], [mb])
                mm(mb.a[:, 128:256], UBS, lsp.a[:, 128:256], True, True, [cm, lsp], [mb])
                mm(mb.a[:, 256:257], lsp.a[:, 0:128], CNEG[:, 0:1], True, True, [cm, lsp], [mb])
                mm(mb.a[:, 257:258], lsp.a[:, 128:256], CNEG[:, 0:1], True, True, [cm, lsp], [mb])
                act(Eq.a, mb.a[:, 0:256], AF.Exp, [mb], [Eq])
                act(Ek.a, mb.a[:, 0:256], AF.Exp, [mb], [Ek], scale=-1.0)
                act(egl.a[:, ch, :], mb.a[:, 256:258], AF.Exp, [mb], [egl])
                lq = P.a[:, 512:640].unsqueeze(1).to_broadcast([128, 2, 128])
                lk = P.a[:, 640:768].unsqueeze(1).to_broadcast([128, 2, 128])
                stt("dve", qg.a, lq, 32 ** -0.5, Eq.a.rearrange("p (d k) -> p d k", d=2), ALU.mult, ALU.mult, [P, Eq], [qg])
                tt("dve", kg.a, lk, Ek.a.rearrange("p (d k) -> p d k", d=2), ALU.mult, [P, Ek], [kg])
                cp("act", lvb.a, P.a[:, 768:1024], [P], [lvb])
                mb = mbank()
                mbb = mb.a.bitcast(BF16)
                for d in range(2):
                    tr(mbb[:, d * 128:(d + 1) * 128], qg.a[:, d, :], idb.a, [qg, idb], [mb])
                    tr(mbb[:, (2 + d) * 128:(3 + d) * 128], kg.a[:, d, :], idb.a, [kg, idb], [mb])
                cp("act", qkT.a.rearrange("p c t -> p (c t)"), mbb[:, 0:512], [mb], [qkT])
                cp("dve", qgTs.a[:, :, u * 128:(u + 1) * 128], qkT.a[:, 0:2, :], [qkT], [qgTs])
                for d in range(2):
                    tt("dve", Qblk.a[:, d, :].rearrange("p (h l) -> p h l", h=4),
                       qkT.a[:, d:d + 1, :].to_broadcast([128, 4, 128]),
                       BM4.unsqueeze(2).to_broadcast([128, 4, 128]), ALU.mult, [qkT, cm], [Qblk])
                po = mbank()
                for d in range(2):
                    mb = mbank()
                    mm(mb.a[:, 0:512], qkT.a[:, 2 + d, :], Qblk.a[:, d, :], True, True, [qkT, Qblk], [mb])
                    msk = (UF if d == 0 else UB).unsqueeze(1).to_broadcast([128, 4, 128])
                    tt("dve", attm.a[:, d, :].rearrange("p (h l) -> p h l", h=4),
                       mb.a[:, 0:512].rearrange("p (h l) -> p h l", h=4), msk, ALU.mult, [mb, cm], [attm])
                for h in range(4):
                    for d in range(2):
                        mm(po.a[:, h * 64:(h + 1) * 64], attm.a[:, d, h * 128:(h + 1) * 128],
                           lvb.a[:, h * 64:(h + 1) * 64], d == 0, d == 1, [attm, lvb], [po])
                cp("act", yos.a[:, u, :], po.a[:, 0:256], [po], [yos])
                mb = mbank()
                for d in range(2):
                    mm(mb.a[:, d * 256:(d + 1) * 256], kg.a[:, d, :], lvb.a, True, True, [kg, lvb], [mb])
                SG = stg[ch % 2]
                for d in range(2):
                    stt("dve", SG.a[:, d * 256:(d + 1) * 256], mb.a[:, d * 256:(d + 1) * 256], egl.a[:, ch, d:d + 1],
                        BMBIG, ALU.mult, ALU.mult, [mb, egl, cm], [SG])
                dma("qpool", stgla_d[ch], SG.a, [SG], [dscr])
                if u == 3:
                    dma("qpool", qgTd[:, :, g0:g0 + 512].rearrange("d p t -> p d t"), qgTs.a, [qgTs], [dscr])
                    dma("qpool", yo_d[g0:g0 + 512, 256:512].rearrange("(u p) c -> p u c", p=128), yos.a, [yos], [dscr])

            NS = len(ctxs)

            def cap_lane(name, fn, cx):
                cur_lane[0] = name
                return fw.capture(fn, cx)

            for k in range(NS + 2):
                lanes = []
                def s1_lane(k=k):
                    if k < NS:
                        S1(ctxs[k])
                    if 0 <= k - 1 < NS and ctxs[k - 1]['u'] == 3:
                        S1c(ctxs[k - 1])
                lanes.append(fw.capture(s1_lane))
                if 0 <= k - 1 < NS:
                    lanes.append(cap_lane("s2a", S2, ctxs[k - 1]))
                    lanes.append(cap_lane("s2b", S2b, ctxs[k - 1]))
                if 0 <= k - 2 < NS:
                    lanes.append(cap_lane("s3", S3, ctxs[k - 2]))
                fw.merge_emit(lanes)
            fw.barrier()
            bump[0] = persist_mark

            xbT = alloc([128, 6, 516], F32, nb=2)
            acc = alloc([128, 6, 512], F32)
            xcb = alloc([128, 6, 512], BF16, nb=2)
            dtt = alloc([128, 4, 8], F32, nb=2)
            xsB = alloc([128, 512], BF16, nb=3)
            d1 = alloc([128, 32], F32, nb=3)
            d2 = alloc([128, 32], F32, nb=3)
            UD = alloc([128, 512], F32, nb=2)
            Ld = alloc([128, 2, 512], F32)
            scT_r = alloc([128, 512], F32, nb=2)
            Mt_r = alloc([128, 2, 512], BF16, nb=2)
            xdt = alloc([128, 512], BF16)
            xde = alloc([128, 512], BF16)
            ys_r = alloc([128, 4, 256], F32, nb=2)
            ytmp = alloc([128, 256], F32)
            eas_r = alloc([128, 4, 8], F32, nb=2)
            stg = alloc([128, 512], F32, nb=2)
            a2_banks = {"t2": banks[0:3], "t3": banks[3:5], "t4": banks[5:8]}
            a2_cnt = {"t2": 0, "t3": 0, "t4": 0}
            a2_lane = ["t2"]

            def mbank8():
                ln = a2_lane[0]
                bl = a2_banks[ln]
                b = bl[a2_cnt[ln] % len(bl)]
                a2_cnt[ln] += 1
                return b

            def T1(sj):
                s, t0, g0 = supers[sj]
                XB = xbT[sj % 2]
                XC = xcb[sj % 2]
                DT = dtt[sj % 2]
                col0 = g0 + 4 * s
                dma("sp", XB.a, xbcpre[:, :, col0:col0 + 516].rearrange("c p t -> p c t"), [], [XB])
                dma("sp", DT.a, dtr_d[:, g0 // 128:g0 // 128 + 4, :], [], [DT])
                for fc in range(6):
                    ts("dve", acc.a[:, fc, :], XB.a[:, fc, 0:512], cw.a[:, fc, 0:1], None, ALU.mult, None, [XB, cw], [acc])
                    for k in range(1, 5):
                        stt("dve", acc.a[:, fc, :], XB.a[:, fc, k:k + 512], cw.a[:, fc, k:k + 1], acc.a[:, fc, :],
                            ALU.mult, ALU.add, [XB, cw, acc], [acc])
                    act(XC.a[:, fc, :], acc.a[:, fc, :], AF.Silu, [acc, cb], [XC], bias=cb.a[:, fc:fc + 1])
                dma("qpool", CTd[:, :, g0:g0 + 512].rearrange("g p t -> p g t"), XC.a[:, 4:6, :], [XC], [dscr])

            cxs2 = []
            for sj, (s, t0, g0) in enumerate(supers):
                for u in range(4):
                    cxs2.append(dict(sj=sj, g0=g0, u=u, k=len(cxs2)))

            def T2(cx):
                sj, g0, u, k = cx['sj'], cx['g0'], cx['u'], cx['k']
                XC = xcb[sj % 2]
                DT = dtt[sj % 2]
                eas = eas_r[sj % 2]
                gt = g0 + u * 128
                ch = gt // 128
                tsl = slice(u * 128, (u + 1) * 128)
                XS = xsB[k % 3]
                A1 = d1[k % 3]
                A2 = d2[k % 3]
                scT = scT_r[k % 2]
                mb = mbank8()
                mbb = mb.a.bitcast(BF16)
                for c in range(4):
                    tr(mbb[:, c * 128:(c + 1) * 128], XC.a[:, c, tsl], idb.a, [XC, idb], [mb])
                cp("act", XS.a, mbb[:, 0:512], [mb], [XS])
                tt("dve", A1.a[:, 0:8], DT.a[:, u, :], dtb.a, ALU.add, [DT, dtb], [A1])
                act(A1.a[:, 0:8], A1.a[:, 0:8], AF.Exp, [A1], [A1])
                act(A1.a[:, 0:8], A1.a[:, 0:8], AF.Ln, [A1], [A1], bias=onesf.a[:, 0:1])
                tt("dve", A1.a[:, 8:16], A1.a[:, 0:8], abc.a, ALU.mult, [A1, abc], [A1])
                mb = mbank8()
                mm(mb.a[:, 0:4], UF, A1.a[:, 8:12], True, True, [cm, A1], [mb])
                mm(mb.a[:, 4:8], UB, A1.a[:, 12:16], True, True, [cm, A1], [mb])
                mm(mb.a[:, 8:16], onesf.a, A1.a[:, 8:16], True, True, [onesf, A1], [mb])
                cp("dve", A2.a[:, 0:16], mb.a[:, 0:16], [mb], [A2])
                ts("dve", A2.a[:, 16:24], A2.a[:, 0:8], -1.0, None, ALU.mult, None, [A2], [A2])
                act(eas.a[:, u, :], A2.a[:, 0:8], AF.Exp, [A2], [eas])
                tt("dve", A2.a[:, 24:32], A2.a[:, 8:16], A2.a[:, 0:8], ALU.subtract, [A2], [A2])
                act(A2.a[:, 24:32], A2.a[:, 24:32], AF.Exp, [A2], [A2])
                act(cdec.a[:, ch, :], A2.a[:, 8:16], AF.Exp, [A2], [cdec])
                msc = mbank8()
                for g in range(2):
                    mm(msc.a[:, g * 128:(g + 1) * 128], XC.a[:, 2 + g, tsl], XC.a[:, 4 + g, tsl], True, True, [XC], [msc])
                for d in range(2):
                    tt("dve", scT.a[:, d * 256:(d + 1) * 256].rearrange("p (g l) -> p g l", g=2),
                       msc.a[:, 0:256].rearrange("p (g l) -> p g l", g=2),
                       (UF if d == 0 else UB).unsqueeze(1).to_broadcast([128, 2, 128]), ALU.mult, [msc, cm], [scT])
                if u == 3:
                    dma("qpool", eacs_d[:, g0 // 128:g0 // 128 + 4, :], eas.a, [eas], [dscr])

            def T3(cx):
                sj, g0, u, k = cx['sj'], cx['g0'], cx['u'], cx['k']
                A1 = d1[k % 3]
                A2 = d2[k % 3]
                scT = scT_r[k % 2]
                Mt = Mt_r[k % 2]
                for d in range(2):
                    U_ = UD[d]
                    tt("dve", U_.a.rearrange("p (h l) -> p h l", h=4),
                       (UF if d == 0 else UB).unsqueeze(1).to_broadcast([128, 4, 128]),
                       A1.a[:, 8 + 4 * d:12 + 4 * d].unsqueeze(2).to_broadcast([128, 4, 128]), ALU.mult, [cm, A1], [U_])
                    mb = mbank8()
                    mm(mb.a[:, 0:512], SU[d], U_.a, True, True, [cm, U_], [mb])
                    act(Ld.a[:, d, :], mb.a[:, 0:512], AF.Exp, [mb], [Ld])
                    tt("dve", Mt.a[:, d, :].rearrange("p (g hh l) -> p g hh l", g=2, hh=2),
                       Ld.a[:, d, :].rearrange("p (g hh l) -> p g hh l", g=2, hh=2),
                       scT.a[:, d * 256:(d + 1) * 256].rearrange("p (g l) -> p g l", g=2).unsqueeze(2).to_broadcast([128, 2, 2, 128]),
                       ALU.mult, [Ld, scT], [Mt])

            def T4(cx):
                sj, g0, u, k = cx['sj'], cx['g0'], cx['u'], cx['k']
                gt = g0 + u * 128
                ch = gt // 128
                XS = xsB[k % 3]
                A1 = d1[k % 3]
                A2 = d2[k % 3]
                Mt = Mt_r[k % 2]
                ys = ys_r[sj % 2]
                tt("dve", xdt.a.rearrange("p (d h e) -> p d h e", d=2, h=4),
                   XS.a[:, 0:256].rearrange("p (h e) -> p h e", h=4).unsqueeze(1).to_broadcast([128, 2, 4, 64]),
                   A1.a[:, 0:8].rearrange("p (d h) -> p d h", d=2).unsqueeze(3).to_broadcast([128, 2, 4, 64]),
                   ALU.mult, [XS, A1], [xdt])
                tt("dve", xde.a.rearrange("p (d h e) -> p d h e", d=2, h=4),
                   xdt.a.rearrange("p (d h e) -> p d h e", d=2, h=4),
                   A2.a[:, 24:32].rearrange("p (d h) -> p d h", d=2).unsqueeze(3).to_broadcast([128, 2, 4, 64]),
                   ALU.mult, [xdt, A2], [xde])
                py = mbank8()
                for h in range(4):
                    for d in range(2):
                        mm(py.a[:, h * 64:(h + 1) * 64], Mt.a[:, d, h * 128:(h + 1) * 128],
                           xdt.a[:, d * 256 + h * 64:d * 256 + (h + 1) * 64], d == 0, d == 1, [Mt, xdt], [py])
                tt("dve", ytmp.a.rearrange("p (h e) -> p h e", h=4), XS.a[:, 0:256].rearrange("p (h e) -> p h e", h=4),
                   dbc.a.unsqueeze(2).to_broadcast([128, 4, 64]), ALU.mult, [XS, dbc], [ytmp])
                tt("dve", ys.a[:, u, :], py.a[:, 0:256], ytmp.a, ALU.add, [py, ytmp], [ys])
                pst = mbank8()
                xdev = xde.a.rearrange("p (d h e) -> p d h e", d=2, h=4)
                for g in range(2):
                    mm(pst.a[:, g * 256:(g + 1) * 256].rearrange("p (d hh e) -> p d hh e", d=2, hh=2),
                       XS.a[:, 256 + g * 128:256 + (g + 1) * 128], xdev[:, :, 2 * g:2 * g + 2, :], True, True, [XS, xde], [pst])
                SG = stg[ch % 2]
                cp("act", SG.a, pst.a[:, 0:512], [pst], [SG])
                dma("qpool", stssd_d[ch], SG.a, [SG], [dscr])
                if u == 3:
                    dma("qpool", yo_d[g0:g0 + 512, 0:256].rearrange("(u p) c -> p u c", p=128), ys.a, [ys], [dscr])

            NS2 = len(cxs2)
            T1(0)
            t1ops = []
            for k in range(NS2 + 2):
                lanes = []
                if k < NS2:
                    if cxs2[k]['u'] == 0:
                        t1ops = fw.capture(T1, cxs2[k]['sj'] + 1) if cxs2[k]['sj'] + 1 < len(supers) else []
                    uu = cxs2[k]['u']
                    q4 = (len(t1ops) + 3) // 4
                    lanes.append(t1ops[uu * q4:(uu + 1) * q4])
                    a2_lane[0] = "t2"
                    lanes.append(fw.capture(T2, cxs2[k]))
                if 0 <= k - 1 < NS2:
                    a2_lane[0] = "t3"
                    lanes.append(fw.capture(T3, cxs2[k - 1]))
                if 0 <= k - 2 < NS2:
                    a2_lane[0] = "t4"
                    lanes.append(fw.capture(T4, cxs2[k - 2]))
                fw.merge_emit(lanes)
            fw.barrier()
            bump[0] = persist_mark

            Sssd = alloc([128, 512], F32)
            Sssdb = alloc([128, 512], BF16)
            Sgla = alloc([128, 2, 256], F32)
            Sglab = alloc([128, 2, 256], BF16)
            NRB = 3
            accB = alloc([128, 512], F32, nb=NRB)
            eaB = alloc([128, 8], F32, nb=NRB)
            stS = alloc([128, 512], F32, nb=NRB)
            stG = alloc([128, 512], F32, nb=NRB)
            ctB = alloc([128, 2, 128], BF16, nb=NRB)
            qgB = alloc([128, 2, 128], BF16, nb=NRB)
            zrB = alloc([128, 512], F32, nb=NRB)
            szB = alloc([128, 512], F32)
            tmpB = alloc([128, 256], F32)
            tmpB2 = alloc([128, 256], F32)
            smB = alloc([128, 16], F32, nb=2)
            outB = alloc([128, 512], BF16)
            mxs = alloc([128, 4, 128], BF16, nb=2)
            yoB = [Buf() for _ in range(nchunks)]
            bbanks = banks[6:8]
            bbi = [0]

            def bbank():
                b = bbanks[bbi[0] % 2]
                bbi[0] += 1
                return b

            def genB():
                it = 0
                for s in range(nseq):
                    nch = seqs[s] // 128
                    c_base = offs[s] // 128
                    for sweep in range(2):
                        d = 1 - sweep
                        fw.op("dve", lambda e: e.memset(Sssd.a, 0.0), [], [Sssd])
                        fw.op("dve", lambda e: e.memset(Sssdb.a, 0.0), [], [Sssdb])
                        fw.op("dve", lambda e: e.memset(Sgla.a, 0.0), [], [Sgla])
                        fw.op("dve", lambda e: e.memset(Sglab.a, 0.0), [], [Sglab])
                        order = list(range(nch - 1, -1, -1)) if d == 1 else list(range(nch))
                        for oi, cl in enumerate(order):
                            ch = c_base + cl
                            gt = ch * 128
                            AC = accB[it % NRB]
                            EA = eaB[it % NRB]
                            SS = stS[it % NRB]
                            SGt = stG[it % NRB]
                            CT_ = ctB[it % NRB]
                            QG = qgB[it % NRB]
                            ZR = zrB[it % NRB]
                            SM = smB[it % 2]
                            MX = mxs[it % 2]
                            it += 1
                            dma("sp", AC.a, yo_d[gt:gt + 128, :], [yoB[ch]], [AC])
                            dma("sp", EA.a, eacs_d[:, ch, :], [], [EA])
                            dma("sp", SS.a, stssd_d[ch], [], [SS])
                            dma("sp", SGt.a, stgla_d[ch], [], [SGt])
                            dma("sp", CT_.a, CTd[:, :, gt:gt + 128].rearrange("g p t -> p g t"), [], [CT_])
                            dma("sp", QG.a, qgTd[:, :, gt:gt + 128].rearrange("d p t -> p d t"), [], [QG])
                            if sweep == 1:
                                dma("sp", ZR.a, zr_d[gt:gt + 128, :], [], [ZR])
                            Sv = Sssdb.a.rearrange("p (g d hh e) -> p g d hh e", g=2, d=2, hh=2)
                            S5 = Sssd.a.rearrange("p (g d hh e) -> p g d hh e", g=2, d=2, hh=2)
                            ST5 = SS.a.rearrange("p (g d hh e) -> p g d hh e", g=2, d=2, hh=2)
                            if oi > 0:
                                cp("act", Sv[:, :, d, :, :], S5[:, :, d, :, :], [Sssd], [Sssdb])
                                cp("act", Sglab.a[:, d, :], Sgla.a[:, d, :], [Sgla], [Sglab])
                            if sweep == 1:
                                act(szB.a, ZR.a, AF.Exp, [ZR], [szB], scale=-1.0)
                            pr = bbank()
                            for g in range(2):
                                mm(pr.a[:, g * 128:(g + 1) * 128].rearrange("p (hh e) -> p hh e", hh=2), CT_.a[:, g, :],
                                   Sv[:, g, d, :, :], True, True, [CT_, Sssdb], [pr])
                            mm(pr.a[:, 256:512], QG.a[:, d, :], Sglab.a[:, d, :], True, True, [QG, Sglab], [pr])
                            tt("dve", tmpB.a.rearrange("p (h e) -> p h e", h=4), pr.a[:, 0:256].rearrange("p (h e) -> p h e", h=4),
                               EA.a[:, 4 * d:4 * d + 4].unsqueeze(2).to_broadcast([128, 4, 64]), ALU.mult, [pr, EA], [tmpB])
                            tt("dve", AC.a[:, 0:256], AC.a[:, 0:256], tmpB.a, ALU.add, [AC, tmpB], [AC])
                            tt("dve", AC.a[:, 256:512], AC.a[:, 256:512], pr.a[:, 256:512], ALU.add, [AC, pr], [AC])
                            cdv = cdec.a[:, ch, 4 * d:4 * d + 4].rearrange("p (g hh) -> p g hh", g=2).unsqueeze(3).to_broadcast([128, 2, 2, 64])
                            tt("dve", S5[:, :, d, :, :], S5[:, :, d, :, :], cdv, ALU.mult, [Sssd, cdec], [Sssd])
                            tt("dve", S5[:, :, d, :, :], S5[:, :, d, :, :], ST5[:, :, d, :, :], ALU.add, [Sssd, SS], [Sssd])
                            stt("dve", Sgla.a[:, d, :], Sgla.a[:, d, :], egl.a[:, ch, d:d + 1], SGt.a[:, d * 256:(d + 1) * 256],
                                ALU.mult, ALU.add, [Sgla, egl, SGt], [Sgla])
                            if sweep == 0:
                                dma("qpool", yo_d[gt:gt + 128, :], AC.a, [AC], [yoB[ch]])
                            else:
                                ts("dve", szB.a, szB.a, 1.0, None, ALU.add, None, [szB], [szB])
                                fw.op("dve", lambda e: e.reciprocal(out=szB.a, in_=szB.a), [szB], [szB])
                                tt("dve", szB.a, szB.a, ZR.a, ALU.mult, [szB, ZR], [szB])
                                tt("dve", tmpB.a, AC.a[:, 0:256], szB.a[:, 0:256], ALU.mult, [AC, szB], [tmpB])
                                act(tmpB2.a, tmpB.a, AF.Square, [tmpB], [tmpB2, SM], accum=SM.a[:, 0:1])
                                act(tmpB2.a.rearrange("p (h e) -> p h e", h=4), AC.a[:, 256:512].rearrange("p (h e) -> p h e", h=4),
                                    AF.Square, [AC], [tmpB2])
                                red(SM.a[:, 1:5], tmpB2.a.rearrange("p (h e) -> p h e", h=4), [tmpB2], [SM])
                                act(SM.a[:, 8:9], SM.a[:, 0:1], AF.Ln, [SM], [SM], bias=epsb.a[:, 0:1], scale=1.0 / 256)
                                act(SM.a[:, 9:13], SM.a[:, 1:5], AF.Ln, [SM], [SM], bias=epsb.a[:, 0:1], scale=1.0 / 64)
                                act(SM.a[:, 8:13], SM.a[:, 8:13], AF.Exp, [SM], [SM], scale=-0.5)
                                stt("dve", outB.a[:, 0:256], tmpB.a, SM.a[:, 8:9], gssd.a, ALU.mult, ALU.mult, [tmpB, SM, gssd], [outB])
                                tt("dve", tmpB2.a.rearrange("p (h e) -> p h e", h=4), AC.a[:, 256:512].rearrange("p (h e) -> p h e", h=4),
                                   SM.a[:, 9:13].unsqueeze(2).to_broadcast([128, 4, 64]), ALU.mult, [AC, SM], [tmpB2])
                                tt("dve", tmpB2.a.rearrange("p (h e) -> p h e", h=4), tmpB2.a.rearrange("p (h e) -> p h e", h=4),
                                   ggla.a.unsqueeze(1).to_broadcast([128, 4, 64]), ALU.mult, [tmpB2, ggla], [tmpB2])
                                tt("dve", outB.a[:, 256:512], tmpB2.a, szB.a[:, 256:512], ALU.mult, [tmpB2, szB], [outB])
                                mb = bbank()
                                mbb = mb.a.bitcast(BF16)
                                for c in range(4):
                                    tr(mbb[:, c * 128:(c + 1) * 128], outB.a[:, c * 128:(c + 1) * 128], idb.a, [outB, idb], [mb])
                                cp("act", MX.a.rearrange("p c t -> p (c t)"), mbb[:, 0:512], [mb], [MX])
                                dma("qpool", mixT_d[0:2, :, gt:gt + 128].rearrange("c p t -> p c t"), MX.a[:, 0:2, :], [MX], [dscr])
                                dma("qpool", mixT_d[4:6, :, gt:gt + 128].rearrange("c p t -> p c t"), MX.a[:, 2:4, :], [MX], [dscr])
                            yield

            TM = max(seqs)
            nbm = TM // 128
            KG = alloc([128, TM], BF16)
            VG = alloc([128, nbm, 130], BF16)
            KM2 = alloc([128, 2, TM], BF16)
            VM2 = alloc([128, nbm, 130], BF16)
            qlo = alloc([128, 512], BF16, nb=2)
            qhi = alloc([128, 512], BF16, nb=2)
            qml = alloc([128, 512], BF16, nb=2)
            PT = alloc([128, 512], BF16, nb=4)
            osb = alloc([128, 512], F32, nb=2)
            ob = alloc([128, 512], BF16, nb=2)
            rlr = alloc([128, 2, 512], BF16, nb=2)
            onesb = alloc([128, 64], BF16)
            fw.op("dve", lambda e: e.memset(onesb.a, 1.0), [], [onesb])
            for q_ in qlo + qhi:
                fw.op("dve", lambda e, q_=q_: e.memset(q_.a, 0.0), [], [q_])
            sc_banks = banks[0:3]
            po_banks = banks[3:5]
            bcb = banks[5]
            cnt = {"sc": 0, "po": 0, "q": 0, "pt": 0}
            gB = genB()
            pending_tail = [None, None]
            for s in range(nseq):
                Ts = seqs[s]
                nb = Ts // 128
                o0 = offs[s]
                for (kind, heads) in (("g", [0, 1, 2, 3]), ("m", [0, 1]), ("m", [2, 3])):
                    if kind == "g":
                        dma("sp", KG.a[:, 0:Ts], KTg[:, o0:o0 + Ts], [], [KG])
                        dma("sp", VG.a[:, 0:nb, :], VAg[:, o0 // 128:o0 // 128 + nb, :], [], [VG])
                    else:
                        for hi_, h in enumerate(heads):
                            dma("sp", KM2.a[0:96, hi_, 0:Ts], KTm[h, :, o0:o0 + Ts], [], [KM2])
                        dma("sp", VM2.a[:, 0:nb, :], VAm[heads[0] // 2, :, o0 // 128:o0 // 128 + nb, :], [], [VM2])
                    for j in range(Ts // 512):
                        gq0 = o0 + j * 512
                        for hi_, h in enumerate(heads):
                            qi = cnt["q"]
                            cnt["q"] += 1
                            if kind == "g":
                                if h < 2:
                                    Qt = qlo[qi % 2]
                                    dma("sp", Qt.a[0:64, :], QTg[h, :, gq0:gq0 + 512], [], [Qt])
                                else:
                                    Qt = qhi[qi % 2]
                                    dma("sp", Qt.a[64:128, :], QTg[h, :, gq0:gq0 + 512], [], [Qt])
                                kr_ = 128
                                scale = 64 ** -0.5
                                kv = h // 2

                                def kslice(kb):
                                    return KG.a[:, kb * 128:(kb + 1) * 128], KG

                                def vslice(kb, kv=kv):
                                    return VG.a[:, kb, kv * 65:(kv + 1) * 65], VG
                                mixc, mixp = 2 + h // 2, (h % 2) * 64
                            else:
                                Qt = qml[qi % 2]
                                dma("sp", Qt.a[0:96, :], QTm[h, :, gq0:gq0 + 512], [], [Qt])
                                kr_ = 96
                                scale = 96 ** -0.5

                                def kslice(kb, hi_=hi_):
                                    return KM2.a[0:96, hi_, kb * 128:(kb + 1) * 128], KM2

                                def vslice(kb, hi_=hi_):
                                    return VM2.a[:, kb, hi_ * 65:(hi_ + 1) * 65], VM2
                                mixc, mixp = 6 + h // 2, (h % 2) * 64
                            po = po_banks[cnt["po"] % 2]
                            OS = osb[cnt["po"] % 2]
                            OB = ob[cnt["po"] % 2]
                            cnt["po"] += 1
                            pend = []
                            fw.cap = []

                            def issue_qk(kb):
                                sc = sc_banks[cnt["sc"] % 3]
                                cnt["sc"] += 1
                                ka, kbuf = kslice(kb)
                                mm(sc.a[:, 0:512], ka, Qt.a[0:kr_, :], True, True, [kbuf, Qt], [sc])
                                pt = PT[cnt["pt"] % 4]
                                cnt["pt"] += 1
                                act(pt.a, sc.a[:, 0:512], AF.Exp, [sc], [pt], scale=scale)
                                pend.append((kb, pt))

                            def issue_pv():
                                kb, pt = pend.pop(0)
                                va, vbuf = vslice(kb)
                                mm(po.a[0:65, 0:512], va, pt.a, kb == 0, kb == nb - 1, [vbuf, pt], [po])

                            for kb in range(nb):
                                issue_qk(kb)
                                if kb == 2 and pending_tail[0] is not None:
                                    fw.cap.extend(pending_tail[0])
                                    pending_tail[0] = None
                                if kb == min(12, nb - 1) and pending_tail[1] is not None:
                                    fw.cap.extend(pending_tail[1])
                                    pending_tail[1] = None
                                if len(pend) > 2:
                                    issue_pv()
                            while pend:
                                issue_pv()
                            uops = fw.cap
                            fw.cap = []
                            RL = rlr[(cnt["po"] - 1) % 2]
                            cp("act", OS.a[0:65, :], po.a[0:65, 0:512], [po], [OS])
                            fw.op("dve", lambda e, OS=OS: e.reciprocal(out=OS.a[64:65, :], in_=OS.a[64:65, :]), [OS], [OS])
                            cp("dve", RL.a[64:65, 0, :], OS.a[64:65, :], [OS], [RL])
                            tt("dve", RL.a[64:65, 1, :], OS.a[64:65, :], RL.a[64:65, 0, :], ALU.subtract, [OS, RL], [RL])
                            pending_tail[0] = fw.cap
                            fw.cap = []
                            mm(bcb.a[0:64, 0:512], onesb.a[64:65, 0:64], RL.a[64:65, 0, :], True, False, [onesb, RL], [bcb])
                            mm(bcb.a[0:64, 0:512], onesb.a[64:65, 0:64], RL.a[64:65, 1, :], False, True, [onesb, RL], [bcb])
                            tt("dve", OB.a[0:64, :], OS.a[0:64, :], bcb.a[0:64, 0:512], ALU.mult, [OS, bcb], [OB])
                            dma("qpool", mixT_d[mixc, mixp:mixp + 64, gq0:gq0 + 512], OB.a[0:64, :], [OB], [dscr])
                            pending_tail[1] = fw.cap
                            fw.cap = []
                            next(gB, None)
                            bops = fw.cap
                            fw.cap = None
                            fw.merge_emit([uops, bops])
            for pi_ in range(2):
                if pending_tail[pi_] is not None:
                    for op_ in pending_tail[pi_]:
                        fw.op(*op_)
                    pending_tail[pi_] = None
            for _ in gB:
                pass
            fw.barrier()
            bump[0] = persist_mark

            wo = alloc([128, 8, DM], BF16)
            w2 = alloc([128, 32, DM], BF16)
            d_mark = bump[0]
            wstg = alloc([128, DM], F32, nb=2)
            for c in range(8):
                wsx = wstg[c % 2]
                dma("sp", wsx.a, Wd['w_out'][l, c * 128:(c + 1) * 128, :], [], [wsx])
                cp("dve" if c % 2 == 0 else "act", wo.a[:, c, :], wsx.a, [wsx], [wo])
            for c in range(32):
                wsx = wstg[c % 2]
                dma("sp", wsx.a, Wd['w_ff2'][l, c * 128:(c + 1) * 128, :], [], [wsx])
                cp("dve" if c % 2 == 0 else "act", w2.a[:, c, :], wsx.a, [wsx], [w2])
            if last:
                dma("sp", g1bc.a, Wd['final_norm_g'].partition_broadcast(128), [], [g1bc])
            fw.barrier()
            bump[0] = d_mark
            x1_r = alloc([128, 4, DM], F32, nb=2)
            x1u = [[Buf() for _ in range(4)] for _ in range(2)]
            hb2 = alloc([128, DM], BF16)
            junk2 = hb2
            h2T_r = alloc([128, 8, 512], BF16, nb=2)
            w1s = alloc([128, 8, 256], BF16, nb=2)
            actT = alloc([128, 32, 512], BF16)
            mixs = alloc([128, 8, 512], BF16)
            rtmp = alloc([128, 512], F32, nb=1)
            rtmp = [rtmp, rtmp]
            smD = alloc([128, 8], F32, nb=2)
            smE = alloc([128, 8], F32, nb=2)
            dcn = {"w1": 0, "k5": 0, "k6": 0}

            def DPa(sj):
                s, t0, g0 = supers[sj]
                x1 = x1_r[sj % 2]
                dma("sp", mixs.a, mixT_d[:, :, g0:g0 + 512].rearrange("c p t -> p c t"), [], [mixs])
                dma("sp", x1.a, xsrc[g0:g0 + 512, :].rearrange("(u p) c -> p u c", p=128), [], [x1] + x1u[sj % 2])
                for u in range(4):
                    for hf in range(2):
                        pb = banks[dcn["k5"] % 2]
                        dcn["k5"] += 1
                        for c in range(8):
                            mm(pb.a[:, 0:512], mixs.a[:, c, u * 128:(u + 1) * 128], wo.a[:, c, hf * 512:(hf + 1) * 512],
                               c == 0, c == 7, [mixs, wo], [pb])
                        tt("dve", x1.a[:, u, hf * 512:(hf + 1) * 512], x1.a[:, u, hf * 512:(hf + 1) * 512], pb.a[:, 0:512],
                           ALU.add, [x1, pb], [x1, x1u[sj % 2][u]])

            def DPb(sj):
                x1 = x1_r[sj % 2]
                h2T = h2T_r[sj % 2]
                for u in range(4):
                    SM = smD[u % 2]
                    act(junk2.a, x1.a[:, u, :], AF.Square, [x1u[sj % 2][u]], [junk2, SM], accum=SM.a[:, 0:1])
                    act(SM.a[:, 1:2], SM.a[:, 0:1], AF.Ln, [SM], [SM], bias=epsb.a[:, 0:1], scale=1.0 / DM)
                    act(SM.a[:, 2:3], SM.a[:, 1:2], AF.Exp, [SM], [SM], scale=-0.5)
                    stt("dve", hb2.a, x1.a[:, u, :], SM.a[:, 2:3], g2bc.a, ALU.mult, ALU.mult, [x1u[sj % 2][u], SM, g2bc], [hb2])
                    pTt = banks[2]
                    pTtb = pTt.a.bitcast(BF16)
                    for c in range(8):
                        tr(pTtb[:, c * 128:(c + 1) * 128], hb2.a[:, c * 128:(c + 1) * 128], idb.a, [hb2, idb], [pTt])
                    cp("act", h2T.a[:, :, u * 128:(u + 1) * 128], pTtb.rearrange("p (c t) -> p c t", c=8), [pTt], [h2T])

            def DM_(sj):
                s, t0, g0 = supers[sj]
                x1 = x1_r[sj % 2]
                h2T = h2T_r[sj % 2]
                for fg in range(16):
                    W1 = w1s[dcn["w1"] % 2]
                    dcn["w1"] += 1
                    dma("sp", W1.a, w1b_d[l, :, fg * 256:(fg + 1) * 256].rearrange("(c p) f -> p c f", p=128), [], [W1])
                    for f2 in range(2):
                        f = fg * 2 + f2
                        pb = banks[3 + (f % 3)]
                        for c in range(8):
                            mm(pb.a[:, 0:512], W1.a[:, c, f2 * 128:(f2 + 1) * 128], h2T.a[:, c, :], c == 0, c == 7, [W1, h2T], [pb])
                        R_ = rtmp[f % 2]
                        act(R_.a, pb.a[:, 0:512], AF.Relu, [pb], [R_])
                        tt("dve", actT.a[:, f, :], R_.a, R_.a, ALU.mult, [R_], [actT])
                for u in range(4):
                    XO = T(x1.a[:, u, :], x1.b)
                    SM = smE[u % 2]
                    for hf in range(2):
                        pb = banks[6 + (dcn["k6"] % 2)]
                        dcn["k6"] += 1
                        for f in range(32):
                            mm(pb.a[:, 0:512], actT.a[:, f, u * 128:(u + 1) * 128], w2.a[:, f, hf * 512:(hf + 1) * 512],
                               f == 0, f == 31, [actT, w2], [pb])
                        tt("dve", XO.a[:, hf * 512:(hf + 1) * 512], x1.a[:, u, hf * 512:(hf + 1) * 512], pb.a[:, 0:512],
                           ALU.add, [x1, pb], [XO, x1u[sj % 2][u]])
                    if not last:
                        dma("qpool", xres_d[g0 + u * 128:g0 + (u + 1) * 128, :], XO.a, [XO], [dscr])
                    else:
                        act(rtmp[0].a.bitcast(BF16), XO.a, AF.Square, [XO], [rtmp[0], SM], accum=SM.a[:, 4:5])
                        act(SM.a[:, 5:6], SM.a[:, 4:5], AF.Ln, [SM], [SM], bias=epsb.a[:, 0:1], scale=1.0 / DM)
                        act(SM.a[:, 6:7], SM.a[:, 5:6], AF.Exp, [SM], [SM], scale=-0.5)
                        stt("dve", XO.a, XO.a, SM.a[:, 6:7], g1bc.a, ALU.mult, ALU.mult, [XO, SM, g1bc], [XO])
                        dma("qpool", y_out[g0 + u * 128:g0 + (u + 1) * 128, :], XO.a, [XO], [dscr])

            DPa(0)
            DPb(0)
            for sj in range(len(supers)):
                lanes = [fw.capture(DM_, sj)]
                wins = [(0.0, 1.0)]
                if sj + 1 < len(supers):
                    lanes.append(fw.capture(DPa, sj + 1))
                    wins.append((0.0, 0.55))
                    lanes.append(fw.capture(DPb, sj + 1))
                    wins.append((0.2, 0.8))
                fw.merge_emit(lanes, wins)
            fw.barrier()
            bump[0] = persist_mark

        fw.finish()
        build.nops = fw.nops
    return nc


SEQS = [2048, 2048, 8192]
DEPTH = 2
_cache = {}


def kernel(**inputs):
    ncores = 8
    xp = np.asarray(inputs['x_prompt'], np.float32)
    xs = np.asarray(inputs['x_sample'], np.float32)
    key = "main"
    if key not in _cache:
        _cache[key] = build(SEQS, DEPTH)
    nc = _cache[key]
    tab, cmat = host_consts(max(SEQS))
    base = {}
    for k in PNAMES:
        a = np.asarray(inputs[k], np.float32)
        if k in ('ssd_dt_bias', 'ssd_a_log'):
            a = a.reshape(DEPTH, 8)
        if k == 'gla_gate_b':
            a = a.reshape(DEPTH, 256)
        base[k] = np.ascontiguousarray(a)
    base['tab'] = tab
    base['cm'] = cmat
    in_maps = []
    for c in range(ncores):
        xc = np.concatenate([xp[2 * c], xp[2 * c + 1], xs[c]], axis=0)
        m = dict(base)
        m['x'] = np.ascontiguousarray(xc)
        in_maps.append(m)
    res = run_bass_kernel_spmd(nc, in_maps, core_ids=list(range(ncores)))
    yp = np.empty_like(xp)
    ys = np.empty_like(xs)
    for c in range(ncores):
        y = res.results[c]['y']
        yp[2 * c] = y[0:2048]
        yp[2 * c + 1] = y[2048:4096]
        ys[c] = y[4096:12288]
    return (yp, ys)
```
